# Optimizing a Trainium2 kernel written in Bass

```python
import math
import jax
import jax.numpy as jnp
from jax import lax
import numpy as np

D_MODEL = 1024
BATCH = 8
SEQ = 4096
DEPTH = 2
DEC_BATCH = 4
DEC_SEQ = 4096
PAST_LEN = 128

N_META = 16
NORM_EPS = 1e-6
N_BRANCH = 4
BRANCH_WIDTH = D_MODEL // 2
D_FF = 4 * D_MODEL

HG_WIDTH = BRANCH_WIDTH
HG_HEAD_DIM = 128
HG_HEADS = HG_WIDTH // HG_HEAD_DIM
HG_CHUNK = 64
LB_FLOOR = 1e-20

SC_WIDTH = BRANCH_WIDTH
SC_KSIZE = 3

DA_HEADS = 4
DA_QK_DIM = 64
DA_V_DIM = 2 * DA_QK_DIM
DA_QK_WIDTH = DA_HEADS * 2 * DA_QK_DIM
DA_WIDTH = DA_HEADS * DA_V_DIM
ROPE_THETA = 500000.0
ROPE_DIM = DA_QK_DIM // 4
Q_BLOCK = 128
SUBLN_EPS = 1e-5

RW_WIDTH = BRANCH_WIDTH
RW_HEAD_DIM = 64
RW_HEADS = RW_WIDTH // RW_HEAD_DIM
RW_DECAY_RANK = 64
RW_ICLR_RANK = 64
RW_GATE_RANK = 128
RW_LNX_EPS = 64e-5
RW_SIZES = (RW_WIDTH, RW_WIDTH, RW_WIDTH, RW_DECAY_RANK, RW_DECAY_RANK, RW_ICLR_RANK, RW_ICLR_RANK, RW_GATE_RANK)
RW_COLS = sum(RW_SIZES)
IN_SIZES = (HG_WIDTH,) * 5 + (SC_WIDTH,) * 3 + (DA_QK_WIDTH, DA_QK_WIDTH, DA_WIDTH, RW_COLS)
N_IN = sum(IN_SIZES)

kernel_name = 'hybrid_bidir_encoder_two_groups'


def _split(z, sizes):
    return jnp.split(z, [int(c) for c in np.cumsum(sizes)[:-1]], axis=-1)


def rms_norm(x, g, eps=NORM_EPS):
    xf = x.astype(jnp.float32)
    y = xf * lax.rsqrt(jnp.mean(xf * xf, axis=-1, keepdims=True) + eps)
    return (y * g.astype(jnp.float32)).astype(x.dtype)


def rope_partial(x, pos):
    half = ROPE_DIM // 2
    inv = ROPE_THETA ** (-jnp.arange(half, dtype=jnp.float32) / half)
    ang = pos[:, None] * inv[None, :]
    cos, sin = jnp.cos(ang), jnp.sin(ang)
    x1 = x[..., :half].astype(jnp.float32)
    x2 = x[..., half:ROPE_DIM].astype(jnp.float32)
    rot = jnp.concatenate([x1 * cos - x2 * sin, x2 * cos + x1 * sin], axis=-1).astype(x.dtype)
    return jnp.concatenate([rot, x[..., ROPE_DIM:]], axis=-1)


def centred_shift(u):
    up = jnp.pad(u, ((0, 0), (1, 1), (0, 0)))
    return 0.5 * (up[:, :-2] + up[:, 2:])


def gla_chunk_scan(q, k, logf, v):
    B, H, T, DK = q.shape
    DV = v.shape[-1]
    n = T // HG_CHUNK

    def to_chunks(a):
        return jnp.moveaxis(a.reshape(B, H, n, HG_CHUNK, a.shape[-1]), 2, 0)

    causal = jnp.tril(jnp.ones((HG_CHUNK, HG_CHUNK), dtype=bool))[:, :, None]

    def step(S, inp):
        qi, ki, gi, vi = inp
        b = jnp.cumsum(gi, axis=2)
        diff = b[:, :, :, None, :] - b[:, :, None, :, :]
        decay = jnp.where(causal, jnp.exp(jnp.where(causal, diff, 0.0)), 0.0)
        attn = jnp.einsum('bhtk,bhtsk,bhsk->bhts', qi, decay, ki)
        o = (jnp.einsum('bhts,bhsv->bhtv', attn, vi)
             + jnp.einsum('bhtk,bhkv->bhtv', qi * jnp.exp(b), S))
        b_last = b[:, :, -1:, :]
        S = (jnp.exp(b_last[:, :, 0, :, None]) * S
             + jnp.einsum('bhsk,bhsv->bhkv', ki * jnp.exp(b_last - b), vi))
        return S, o

    S0 = jnp.zeros((B, H, DK, DV), jnp.float32)
    _, o = lax.scan(step, S0, (to_chunks(q), to_chunks(k), to_chunks(logf), to_chunks(v)))
    return jnp.moveaxis(o, 0, 2).reshape(B, H, T, DV)


def hgrn2_mixer(q, f_fwd, f_bwd, i, g, lb, onorm_g):
    dt = q.dtype
    B, L, _ = q.shape
    pad = (-N_META) % HG_CHUNK

    def heads(t):
        t = jnp.pad(t.astype(jnp.float32), ((0, 0), (pad, 0), (0, 0)))
        return t.reshape(B, L + pad, HG_HEADS, HG_HEAD_DIM).transpose(0, 2, 1, 3)

    qh, vh = heads(q), heads(i)
    outs = []
    for d, f_raw in enumerate((f_fwd, f_bwd)):
        lb_d = lb[d].astype(jnp.float32)
        logf = jnp.logaddexp(jnp.log(jnp.maximum(lb_d, LB_FLOOR)),
                             jnp.log1p(-lb_d) + jax.nn.log_sigmoid(f_raw.astype(jnp.float32)))
        seqs = (qh, heads(-jnp.expm1(logf)), heads(logf), vh)
        if d == 1:
            seqs = tuple(jnp.flip(t, axis=2) for t in seqs)
        o = gla_chunk_scan(*seqs)
        if d == 1:
            o = jnp.flip(o, axis=2)
        outs.append(o)
    o = (outs[0] + outs[1]).transpose(0, 2, 1, 3)[:, pad:]
    o = rms_norm(o, onorm_g.reshape(HG_HEADS, HG_HEAD_DIM)).reshape(B, L, HG_WIDTH)
    return (o * jax.nn.silu(g.astype(jnp.float32))).astype(dt)


def shortconv_mixer(b, c, h, conv_w):
    L = h.shape[1]
    half = SC_KSIZE // 2
    u = jnp.pad(c * h, ((0, 0), (half, half), (0, 0)))
    y = u[:, 0:L] * conv_w[0]
    for j in range(1, SC_KSIZE):
        y = y + u[:, j:j + L] * conv_w[j]
    return (b * y).astype(h.dtype)


def diff_attention_mixer(q, k, v, qn_g, kn_g, lam_p, subln_g, lam_init, pos):
    dt = v.dtype
    B, L, _ = q.shape
    q = q.reshape(B, L, DA_HEADS, 2, DA_QK_DIM).transpose(0, 2, 3, 1, 4)
    k = k.reshape(B, L, DA_HEADS, 2, DA_QK_DIM).transpose(0, 2, 3, 1, 4)
    v = v.reshape(B, L, DA_HEADS, DA_V_DIM).transpose(0, 2, 1, 3)
    q = rope_partial(rms_norm(q, qn_g), pos) * (DA_QK_DIM ** -0.5)
    k = rope_partial(rms_norm(k, kn_g), pos)
    lam_p = lam_p.astype(jnp.float32)
    lam = jnp.exp(jnp.sum(lam_p[0] * lam_p[1])) - jnp.exp(jnp.sum(lam_p[2] * lam_p[3])) + lam_init
    n_blk = -(-L // Q_BLOCK)
    qp = jnp.pad(q, ((0, 0), (0, 0), (0, 0), (0, n_blk * Q_BLOCK - L), (0, 0)))
    qb = jnp.moveaxis(qp.reshape(B, DA_HEADS, 2, n_blk, Q_BLOCK, DA_QK_DIM), 3, 0)

    def block(qi):
        s = jnp.einsum('bhmqd,bhmkd->bhmqk', qi, k).astype(jnp.float32)
        p = jax.nn.softmax(s, axis=-1)
        w = p[:, :, 0] - lam * p[:, :, 1]
        return jnp.einsum('bhqk,bhkv->bhqv', w.astype(dt), v)

    o = lax.map(block, qb)
    o = jnp.moveaxis(o, 0, 2).reshape(B, DA_HEADS, n_blk * Q_BLOCK, DA_V_DIM)[:, :, :L]
    o = rms_norm(o, subln_g, SUBLN_EPS) * (1.0 - lam_init)
    return o.transpose(0, 2, 1, 3).reshape(B, L, DA_WIDTH).astype(dt)


def rwkv7_scan(r, w, k, v, kk, a):
    B, L, H, N = r.shape

    def step(S, inp):
        rt, wt, kt, vt, kkt, at = inp
        sa = jnp.einsum('bhvk,bhk->bhv', S, -kkt)
        S = (S * wt[:, :, None, :] + sa[..., None] * (kkt * at)[:, :, None, :]
             + vt[..., None] * kt[:, :, None, :])
        return S, jnp.einsum('bhvk,bhk->bhv', S, rt)

    xs = tuple(jnp.moveaxis(t, 1, 0) for t in (r, w, k, v, kk, a))
    _, o = lax.scan(step, jnp.zeros((B, H, N, N), jnp.float32), xs)
    return jnp.moveaxis(o, 0, 1)


def rwkv7_mixer(cols, mu, w0, w2, a0, a2, g2, k_k, k_a, r_k, lnx_g, lnx_b):
    dt = cols.dtype
    B, L, _ = cols.shape
    u = cols.astype(jnp.float32)
    xm = u + mu * (centred_shift(u) - u)
    r, k, v, wl_f, wl_b, al_f, al_b, gl = _split(xm, RW_SIZES)

    def heads(t):
        return t.reshape(B, L, RW_HEADS, RW_HEAD_DIM)

    kk = heads(k * k_k)
    kk = kk / jnp.maximum(jnp.sqrt(jnp.sum(kk * kk, axis=-1, keepdims=True)), 1e-12)
    rh, vh = heads(r), heads(v)
    outs, keys = [], []
    for d, (wl, al) in enumerate(((wl_f, al_f), (wl_b, al_b))):
        wlog = -jax.nn.softplus(-(w0[d] + jnp.tanh(wl) @ w2[d])) - 0.5
        decay = heads(jnp.exp(-jnp.exp(wlog)))
        a = jax.nn.sigmoid(a0[d] + al @ a2[d])
        kd = heads(k * (1.0 + (a - 1.0) * k_a))
        seqs = (rh, decay, kd, vh, kk, heads(a))
        if d == 1:
            seqs = tuple(jnp.flip(t, axis=1) for t in seqs)
        o = rwkv7_scan(*seqs)
        if d == 1:
            o = jnp.flip(o, axis=1)
        outs.append(o)
        keys.append(kd)
    o = outs[0] + outs[1]
    mean = jnp.mean(o, axis=-1, keepdims=True)
    var = jnp.mean(jnp.square(o - mean), axis=-1, keepdims=True)
    o = ((o - mean) * lax.rsqrt(var + RW_LNX_EPS)).reshape(B, L, RW_WIDTH) * lnx_g + lnx_b
    bonus = jnp.sum(rh * (keys[0] + keys[1]) * r_k, axis=-1, keepdims=True) * vh
    y = (o + bonus.reshape(B, L, RW_WIDTH)) * (jax.nn.sigmoid(gl) @ g2)
    return y.astype(dt)


def encoder_layer(x, l, p, lb, pos):
    B, L, _ = x.shape
    h = rms_norm(x, p['norm_mix_g'][l])
    z = h @ p['w_in'][l]
    hq, hf_f, hf_b, hi, hg, sb, sc, sh, dq, dk, dv, rw = _split(z, IN_SIZES)
    y_hg = hgrn2_mixer(hq, hf_f, hf_b, hi, hg, lb, p['hgrn_onorm_g'][l])
    y_sc = shortconv_mixer(sb, sc, sh, p['conv_w'][l])
    y_da = diff_attention_mixer(dq, dk, dv, p['diff_qnorm_g'][l], p['diff_knorm_g'][l],
                                p['diff_lambda'][l], p['diff_subln_g'][l],
                                0.8 - 0.6 * math.exp(-0.3 * l), pos)
    y_rw = rwkv7_mixer(rw, p['rwkv_mu'][l], p['rwkv_w0'][l], p['rwkv_w2'][l], p['rwkv_a0'][l],
                       p['rwkv_a2'][l], p['rwkv_g2'][l], p['rwkv_k_k'][l], p['rwkv_k_a'][l],
                       p['rwkv_r_k'][l], p['rwkv_lnx_g'][l], p['rwkv_lnx_b'][l])
    gates = jax.nn.sigmoid(h @ p['w_gate'][l]).reshape(B, L, N_BRANCH, D_MODEL)
    merged = gates[:, :, 0] * (y_hg @ p['branch_proj'][l, 0])
    for n, y_n in enumerate((y_sc, y_da, y_rw), start=1):
        merged = merged + gates[:, :, n] * (y_n @ p['branch_proj'][l, n])
    x = x + merged @ p['w_out'][l]
    h2 = rms_norm(x, p['norm_mlp_g'][l])
    x = x + jnp.square(jax.nn.relu(h2 @ p['mlp_w1'][l])) @ p['mlp_w2'][l]
    return x


def trunk(x, p):
    B, S, _ = x.shape
    meta = jnp.broadcast_to(p['meta_tokens'].astype(x.dtype)[None], (B, N_META, D_MODEL))
    h = jnp.concatenate([meta, x], axis=1)
    pos = jnp.arange(N_META + S, dtype=jnp.float32)
    sm = jax.nn.softmax(p['hgrn_lb_logits'].astype(jnp.float32), axis=1)
    lb = jnp.cumsum(sm, axis=1) - sm[:, :1]
    for l in range(DEPTH):
        h = encoder_layer(h, l, p, lb[:, l], pos)
    return h[:, N_META:]


def setup_inputs(seed: int = 0) -> dict:
    key = jax.random.key(seed)
    keys = jax.random.split(key, 40)
    ks = (keys[i] for i in range(40))

    def nrm(shape, scale):
        return scale * jax.random.normal(next(ks), shape, jnp.float32)

    def gain(shape):
        return 1.0 + 0.02 * jax.random.normal(next(ks), shape, jnp.float32)

    def unif(shape, lo, hi):
        return jax.random.uniform(next(ks), shape, jnp.float32, lo, hi)

    return {
        'x_prompt': nrm((BATCH, SEQ, D_MODEL), 1.0),
        'x_sample': nrm((DEC_BATCH, DEC_SEQ, D_MODEL), 1.0),
        'meta_tokens': nrm((N_META, D_MODEL), 1.0),
        'norm_mix_g': gain((DEPTH, D_MODEL)),
        'w_in': nrm((DEPTH, D_MODEL, N_IN), D_MODEL ** -0.5),
        'hgrn_lb_logits': nrm((2, DEPTH, HG_WIDTH), 0.5),
        'hgrn_onorm_g': gain((DEPTH, HG_WIDTH)),
        'conv_w': nrm((DEPTH, SC_KSIZE, SC_WIDTH), SC_KSIZE ** -0.5),
        'diff_qnorm_g': gain((DEPTH, DA_QK_DIM)),
        'diff_knorm_g': gain((DEPTH, DA_QK_DIM)),
        'diff_lambda': nrm((DEPTH, 4, DA_QK_DIM), 0.1),
        'diff_subln_g': gain((DEPTH, DA_V_DIM)),
        'rwkv_mu': unif((DEPTH, RW_COLS), 0.0, 1.0),
        'rwkv_w0': unif((DEPTH, 2, RW_WIDTH), -6.5, -1.5),
        'rwkv_w2': nrm((DEPTH, 2, RW_DECAY_RANK, RW_WIDTH), 0.1 * RW_DECAY_RANK ** -0.5),
        'rwkv_a0': nrm((DEPTH, 2, RW_WIDTH), 0.1),
        'rwkv_a2': nrm((DEPTH, 2, RW_ICLR_RANK, RW_WIDTH), RW_ICLR_RANK ** -0.5),
        'rwkv_g2': nrm((DEPTH, RW_GATE_RANK, RW_WIDTH), RW_GATE_RANK ** -0.5),
        'rwkv_k_k': 0.85 + nrm((DEPTH, RW_WIDTH), 0.02),
        'rwkv_k_a': gain((DEPTH, RW_WIDTH)),
        'rwkv_r_k': nrm((DEPTH, RW_HEADS, RW_HEAD_DIM), 0.1),
        'rwkv_lnx_g': gain((DEPTH, RW_WIDTH)),
        'rwkv_lnx_b': nrm((DEPTH, RW_WIDTH), 0.02),
        'w_gate': nrm((DEPTH, D_MODEL, N_BRANCH * D_MODEL), D_MODEL ** -0.5),
        'branch_proj': nrm((DEPTH, N_BRANCH, BRANCH_WIDTH, D_MODEL), BRANCH_WIDTH ** -0.5),
        'w_out': nrm((DEPTH, D_MODEL, D_MODEL), D_MODEL ** -0.5),
        'norm_mlp_g': gain((DEPTH, D_MODEL)),
        'mlp_w1': nrm((DEPTH, D_MODEL, D_FF), D_MODEL ** -0.5),
        'mlp_w2': nrm((DEPTH, D_FF, D_MODEL), D_FF ** -0.5),
    }


def reference(x_prompt, x_sample, meta_tokens, norm_mix_g, w_in, hgrn_lb_logits, hgrn_onorm_g,
              conv_w, diff_qnorm_g, diff_knorm_g, diff_lambda, diff_subln_g, rwkv_mu, rwkv_w0,
              rwkv_w2, rwkv_a0, rwkv_a2, rwkv_g2, rwkv_k_k, rwkv_k_a, rwkv_r_k, rwkv_lnx_g,
              rwkv_lnx_b, w_gate, branch_proj, w_out, norm_mlp_g, mlp_w1, mlp_w2):
    p = {
        'meta_tokens': meta_tokens, 'norm_mix_g': norm_mix_g, 'w_in': w_in,
        'hgrn_lb_logits': hgrn_lb_logits, 'hgrn_onorm_g': hgrn_onorm_g, 'conv_w': conv_w,
        'diff_qnorm_g': diff_qnorm_g, 'diff_knorm_g': diff_knorm_g, 'diff_lambda': diff_lambda,
        'diff_subln_g': diff_subln_g, 'rwkv_mu': rwkv_mu, 'rwkv_w0': rwkv_w0, 'rwkv_w2': rwkv_w2,
        'rwkv_a0': rwkv_a0, 'rwkv_a2': rwkv_a2, 'rwkv_g2': rwkv_g2, 'rwkv_k_k': rwkv_k_k,
        'rwkv_k_a': rwkv_k_a, 'rwkv_r_k': rwkv_r_k, 'rwkv_lnx_g': rwkv_lnx_g,
        'rwkv_lnx_b': rwkv_lnx_b, 'w_gate': w_gate, 'branch_proj': branch_proj, 'w_out': w_out,
        'norm_mlp_g': norm_mlp_g, 'mlp_w1': mlp_w1, 'mlp_w2': mlp_w2,
    }
    y_prompt = trunk(x_prompt, p)
    y_sample = trunk(x_sample, p)
    return (y_prompt, y_sample)
```

```python
import numpy as np
import ml_dtypes
import concourse.bass as bass
import concourse.mybir as mybir
from concourse.bass_utils import run_bass_kernel_spmd

F32 = mybir.dt.float32
BF16 = mybir.dt.bfloat16
AF = mybir.ActivationFunctionType
ALU = mybir.AluOpType
AX = mybir.AxisListType

EPOCH = 24000
DMA_SEM_MAX = 24000


class Obj:
    def __init__(self, k, t, name, is_dram=False):
        self.k = k
        self.t = t
        self.name = name
        self.is_dram = is_dram
        self.w = []
        self.r = []
        self.sem = None

    def ap(self):
        return self.t.ap() if self.is_dram else self.t[:]

    def __getitem__(self, key):
        base = self.t.ap() if self.is_dram else self.t
        return V(self, base[key])


class SubObj(Obj):
    def __init__(self, view_ap, name):
        self.t = None
        self.view = view_ap
        self.name = name
        self.is_dram = False
        self.w = []
        self.r = []
        self.sem = None

    def __getitem__(self, key):
        return V(self, self.view[key])


class V:
    def __init__(self, obj, ap):
        self.obj = obj
        self.ap = ap

    def __getitem__(self, key):
        return V(self.obj, self.ap[key])

    def __getattr__(self, name):
        attr = getattr(self.ap, name)
        if callable(attr):
            def f(*a, **kw):
                r = attr(*a, **kw)
                if isinstance(r, bass.AP):
                    return V(self.obj, r)
                return r
            return f
        return attr


ENGS = ("pe", "act", "dve", "pool", "sp")


class K:
    def __init__(self, nc):
        self.nc = nc
        self.eng = {"pe": nc.tensor, "act": nc.scalar, "dve": nc.vector,
                    "pool": nc.gpsimd, "sp": nc.sync}
        self.cnt = {e: 0 for e in ENGS}
        self.esems = {e: [] for e in ENGS}
        self.known = {e: {} for e in ENGS}
        self.free_dma_sems = []
        self.all_dma_sems = []
        self.n_sems = 0
        self.n_inst = 0
        self.n_wait = 0
        self.sb_off = 0
        self.sb_base = 16640
        self.sb_limit = 229376
        self.sb_stack = []
        self.objs_live = []
        self.uid = 0

    def _new_sem(self, name):
        self.n_sems += 1
        return self.nc.alloc_semaphore(name)

    def _esem(self, e, idx):
        lst = self.esems[e]
        while len(lst) <= idx:
            lst.append(self._new_sem(f"e_{e}_{len(lst)}"))
        return lst[idx]

    def _dma_sem(self, obj):
        if obj.sem is None or obj.sem[1] > DMA_SEM_MAX:
            if self.free_dma_sems:
                obj.sem = self.free_dma_sems.pop()
            else:
                ent = [self._new_sem(f"d_{len(self.all_dma_sems)}"), 0]
                self.all_dma_sems.append(ent)
                obj.sem = ent
            if obj.sem[1] > DMA_SEM_MAX:
                ent = [self._new_sem(f"d_{len(self.all_dma_sems)}"), 0]
                self.all_dma_sems.append(ent)
                obj.sem = ent
        return obj.sem

    def _resolve(self, ev):
        if ev[0] == "e":
            _, e, n = ev
            return self._esem(e, (n - 1) // EPOCH), (n - 1) % EPOCH + 1
        else:
            ent = ev[1]
            return ent[0], ent[1]

    def _wait(self, e, ev):
        if ev[0] == "e" and ev[1] == "pe" and e == "pe":
            return
        sem, val = self._resolve(ev)
        key = sem.name if hasattr(sem, "name") else id(sem)
        if self.known[e].get(key, -1) >= val:
            return
        self.known[e][key] = val
        self.eng[e].wait_ge(sem, val)
        self.n_wait += 1

    def _deps(self, e, reads, writes, wd=False):
        for o in reads:
            for ev in o.w:
                self._wait(e, ev)
            if getattr(o, "is_psum", False):
                for ev in o.r:
                    if not (ev[0] == "e" and ev[1] == e):
                        self._wait(e, ev)
        for o in writes:
            if not wd:
                for ev in o.w:
                    self._wait(e, ev)
            for ev in o.r:
                self._wait(e, ev)

    def _record(self, ev, reads, writes, wd=False):
        for o in reads:
            if o in writes:
                continue
            o.r.append(ev)
            if len(o.r) > 12:
                o.r = self._compact(o.r)
        for o in writes:
            if wd:
                o.w.append(ev)
                if len(o.w) > 12:
                    o.w = self._compact(o.w)
            else:
                o.w = [ev]
            o.r = []

    @staticmethod
    def _compact(evs):
        best = {}
        out = []
        for ev in evs:
            if ev[0] == "e":
                if ev[1] not in best or best[ev[1]][2] < ev[2]:
                    best[ev[1]] = ev
            else:
                if not any(x[0] == "d" and x[1] is ev[1] for x in out):
                    out.append(ev)
        return out + list(best.values())

    def op(self, e, method, **kw):
        reads, writes, args = [], [], {}
        wd = kw.pop("wd", False)
        for name, v in kw.items():
            if isinstance(v, V):
                (writes if name in ("out", "accum_out", "ap") else reads).append(v.obj)
                args[name] = v.ap
            else:
                args[name] = v
        self._deps(e, reads, writes, wd)
        ins = getattr(self.eng[e], method)(**args)
        self.cnt[e] += 1
        n = self.cnt[e]
        ins.then_inc(self._esem(e, (n - 1) // EPOCH), 1)
        self._record(("e", e, n), reads, writes, wd)
        self.n_inst += 1
        return ins

    def dma(self, out, in_, q="sp", wd=False, **kw):
        oo, io = out.obj, in_.obj
        owner = io if (oo.is_dram and not io.is_dram) else oo
        self._deps(q, [io], [oo], wd)
        ent = self._dma_sem(owner)
        ins = self.eng[q].dma_start(out=out.ap, in_=in_.ap, **kw)
        ent[1] += 16
        ins.then_inc(ent[0], 16)
        self._record(("d", ent), [io], [oo], wd)
        self.n_inst += 1
        return ins

    def barrier(self):
        for e in ENGS:
            for f in ENGS:
                if f == "sp" or self.cnt[f] == 0:
                    continue
                if f == e:
                    continue
                self._wait(e, ("e", f, self.cnt[f]))
            for ent in self.all_dma_sems:
                if ent[1] > 0:
                    self._wait(e, ("d", ent))

    def finish(self):
        self.barrier()

    def dram(self, name, shape, dtype, kind="Internal"):
        t = self.nc.dram_tensor(name, list(shape), dtype, kind=kind)
        return Obj(self, t, name, is_dram=True)

    def push(self):
        self.sb_stack.append((self.sb_off, len(self.objs_live)))

    def pop(self):
        off, n = self.sb_stack.pop()
        for o in self.objs_live[n:]:
            if o.sem is not None:
                self.free_dma_sems.append(o.sem)
                o.sem = None
        del self.objs_live[n:]
        self.sb_off = off

    def sb(self, name, shape, dtype):
        nbytes = int(np.prod(shape[1:])) * mybir.dt.size(dtype)
        nbytes = (nbytes + 63) // 64 * 64
        self.uid += 1
        t = self.nc.alloc_sbuf_tensor_at(f"{name}_{self.uid}", list(shape), dtype,
                                         offset=self.sb_base + self.sb_off)
        self.sb_off += nbytes
        assert self.sb_base + self.sb_off <= self.sb_limit, \
            f"SBUF overflow: {self.sb_off} at {name}"
        o = Obj(self, t, name)
        self.objs_live.append(o)
        return o

    def ps(self, name, shape, dtype=F32):
        t = self.nc.alloc_psum_tensor(name, list(shape), dtype)
        o = Obj(self, t, name)
        o.is_psum = True
        return o


D = 1024
NIN = 7552
DFF = 4096
PAD = 112
NORM_EPS = 1e-6
SUBLN_EPS = 1e-5
RW_LNX_EPS = 64e-5
KAPPA = float(np.exp(-0.5))
B_HQ, B_HFF, B_HFB, B_HI, B_HG = 0, 4, 8, 12, 16
B_SB, B_SC, B_SH = 20, 24, 28
B_DQ, B_DK, B_DV = 32, 36, 40
B_RW = 44
NZB = 59
C_ONORM, C_CONV, C_QN, C_KN, C_SUBLN, C_MU, C_W0, C_A0, C_KK, C_KA, C_RK = 0, 4, 16, 17, 18, 19, 34, 42, 50, 54, 58
RPL = 62

WSHAPES = lambda DEPTH: {
    "meta_tokens": (16, D), "norm_mix_g": (DEPTH, D), "w_in": (DEPTH, D, NIN),
    "hgrn_lb_logits": (2, DEPTH, 512), "hgrn_onorm_g": (DEPTH, 512), "conv_w": (DEPTH, 3, 512),
    "diff_qnorm_g": (DEPTH, 64), "diff_knorm_g": (DEPTH, 64), "diff_lambda": (DEPTH, 4, 64),
    "diff_subln_g": (DEPTH, 128), "rwkv_mu": (DEPTH, 1920), "rwkv_w0": (DEPTH, 2, 512),
    "rwkv_w2": (DEPTH, 2, 64, 512), "rwkv_a0": (DEPTH, 2, 512), "rwkv_a2": (DEPTH, 2, 64, 512),
    "rwkv_g2": (DEPTH, 128, 512), "rwkv_k_k": (DEPTH, 512), "rwkv_k_a": (DEPTH, 512),
    "rwkv_r_k": (DEPTH, 8, 64), "rwkv_lnx_g": (DEPTH, 512), "rwkv_lnx_b": (DEPTH, 512),
    "w_gate": (DEPTH, D, 4 * D), "branch_proj": (DEPTH, 4, 512, D), "w_out": (DEPTH, D, D),
    "norm_mlp_g": (DEPTH, D), "mlp_w1": (DEPTH, D, DFF), "mlp_w2": (DEPTH, DFF, D),
}


class Cfg:
    def __init__(self, S, NSEQ, DEPTH):
        assert S % 128 == 0
        self.S, self.NSEQ, self.DEPTH = S, NSEQ, DEPTH
        self.L = S + 16
        self.LP = S + 128
        self.NT = self.LP // 128
        self.TB = [(c, min(512, self.LP - c)) for c in range(0, self.LP, 512)]
        self.NKT = (self.L + 127) // 128
        self.NC64 = self.LP // 64


def host_consts(cfg):
    LP = cfg.LP
    pos = (np.arange(LP, dtype=np.float32) - PAD).astype(np.float32)
    inv = (500000.0 ** (-np.arange(8, dtype=np.float32) / 8)).astype(np.float32)
    C = np.ones((128, LP), np.float32)
    Sn = np.zeros((128, LP), np.float32)
    rotT = np.zeros((128, 128), np.float32)
    for p in range(128):
        d = p % 64
        if d < 16:
            ang = (pos * inv[d % 8]).astype(np.float32)
            C[p] = np.cos(ang)
            Sn[p] = np.sin(ang)
            if d < 8:
                rotT[p + 8, p] = -1.0
            else:
                rotT[p - 8, p] = 1.0
    idx = np.arange(128)
    lv = np.zeros((14, 128, 128), np.float32)
    for kk_ in range(7):
        b = 1 << kk_
        same = (idx[:, None] // (2 * b)) == (idx[None, :] // (2 * b))
        mk = same & ((idx[:, None] % (2 * b)) < b) & ((idx[None, :] % (2 * b)) >= b)
        lv[kk_] = mk
        lv[7 + kk_] = mk.T
    return {"c_rope": np.stack([C, Sn]).astype(np.float32), "c_rotT": rotT, "c_lvl": lv}


class M:
    def __init__(self, cfg, debug=False, stop=None):
        self.cfg = cfg
        self.debug = debug
        self.stop = stop
        nc = bass.Bass("TRN2", target_bir_lowering=False)
        self.nc = nc
        k = K(nc)
        self.k = k
        S, LP, NSEQ, DEPTH = cfg.S, cfg.LP, cfg.NSEQ, cfg.DEPTH
        dk = "ExternalOutput" if debug else "Internal"
        self.xin = [k.dram(f"xin{s}", [S, D], F32, kind="ExternalInput") for s in range(NSEQ)]
        self.yout = [k.dram(f"yout{s}", [S, D], F32, kind="ExternalOutput") for s in range(NSEQ)]
        self.W = {n: k.dram(n, list(sh), F32, kind="ExternalInput") for n, sh in WSHAPES(DEPTH).items()}
        self.c_rope = k.dram("c_rope", [2, 128, LP], F32, kind="ExternalInput")
        self.c_rotT = k.dram("c_rotT", [128, 128], F32, kind="ExternalInput")
        self.c_lvl = k.dram("c_lvl", [14, 128, 128], F32, kind="ExternalInput")
        self.wb_in = k.dram("wb_in", [DEPTH, D, NIN], BF16)
        self.wb_gate = k.dram("wb_gate", [DEPTH, D, 4 * D], BF16)
        self.wb_bp = k.dram("wb_bp", [DEPTH, 4, 512, D], BF16)
        self.wb_out = k.dram("wb_out", [DEPTH, D, D], BF16)
        self.wb_w1 = k.dram("wb_w1", [DEPTH, D, DFF], BF16)
        self.wb_w2 = k.dram("wb_w2", [DEPTH, DFF, D], BF16)
        self.wb_rw2 = k.dram("wb_rw2", [DEPTH, 128, 512], BF16, kind=dk)
        self.wb_ra2 = k.dram("wb_ra2", [DEPTH, 128, 512], BF16, kind=dk)
        self.wb_rg2 = k.dram("wb_rg2", [DEPTH, 128, 512], BF16, kind=dk)
        self.xs = k.dram("xs", [LP, D], F32, kind=dk)
        self.xs2 = k.dram("xs2", [LP, D], F32, kind=dk)
        self.zT = k.dram("zT", [NZB, 128, LP], F32, kind=dk)
        self.gT = k.dram("gT", [32, 128, LP], BF16, kind=dk)
        self.vtok = k.dram("vtok", [cfg.NKT * 128, 512], BF16, kind=dk)
        self.yT = k.dram("yT", [4, 4, 128, LP], BF16, kind=dk)
        self.xmT = k.dram("xmT", [15, 128, LP], BF16, kind=dk)
        self.ps = [k.ps(f"ps{i}", [128, 512], F32) for i in range(8)]
        self.ei = 0

    def evac(self, out, in_, eng=None):
        k = self.k
        if eng is None:
            eng = ("act", "dve")[self.ei % 2]
            self.ei += 1
        if eng == "act":
            k.op("act", "activation", out=out, in_=in_, func=AF.Copy)
        else:
            k.op(eng, "tensor_copy", out=out, in_=in_)

    def mm(self, out, lhsT, rhs, start=True, stop=True):
        self.k.op("pe", "matmul", out=out, lhsT=lhsT, rhs=rhs, start=start, stop=stop)

    def setup(self):
        k, cfg, W = self.k, self.cfg, self.W
        DEPTH = cfg.DEPTH
        self.idb = k.sb("idb", [128, 128], BF16)
        self.idf = k.sb("idf", [128, 128], F32)
        self.ones_bf = k.sb("ones_bf", [128, 128], BF16)
        self.bones_bf = k.sb("bones_bf", [128, 128], BF16)
        self.MU = k.sb("MU", [128, 128], F32)
        self.MUi = k.sb("MUi", [128, 128], F32)
        self.ML = k.sb("ML", [128, 128], F32)
        self.MLi = k.sb("MLi", [128, 128], F32)
        self.rotT_bf = k.sb("rotT_bf", [128, 128], BF16)
        self.prm = [k.sb(f"prm{l}", [128, RPL], F32) for l in range(DEPTH)]
        self.lbc = k.sb("lbc", [128, 2, DEPTH, 4, 2], F32)
        self.nlam = k.sb("nlam", [128, DEPTH], F32)
        self.epsc = k.sb("epsc", [128, 2], F32)
        for t, c in ((self.idb, 0.0), (self.idf, 0.0), (self.ones_bf, 1.0), (self.bones_bf, 0.0),
                     (self.MU, 1.0), (self.MUi, 1.0), (self.ML, 1.0), (self.MLi, 1.0)):
            k.op("pool", "memset", ap=t[:], constant=c)
        for t in (self.idb, self.idf):
            k.op("pool", "affine_select", out=t[:], in_=t[:], pattern=[[-1, 128]],
                 compare_op=ALU.not_equal, fill=1.0, base=0, channel_multiplier=1)
        for t, cmp, sg in ((self.MU, ALU.is_gt, -1), (self.MUi, ALU.is_ge, -1), (self.ML, ALU.is_gt, 1), (self.MLi, ALU.is_ge, 1)):
            k.op("pool", "affine_select", out=t[:], in_=t[:], pattern=[[-sg, 128]],
                 compare_op=cmp, fill=0.0, base=0, channel_multiplier=sg)
        k.op("pool", "memset", ap=self.epsc[:, 0:1], constant=NORM_EPS)
        k.op("pool", "memset", ap=self.epsc[:, 1:2], constant=SUBLN_EPS)
        k.op("pool", "memset", ap=self.bones_bf[0:64, 0:64], constant=1.0)
        k.op("pool", "memset", ap=self.bones_bf[64:128, 64:128], constant=1.0)
        k.dma(out=self.rotT_bf[:], in_=self.c_rotT[:], q="pool")
        for l in range(DEPTH):
            for r in range(8):
                rs = slice(128 * r, 128 * (r + 1))
                k.dma(out=self.wb_in[l, rs, :], in_=W["w_in"][l, rs, :], q="pool", wd=True)
                k.dma(out=self.wb_gate[l, rs, :], in_=W["w_gate"][l, rs, :], q="pool", wd=True)
                k.dma(out=self.wb_w1[l, rs, :], in_=W["mlp_w1"][l, rs, :], q="pool", wd=True)
                k.dma(out=self.wb_out[l, rs, :], in_=W["w_out"][l, rs, :], q="pool", wd=True)
            for r in range(4):
                k.dma(out=self.wb_w2[l, 1024 * r:1024 * (r + 1), :], in_=W["mlp_w2"][l, 1024 * r:1024 * (r + 1), :], q="pool", wd=True)
                k.dma(out=self.wb_bp[l, r], in_=W["branch_proj"][l, r], q="pool", wd=True)
            k.dma(out=self.wb_rw2[l], in_=W["rwkv_w2"][l].rearrange("d r c -> (d r) c"), q="pool", wd=True)
            k.dma(out=self.wb_ra2[l], in_=W["rwkv_a2"][l].rearrange("d r c -> (d r) c"), q="pool", wd=True)
            k.dma(out=self.wb_rg2[l], in_=W["rwkv_g2"][l], q="pool", wd=True)
        k.push()
        for l in range(DEPTH):
            pr = k.sb(f"pr{l}", [RPL, 128], F32)
            def rows(r0, ap, n):
                k.dma(out=pr[r0:r0 + n, :], in_=ap, wd=True)
            rows(C_ONORM, W["hgrn_onorm_g"][l].rearrange("(b p) -> b p", p=128), 4)
            rows(C_CONV, W["conv_w"][l].rearrange("j (b p) -> (j b) p", p=128), 12)
            for h in range(2):
                k.dma(out=pr[C_QN:C_QN + 1, 64 * h:64 * h + 64], in_=W["diff_qnorm_g"][l:l + 1, :], wd=True)
                k.dma(out=pr[C_KN:C_KN + 1, 64 * h:64 * h + 64], in_=W["diff_knorm_g"][l:l + 1, :], wd=True)
            rows(C_SUBLN, W["diff_subln_g"][l:l + 1, :], 1)
            rows(C_MU, W["rwkv_mu"][l].rearrange("(b p) -> b p", p=128), 15)
            rows(C_W0, W["rwkv_w0"][l].rearrange("d (b p) -> (d b) p", p=128), 8)
            rows(C_A0, W["rwkv_a0"][l].rearrange("d (b p) -> (d b) p", p=128), 8)
            rows(C_KK, W["rwkv_k_k"][l].rearrange("(b p) -> b p", p=128), 4)
            rows(C_KA, W["rwkv_k_a"][l].rearrange("(b p) -> b p", p=128), 4)
            rows(C_RK, W["rwkv_r_k"][l].rearrange("(b q) n -> b (q n)", q=2), 4)
            k.op("pe", "transpose", out=self.ps[0][:, 0:RPL], in_=pr[:, :], identity=self.idf[0:RPL, 0:RPL])
            k.op("dve", "tensor_copy", out=self.prm[l][:], in_=self.ps[0][:, 0:RPL])
        nl = 2 * DEPTH * 4
        pl = k.sb("pl", [nl, 128], F32)
        k.dma(out=pl[:, :], in_=W["hgrn_lb_logits"][:].rearrange("d l (b p) -> (d l b) p", p=128))
        k.op("pe", "transpose", out=self.ps[1][:, 0:nl], in_=pl[:, :], identity=self.idf[0:nl, 0:nl])
        lg = k.sb("lg", [128, 2, DEPTH, 4], F32)
        k.op("dve", "tensor_copy", out=lg[:].rearrange("p d l b -> p (d l b)"), in_=self.ps[1][:, 0:nl])
        mx = k.sb("mx", [128, 2, 4], F32)
        sm = k.sb("sm", [128, 2, 4], F32)
        k.op("dve", "tensor_copy", out=mx[:], in_=lg[:, :, 0, :])
        for l in range(1, DEPTH):
            k.op("dve", "tensor_tensor", out=mx[:], in0=mx[:], in1=lg[:, :, l, :], op=ALU.max)
        for l in range(DEPTH):
            k.op("dve", "tensor_tensor", out=lg[:, :, l, :], in0=lg[:, :, l, :], in1=mx[:], op=ALU.subtract)
        k.op("act", "activation", out=lg[:].rearrange("p d l b -> p (d l b)"), in_=lg[:].rearrange("p d l b -> p (d l b)"), func=AF.Exp)
        k.op("dve", "tensor_copy", out=sm[:], in_=lg[:, :, 0, :])
        for l in range(1, DEPTH):
            k.op("dve", "tensor_tensor", out=sm[:], in0=sm[:], in1=lg[:, :, l, :], op=ALU.add)
        k.op("dve", "reciprocal", out=sm[:], in_=sm[:])
        for l in range(DEPTH):
            k.op("dve", "tensor_tensor", out=lg[:, :, l, :], in0=lg[:, :, l, :], in1=sm[:], op=ALU.mult)
        acc = k.sb("acc", [128, 2, 4], F32)
        k.op("dve", "memset", ap=acc[:], constant=0.0)
        for l in range(DEPTH):
            if l > 0:
                k.op("dve", "tensor_tensor", out=acc[:], in0=acc[:], in1=lg[:, :, l, :], op=ALU.add)
            k.op("dve", "tensor_scalar", out=self.lbc[:, :, l, :, 0], in0=acc[:], scalar1=-1.0, scalar2=1.0, op0=ALU.mult, op1=ALU.add)
            k.op("dve", "tensor_scalar", out=self.lbc[:, :, l, :, 1], in0=acc[:], scalar1=1e-20, scalar2=None, op0=ALU.max)
        lt = k.sb("lt", [128, DEPTH, 256], F32)
        pr2 = k.sb("pr2", [128, DEPTH, 2, 64], F32)
        ss = k.sb("ss2", [128, DEPTH, 2], F32)
        for l in range(DEPTH):
            k.dma(out=lt[:, l, :], in_=W["diff_lambda"][l].rearrange("a n -> (a n)").partition_broadcast(128), wd=True)
        for l in range(DEPTH):
            for j in range(2):
                k.op("dve", "tensor_tensor", out=pr2[:, l, j, :], in0=lt[:, l, 128 * j:128 * j + 64], in1=lt[:, l, 128 * j + 64:128 * j + 128], op=ALU.mult)
                k.op("dve", "tensor_reduce", out=ss[:, l, j:j + 1], in_=pr2[:, l, j, :], axis=AX.X, op=ALU.add)
        k.op("act", "activation", out=ss[:].rearrange("p l j -> p (l j)"), in_=ss[:].rearrange("p l j -> p (l j)"), func=AF.Exp)
        for l in range(DEPTH):
            lam_init = 0.8 - 0.6 * float(np.exp(-0.3 * l))
            k.op("dve", "tensor_tensor", out=self.nlam[:, l:l + 1], in0=ss[:, l, 1:2], in1=ss[:, l, 0:1], op=ALU.subtract)
            k.op("dve", "tensor_scalar", out=self.nlam[:, l:l + 1], in0=self.nlam[:, l:l + 1], scalar1=-lam_init, scalar2=None, op0=ALU.add)
        if self.debug:
            self.dbg_prm = k.dram("dbg_prm", [cfg.DEPTH, 128, RPL], F32, kind="ExternalOutput")
            for l in range(DEPTH):
                k.dma(out=self.dbg_prm[l], in_=self.prm[l][:], q="pool", wd=True)
        k.barrier()
        k.pop()

    def p0(self, s):
        k, cfg = self.k, self.cfg
        k.push()
        zt = k.sb("zt", [128, D], F32)
        k.op("pool", "memset", ap=zt[:], constant=0.0)
        k.dma(out=self.xs[0:PAD, :], in_=zt[0:PAD, :], q="pool", wd=True)
        k.dma(out=self.xs[PAD:128, :], in_=self.W["meta_tokens"][:, :], wd=True)
        for r in range(0, cfg.S, 512):
            rr = min(512, cfg.S - r)
            k.dma(out=self.xs[128 + r:128 + r + rr, :], in_=self.xin[s][r:r + rr, :], wd=True)
        k.barrier()
        k.pop()

    def norm_T(self, x_, gb, hb, st, sq, dst, pst):
        k = self.k
        k.op("pool", "memset", ap=st[:], constant=0.0)
        k.op("act", "activation", out=sq[:], in_=x_, func=AF.Square, accum_out=st[:, 0:1])
        k.op("act", "activation", out=st[:, 1:2], in_=st[:, 0:1], func=AF.Sqrt, scale=1.0 / D, bias=NORM_EPS)
        k.op("dve", "reciprocal", out=st[:, 1:2], in_=st[:, 1:2])
        k.op("dve", "scalar_tensor_tensor", out=hb[:], in0=x_, scalar=st[:, 1:2], in1=gb[:], op0=ALU.mult, op1=ALU.mult)
        pb = pst[:].bitcast(BF16)
        for kc in range(8):
            k.op("pe", "transpose", out=pb[:, kc * 128:(kc + 1) * 128], in_=hb[:, kc * 128:(kc + 1) * 128], identity=self.idb[:])
        self.evac(dst, pb[:, :].rearrange("p (kc t) -> p kc t", kc=8))

    def p1(self, l):
        k, cfg = self.k, self.cfg
        LP, NT = cfg.LP, cfg.NT
        k.push()
        self.hT = k.sb("hT", [128, 8, LP], BF16)
        k.push()
        gb = k.sb("gb", [128, D], F32)
        k.dma(out=gb[:], in_=self.W["norm_mix_g"][l].partition_broadcast(128))
        xt = [k.sb(f"xt{i}", [128, D], F32) for i in range(2)]
        hb = [k.sb(f"hb{i}", [128, D], BF16) for i in range(2)]
        sq = k.sb("sq", [128, D], F32)
        st = [k.sb(f"st{i}", [128, 2], F32) for i in range(2)]
        for i in range(NT):
            x_ = xt[i % 2]
            k.dma(out=x_[:], in_=self.xs[128 * i:128 * (i + 1), :])
            self.norm_T(x_[:], gb, hb[i % 2], st[i % 2], sq, self.hT[:, :, 128 * i:128 * (i + 1)], self.ps[i % 2])
        k.op("pool", "memset", ap=self.hT[:, :, 0:PAD], constant=0.0)
        k.barrier()
        k.pop()

    def proj_fm(self, wsrc, ncols, dst, skip=(), sigmoid=False, odt=F32, tag="pa"):
        k, cfg = self.k, self.cfg
        LP = cfg.LP
        k.push()
        wt = [k.sb(f"{tag}_wt{i}", [128, 8, 512], BF16) for i in range(2)]
        stg = [k.sb(f"{tag}_stg{i}", [128, LP], odt) for i in range(3)]
        wv = wsrc.rearrange("(kc p) n -> p kc n", p=128)
        cnt = 0
        pi = 0
        for g in range((ncols + 511) // 512):
            c0 = 512 * g
            cw = min(512, ncols - c0)
            w_ = wt[g % 2]
            k.dma(out=w_[:, :, 0:cw], in_=wv[:, :, c0:c0 + cw])
            for m in range(cw // 128):
                j = c0 // 128 + m
                if j in skip:
                    continue
                s_ = stg[cnt % 3]
                cnt += 1
                for (t0, n) in cfg.TB:
                    p = self.ps[pi % 4]
                    pi += 1
                    for kc in range(8):
                        self.mm(p[:, 0:n], w_[:, kc, 128 * m:128 * m + 128], self.hT[:, kc, t0:t0 + n], kc == 0, kc == 7)
                    if sigmoid:
                        k.op("act", "activation", out=s_[:, t0:t0 + n], in_=p[:, 0:n], func=AF.Sigmoid)
                    else:
                        self.evac(s_[:, t0:t0 + n], p[:, 0:n])
                k.dma(out=dst[j], in_=s_[:], q="pool", wd=True)
        k.barrier()
        k.pop()

    def pa(self, l):
        k, cfg = self.k, self.cfg
        self.proj_fm(self.wb_in[l], NIN, self.zT, skip=(40, 41, 42, 43), tag="pa")
        k.push()
        wvv = k.sb("wvv", [128, 8, 512], BF16)
        k.dma(out=wvv[:], in_=self.wb_in[l].rearrange("(kc p) n -> p kc n", p=128)[:, :, 5120:5632])
        vst = [k.sb(f"vst{i}", [128, 512], BF16) for i in range(2)]
        for kt in range(cfg.NKT):
            c = PAD + 128 * kt
            kn = min(128, cfg.LP - c)
            p = self.ps[kt % 4]
            for kc in range(8):
                self.mm(p[0:kn, :], self.hT[:, kc, c:c + kn], wvv[:, kc, :], kc == 0, kc == 7)
            self.evac(vst[kt % 2][0:kn, :], p[0:kn, :])
            k.dma(out=self.vtok[128 * kt:128 * kt + kn, :], in_=vst[kt % 2][0:kn, :], q="pool", wd=True)
        k.barrier()
        k.pop()

    def p2(self, l):
        self.proj_fm(self.wb_gate[l], 4 * D, self.gT, sigmoid=True, odt=BF16, tag="pg")

    def p3(self, l):
        k, cfg = self.k, self.cfg
        LP = cfg.LP
        prm = self.prm[l]
        k.push()
        zb = [k.sb(f"c_zb{i}", [128, LP], F32) for i in range(2)]
        zc = [k.sb(f"c_zc{i}", [128, LP], F32) for i in range(2)]
        zh = [k.sb(f"c_zh{i}", [128, LP], F32) for i in range(2)]
        u = k.sb("c_u", [128, LP + 2], F32)
        t1 = k.sb("c_t1", [128, LP], F32)
        yb = [k.sb(f"c_y{i}", [128, LP], BF16) for i in range(2)]
        k.op("pool", "memset", ap=u[:], constant=0.0)
        for b in range(4):
            i = b % 2
            k.dma(out=zb[i][:], in_=self.zT[B_SB + b])
            k.dma(out=zc[i][:], in_=self.zT[B_SC + b])
            k.dma(out=zh[i][:], in_=self.zT[B_SH + b])
            k.op("dve", "tensor_tensor", out=u[:, 1:LP + 1], in0=zc[i][:], in1=zh[i][:], op=ALU.mult)
            k.op("dve", "tensor_scalar", out=t1[:], in0=u[:, 0:LP], scalar1=prm[:, C_CONV + b:C_CONV + b + 1], scalar2=None, op0=ALU.mult)
            k.op("dve", "scalar_tensor_tensor", out=t1[:], in0=u[:, 1:LP + 1], scalar=prm[:, C_CONV + 4 + b:C_CONV + 5 + b], in1=t1[:], op0=ALU.mult, op1=ALU.add)
            k.op("dve", "scalar_tensor_tensor", out=t1[:], in0=u[:, 2:LP + 2], scalar=prm[:, C_CONV + 8 + b:C_CONV + 9 + b], in1=t1[:], op0=ALU.mult, op1=ALU.add)
            k.op("dve", "tensor_tensor", out=yb[i][:], in0=t1[:], in1=zb[i][:], op=ALU.mult)
            k.dma(out=self.yT[1, b], in_=yb[i][:], q="pool", wd=True)
        k.barrier()
        k.pop()

    def tr_chunks(self, src, dst, nchunks, width, psbase=6):
        k = self.k
        per = 1024 // 128
        for gi, c0 in enumerate(range(0, nchunks, per)):
            nb = min(per, nchunks - c0)
            pb = self.ps[psbase + gi % 2][:].bitcast(BF16)
            for j in range(nb):
                k.op("pe", "transpose", out=pb[0:width, 128 * j:128 * j + 128],
                     in_=src[:, width * (c0 + j):width * (c0 + j + 1)], identity=self.idb[:])
            self.evac(dst[:, c0:c0 + nb, :], pb[0:width, 0:128 * nb].rearrange("p (c t) -> p c t", t=128))

    def p4(self, l):
        k, cfg = self.k, self.cfg
        LP = cfg.LP
        NC = LP // 64
        prm, lbc = self.prm[l], self.lbc
        ps = self.ps
        k.push()
        one64 = k.sb("h_one", [128, 64], F32)
        k.op("pool", "memset", ap=one64[:], constant=1.0)
        zq = k.sb("h_zq", [128, LP], F32)
        zf = k.sb("h_zf", [128, LP], F32)
        tmp1 = k.sb("h_tmp1", [128, LP], F32)
        cum = k.sb("h_cum", [128, LP], F32)
        tmpA = k.sb("h_tmpA", [128, LP], F32)
        split = min(LP, 512 * ((LP // 2) // 512))
        oL = k.sb("h_oL", [128, max(split, 64)], F32)
        oH = k.sb("h_oH", [128, LP - split], F32)

        def osl(c0, n):
            return oL[:, c0:c0 + n] if c0 < split else oH[:, c0 - split:c0 - split + n]
        Qt = [k.sb(f"h_Qt{i}", [128, LP], BF16) for i in range(2)]
        Kt = [k.sb(f"h_Kt{i}", [128, LP], BF16) for i in range(2)]
        vtok = k.sb("h_vtok", [64, NC, 128], BF16)
        Kttok = [k.sb(f"h_Kttok{i}", [64, NC, 128], BF16) for i in range(2)]
        eref = [k.sb(f"h_eref{i}", [128, NC], F32) for i in range(2)]
        dec = [k.sb(f"h_dec{i}", [128, NC], F32) for i in range(2)]
        e5 = [k.sb(f"h_e5{i}", [128, NC], F32) for i in range(2)]
        S32s = [[k.sb(f"h_S32{d}{i}", [128, 128], F32) for i in range(2)] for d in range(2)]
        tmpSs = [[k.sb(f"h_tmpS{d}{i}", [128, 128], F32) for i in range(2)] for d in range(2)]
        Sbs = [[k.sb(f"h_Sb{d}{i}", [128, 128], BF16) for i in range(2)] for d in range(2)]
        Ats = [[k.sb(f"h_At{d}{i}", [64, 64], BF16) for i in range(2)] for d in range(2)]
        cum3 = cum[:].rearrange("p (c t) -> p c t", t=64)
        tmpA3 = tmpA[:].rearrange("p (c t) -> p c t", t=64)
        orders = [list(range(1, NC)), list(range(NC - 1, 0, -1))]
        masks = [self.MUi, self.MLi]
        for hd in range(4):
            k.dma(out=zq[:], in_=self.zT[B_HQ + hd])
            k.dma(out=tmpA[:], in_=self.zT[B_HI + hd])
            vb = Qt[1]
            k.op("act", "activation", out=vb[:], in_=tmpA[:], func=AF.Copy)
            self.tr_chunks(vb, vtok, NC, 64)
            for d in range(2):
                k.dma(out=zf[:], in_=self.zT[B_HFF + 4 * d + hd])
                k.op("act", "activation", out=zf[:], in_=zf[:], func=AF.Sigmoid)
                k.op("dve", "tensor_scalar", out=zf[:], in0=zf[:], scalar1=lbc[:, d, l, hd, 0:1], scalar2=lbc[:, d, l, hd, 1:2], op0=ALU.mult, op1=ALU.add)
                k.op("act", "activation", out=tmp1[:], in_=zf[:], func=AF.Ln)
                k.op("dve", "tensor_scalar", out=zf[:], in0=zf[:], scalar1=-1.0, scalar2=1.0, op0=ALU.mult, op1=ALU.add)
                for c in range(NC):
                    cs = slice(64 * c, 64 * c + 64)
                    k.op("dve", "tensor_tensor_scan", out=cum[:, cs], data0=one64[:], data1=tmp1[:, cs], initial=0.0, op0=ALU.mult, op1=ALU.add)
                if d == 1:
                    k.op("dve", "tensor_tensor", out=tmpA3, in0=cum3[:, :, 63:64].to_broadcast([128, NC, 64]), in1=cum3, op=ALU.subtract)
                    k.op("dve", "tensor_tensor", out=cum[:], in0=tmpA[:], in1=tmp1[:], op=ALU.add)
                tot = cum3[:, :, 63] if d == 0 else cum3[:, :, 0]
                refi = 31 if d == 0 else 32
                k.op("act", "activation", out=eref[d][:], in_=cum3[:, :, refi], func=AF.Exp)
                k.op("act", "activation", out=dec[d][:], in_=tot, func=AF.Exp)
                k.op("dve", "tensor_tensor", out=e5[d][:], in0=tot, in1=cum3[:, :, refi], op=ALU.subtract)
                k.op("act", "activation", out=e5[d][:], in_=e5[d][:], func=AF.Exp)
                k.op("dve", "tensor_tensor", out=tmpA3, in0=cum3, in1=cum3[:, :, refi:refi + 1].to_broadcast([128, NC, 64]), op=ALU.subtract)
                k.op("act", "activation", out=tmp1[:], in_=tmpA[:], func=AF.Exp)
                k.op("dve", "tensor_tensor", out=Qt[d][:], in0=zq[:], in1=tmp1[:], op=ALU.mult)
                k.op("act", "activation", out=tmpA[:], in_=tmpA[:], func=AF.Exp, scale=-1.0)
                k.op("dve", "tensor_tensor", out=Kt[d][:], in0=zf[:], in1=tmpA[:], op=ALU.mult)
                self.tr_chunks(Kt[d], Kttok[d], NC, 64)
                k.op("pool", "memset", ap=S32s[d][0][:], constant=0.0)
                k.op("pool", "memset", ap=Sbs[d][0][:], constant=0.0)
            touched = set()
            for ci in range(NC - 1):
                for d in range(2):
                    order = orders[d]
                    c = order[ci]
                    cs = slice(64 * c, 64 * c + 64)
                    pA, pO, pS = ps[d], ps[2 + d], ps[4 + d]
                    at = Ats[d][ci % 2]
                    Sb, tmpS = Sbs[d][ci % 2], tmpSs[d][ci % 2]
                    S32, S32n = S32s[d][ci % 2], S32s[d][(ci + 1) % 2]
                    self.mm(pA[0:64, 0:64], Kt[d][:, cs], Qt[d][:, cs])
                    if ci + 1 < len(order):
                        cn = order[ci + 1]
                        self.mm(pS[:, 0:128], Kttok[d][:, c, :], vtok[:, c, :])
                        k.op("dve", "tensor_scalar", out=tmpS[:], in0=S32[:], scalar1=dec[d][:, c:c + 1], scalar2=None, op0=ALU.mult)
                        k.op("dve", "scalar_tensor_tensor", out=S32n[:], in0=pS[:, 0:128], scalar=e5[d][:, c:c + 1], in1=tmpS[:], op0=ALU.mult, op1=ALU.add)
                        k.op("act", "activation", out=Sbs[d][(ci + 1) % 2][:], in_=S32n[:], func=AF.Copy, scale=eref[d][:, cn:cn + 1])
                    k.op("dve", "tensor_tensor", out=at[:], in0=pA[0:64, 0:64], in1=masks[d][0:64, 0:64], op=ALU.mult)
                    self.mm(pO[:, 0:64], vtok[:, c, :], at[:], True, False)
                    self.mm(pO[:, 0:64], Sb[:], Qt[d][:, cs], False, True)
                    o_ = osl(64 * c, 64)
                    if c not in touched:
                        touched.add(c)
                        k.op("act", "activation", out=o_, in_=pO[:, 0:64], func=AF.Copy)
                    else:
                        k.op("dve", "tensor_tensor", out=o_, in0=pO[:, 0:64], in1=o_, op=ALU.add)
            k.op("pool", "memset", ap=osl(0, 64), constant=0.0)
            sqb, yb = Kt[0], Qt[0]
            parts = [(0, split, oL), (split, LP - split, oH)] if split > 0 else [(0, LP, oH)]
            for (c0, n_, t_) in parts:
                k.op("act", "activation", out=sqb[:, c0:c0 + n_], in_=t_[:, 0:n_], func=AF.Square)
            for bi, (t0, n) in enumerate(cfg.TB):
                p = ps[bi % 2]
                self.mm(p[:, 0:n], self.ones_bf[:], sqb[:, t0:t0 + n])
                k.op("act", "activation", out=tmpA[:, t0:t0 + n], in_=p[:, 0:n], func=AF.Ln, scale=1.0 / 128, bias=self.epsc[:, 0:1])
            k.op("act", "activation", out=tmpA[:], in_=tmpA[:], func=AF.Exp, scale=-0.5)
            k.dma(out=zf[:], in_=self.zT[B_HG + hd])
            k.op("act", "activation", out=zf[:], in_=zf[:], func=AF.Silu)
            for (c0, n_, t_) in parts:
                k.op("dve", "scalar_tensor_tensor", out=cum[:, c0:c0 + n_], in0=t_[:, 0:n_], scalar=prm[:, C_ONORM + hd:C_ONORM + hd + 1],
                     in1=tmpA[:, c0:c0 + n_], op0=ALU.mult, op1=ALU.mult)
            k.op("dve", "tensor_tensor", out=yb[:], in0=cum[:], in1=zf[:], op=ALU.mult)
            k.dma(out=self.yT[0, hd], in_=yb[:], q="pool", wd=True)
        k.barrier()
        k.pop()

    def p5(self, l):
        k, cfg = self.k, self.cfg
        LP, NKT = cfg.LP, cfg.NKT
        prm = self.prm[l]
        lam_init = 0.8 - 0.6 * float(np.exp(-0.3 * l))
        k.push()
        ropeC = k.sb("a_rc", [128, LP], F32)
        ropeS = k.sb("a_rs", [128, LP], F32)
        k.dma(out=ropeC[:], in_=self.c_rope[0])
        k.dma(out=ropeS[:], in_=self.c_rope[1])
        gq = k.sb("a_gq", [128, 3], F32)
        k.op("dve", "tensor_scalar", out=gq[:, 0:1], in0=prm[:, C_QN:C_QN + 1], scalar1=0.125, scalar2=None, op0=ALU.mult)
        k.op("dve", "tensor_copy", out=gq[:, 1:2], in_=prm[:, C_KN:C_KN + 1])
        k.op("dve", "tensor_scalar", out=gq[:, 2:3], in0=prm[:, C_SUBLN:C_SUBLN + 1], scalar1=1.0 - lam_init, scalar2=None, op0=ALU.mult)
        z = k.sb("a_z", [128, LP], F32)
        tmpA = k.sb("a_tmpA", [128, LP], F32)
        sqb = k.sb("a_sqb", [128, LP], BF16)
        znb = k.sb("a_znb", [128, LP], BF16)
        qh = k.sb("a_qh", [128, LP], BF16)
        kh = k.sb("a_kh", [128, LP], BF16)
        vt = k.sb("a_vt", [128, NKT, 128], BF16)
        oT = k.sb("a_oT", [128, LP], F32)
        E = [k.sb(f"a_E{i}", [128, 512], BF16) for i in range(4)]
        accE = [k.sb(f"a_acc{m}", [128, 512], F32) for m in range(2)]
        accb = [k.sb(f"a_accb{m}", [128, 512], BF16) for m in range(2)]
        r0 = k.sb("a_r0", [128, 512], F32)
        r1 = k.sb("a_r1", [128, 512], F32)
        t1 = k.sb("a_t1", [128, 512], F32)
        for hd in range(4):
            for blk, gcol, dst in ((B_DQ + hd, 0, qh), (B_DK + hd, 1, kh)):
                k.dma(out=z[:], in_=self.zT[blk])
                k.op("act", "activation", out=sqb[:], in_=z[:], func=AF.Square)
                for bi, (t0, n) in enumerate(cfg.TB):
                    p = self.ps[bi % 2]
                    self.mm(p[:, 0:n], self.bones_bf[:], sqb[:, t0:t0 + n])
                    k.op("act", "activation", out=tmpA[:, t0:t0 + n], in_=p[:, 0:n], func=AF.Ln, scale=1.0 / 64, bias=self.epsc[:, 0:1])
                k.op("act", "activation", out=tmpA[:], in_=tmpA[:], func=AF.Exp, scale=-0.5)
                k.op("dve", "scalar_tensor_tensor", out=z[:], in0=z[:], scalar=gq[:, gcol:gcol + 1], in1=tmpA[:], op0=ALU.mult, op1=ALU.mult)
                k.op("act", "activation", out=znb[:], in_=z[:], func=AF.Copy)
                for bi, (t0, n) in enumerate(cfg.TB):
                    p = self.ps[2 + bi % 2]
                    self.mm(p[:, 0:n], self.rotT_bf[:], znb[:, t0:t0 + n])
                    k.op("dve", "tensor_tensor", out=tmpA[:, t0:t0 + n], in0=p[:, 0:n], in1=ropeS[:, t0:t0 + n], op=ALU.mult)
                k.op("dve", "tensor_tensor", out=z[:], in0=z[:], in1=ropeC[:], op=ALU.mult)
                k.op("dve", "tensor_tensor", out=dst[:], in0=z[:], in1=tmpA[:], op=ALU.add)
            k.dma(out=vt[:], in_=self.vtok[:, hd * 128:(hd + 1) * 128].rearrange("(kt p) v -> p kt v", p=128))
            for qi, (q0, n) in enumerate(cfg.TB):
                def scores(kt):
                    c = PAD + 128 * kt
                    kn = min(128, LP - c)
                    es = []
                    for m in range(2):
                        pS = self.ps[2 * (kt % 2) + m]
                        self.mm(pS[0:kn, 0:n], kh[64 * m:64 * m + 64, c:c + kn], qh[64 * m:64 * m + 64, q0:q0 + n])
                        e_ = E[2 * (kt % 2) + m]
                        k.op("act", "activation", out=e_[0:kn, 0:n], in_=pS[0:kn, 0:n], func=AF.Exp)
                        es.append(e_)
                    return kn, es
                nxt = scores(0)
                for kt in range(NKT):
                    kn, es = nxt
                    if kt + 1 < NKT:
                        nxt = scores(kt + 1)
                    for m in range(2):
                        self.mm(self.ps[4 + m][:, 0:n], vt[0:kn, kt, :], es[m][0:kn, 0:n], kt == 0, kt == NKT - 1)
                        ae = ("dve", "pool")[m]
                        if kt == 0:
                            k.op(ae, "tensor_copy", out=accE[m][0:kn, 0:n], in_=es[m][0:kn, 0:n])
                        else:
                            k.op(ae, "tensor_tensor", out=accE[m][0:kn, 0:n], in0=accE[m][0:kn, 0:n], in1=es[m][0:kn, 0:n], op=ALU.add)
                for m in range(2):
                    k.op("act", "activation", out=accb[m][:, 0:n], in_=accE[m][:, 0:n], func=AF.Copy)
                    self.mm(self.ps[6 + m][:, 0:n], self.ones_bf[:], accb[m][:, 0:n])
                k.op("dve", "reciprocal", out=r0[:, 0:n], in_=self.ps[6][:, 0:n])
                k.op("dve", "reciprocal", out=r1[:, 0:n], in_=self.ps[7][:, 0:n])
                k.op("dve", "tensor_tensor", out=oT[:, q0:q0 + n], in0=self.ps[4][:, 0:n], in1=r0[:, 0:n], op=ALU.mult)
                k.op("dve", "tensor_tensor", out=t1[:, 0:n], in0=self.ps[5][:, 0:n], in1=r1[:, 0:n], op=ALU.mult)
                k.op("dve", "scalar_tensor_tensor", out=oT[:, q0:q0 + n], in0=t1[:, 0:n], scalar=self.nlam[:, l:l + 1], in1=oT[:, q0:q0 + n], op0=ALU.mult, op1=ALU.add)
            k.op("act", "activation", out=sqb[:], in_=oT[:], func=AF.Square)
            for bi, (t0, n) in enumerate(cfg.TB):
                p = self.ps[bi % 2]
                self.mm(p[:, 0:n], self.ones_bf[:], sqb[:, t0:t0 + n])
                k.op("act", "activation", out=tmpA[:, t0:t0 + n], in_=p[:, 0:n], func=AF.Ln, scale=1.0 / 128, bias=self.epsc[:, 1:2])
            k.op("act", "activation", out=tmpA[:], in_=tmpA[:], func=AF.Exp, scale=-0.5)
            k.op("dve", "scalar_tensor_tensor", out=znb[:], in0=oT[:], scalar=gq[:, 2:3], in1=tmpA[:], op0=ALU.mult, op1=ALU.mult)
            k.dma(out=self.yT[2, hd], in_=znb[:], q="pool", wd=True)
        k.barrier()
        k.pop()

    def p6a(self, l):
        k, cfg = self.k, self.cfg
        LP = cfg.LP
        prm = self.prm[l]
        k.push()
        zb = [k.sb(f"r_zb{i}", [128, LP + 2], F32) for i in range(2)]
        s_ = k.sb("r_s", [128, LP], F32)
        xm = [k.sb(f"r_xm{i}", [128, LP], BF16) for i in range(2)]
        hm = k.sb("r_hm", [128, 15, 2], F32)
        for t in zb:
            k.op("pool", "memset", ap=t[:], constant=0.0)
        k.op("dve", "tensor_scalar", out=hm[:, :, 0], in0=prm[:, C_MU:C_MU + 15], scalar1=-1.0, scalar2=1.0, op0=ALU.mult, op1=ALU.add)
        k.op("dve", "tensor_scalar", out=hm[:, :, 1], in0=prm[:, C_MU:C_MU + 15], scalar1=0.5, scalar2=None, op0=ALU.mult)
        for b in range(15):
            i = b % 2
            k.dma(out=zb[i][:, 1:LP + 1], in_=self.zT[B_RW + b])
            k.op("dve", "tensor_tensor", out=s_[:], in0=zb[i][:, 0:LP], in1=zb[i][:, 2:LP + 2], op=ALU.add)
            k.op("dve", "tensor_scalar", out=s_[:], in0=s_[:], scalar1=hm[:, b, 1:2], scalar2=None, op0=ALU.mult)
            k.op("dve", "scalar_tensor_tensor", out=xm[i][:], in0=zb[i][:, 1:LP + 1], scalar=hm[:, b, 0:1], in1=s_[:], op0=ALU.mult, op1=ALU.add)
            k.op("pool", "memset", ap=xm[i][:, 0:PAD], constant=0.0)
            k.dma(out=self.xmT[b], in_=xm[i][:], q="pool", wd=True)
        k.barrier()
        k.pop()

    def p6b(self, l):
        k, cfg = self.k, self.cfg
        LP, NT = cfg.LP, cfg.NT
        prm = self.prm[l]
        ps = self.ps
        k.push()
        Of = k.sb("w_Of", [128, NT, 512], BF16)
        Bn = k.sb("w_Bn", [128, 4, LP], BF16)
        w2b = k.sb("w_w2b", [128, 512], BF16)
        a2b = k.sb("w_a2b", [128, 512], BF16)
        g2b = k.sb("w_g2b", [128, 512], BF16)
        lng = k.sb("w_lng", [128, 512], F32)
        lnb = k.sb("w_lnb", [128, 512], F32)
        k.dma(out=w2b[:], in_=self.wb_rw2[l])
        k.dma(out=a2b[:], in_=self.wb_ra2[l])
        k.dma(out=g2b[:], in_=self.wb_rg2[l])
        k.dma(out=lng[:], in_=self.W["rwkv_lnx_g"][l].partition_broadcast(128))
        k.dma(out=lnb[:], in_=self.W["rwkv_lnx_b"][l].partition_broadcast(128))
        LM = k.sb("w_LM", [128, 14, 128], BF16)
        k.dma(out=LM[:], in_=self.c_lvl[:].rearrange("m p j -> p m j"), q="pool")
        rmask = k.sb("w_rmask", [128, 4, 128], F32)
        k.op("pool", "memset", ap=rmask[:], constant=1.0)
        k.op("pool", "memset", ap=rmask[:, :, 0:1], constant=0.0)
        ST32 = k.sb("w_ST32", [128, 4, 64], F32)
        STb = k.sb("w_STb", [128, 4, 64], BF16)
        f32t = lambda n: k.sb("w_" + n, [128, 4, 128], F32)
        bf16t = lambda n: k.sb("w_" + n, [128, 4, 128], BF16)
        kkr, rn, kk, sgw, a_, G, Gex, E1, E2, kd, akk = [
            f32t(n) for n in ("kkr", "rn", "kk", "sgw", "a", "G", "Gex", "E1", "E2", "kd", "akk")]
        tmp = Gex
        Bg, Kg, rkb = [bf16t(n) for n in ("Bg", "Kg", "rkb")]
        sqk = rkb
        thb = k.sb("w_thb", [128, 128], BF16)
        sgl = k.sb("w_sgl", [128, 128], BF16)
        PB = []
        for i_ in range(2):
            PB.append(dict(
                X=k.sb(f"w_X{i_}", [128, 15, 128], BF16),
                AR=k.sb(f"w_AR{i_}", [128, 4, 2, 128], BF16),
                Bt=bf16t(f"Bt{i_}"), Ktl=bf16t(f"Ktl{i_}"),
                Bgtok=k.sb(f"w_Bgtok{i_}", [128, 512], BF16), Kgtok=k.sb(f"w_Kgtok{i_}", [128, 512], BF16),
                Vtok=k.sb(f"w_Vtok{i_}", [128, 512], BF16), gamC=k.sb(f"w_gamC{i_}", [128, 4], F32),
                bont=f32t(f"bont{i_}"), Gt=f32t(f"Gt{i_}")))
        Ot, sq2, otmp = f32t("Ot"), f32t("sq2"), f32t("otmp")
        Xnb, yst = bf16t("Xnb"), bf16t("yst")
        st8 = k.sb("w_st8", [128, 4, 8], F32)
        MSS = [k.sb(f"w_MSS{i}", [128, 2, 128], BF16) for i in range(2)]
        for i_, (ma, mb) in enumerate(((self.MU, self.MUi), (self.ML, self.MLi))):
            k.op("dve", "tensor_copy", out=MSS[i_][:, 0, :], in_=ma[:])
            k.op("dve", "tensor_copy", out=MSS[i_][:, 1, :], in_=mb[:])
        hs = []
        for hd_ in range(8):
            d_ = {}
            d_["NmT"] = k.sb(f"w_NmT{hd_}", [128, 128], BF16)
            for n in ("NN", "MM", "TT2", "ZZ"):
                d_[n] = k.sb(f"w_{n}{hd_}", [128, 2, 128], BF16)
            d_["NkA"] = k.sb(f"w_NkA{hd_}", [128, 6, 128], BF16)
            d_["NkTA"] = k.sb(f"w_NkTA{hd_}", [128, 7, 128], BF16)
            d_["WTb"] = k.sb(f"w_WTb{hd_}", [128, 64], BF16)
            d_["UTb"] = k.sb(f"w_UTb{hd_}", [128, 64], BF16)
            hs.append(d_)
        SB_ = ((0, 2, 6), (1, 3, 7))
        sctr = [0, 0]
        nbk = [3]

        def slot(par_):
            i_ = sctr[par_]
            sctr[par_] += 1
            j_ = (i_ // nbk[0]) % 4
            return ps[SB_[par_][i_ % nbk[0]]][:, 128 * j_:128 * j_ + 128]

        def dslot(par_):
            i_ = sctr[par_]
            sctr[par_] += 1
            j_ = (i_ // nbk[0]) % 2
            return ps[SB_[par_][i_ % nbk[0]]][:, 256 * j_:256 * j_ + 256]

        fl = lambda t: t[:].rearrange("p b t -> p (b t)")
        v3 = lambda ap_: ap_.rearrange("p (a t) -> p a t", a=2)
        kkp = prm[:, C_KK:C_KK + 4].unsqueeze(2).to_broadcast([128, 4, 128])
        kap = prm[:, C_KA:C_KA + 4].unsqueeze(2).to_broadcast([128, 4, 128])
        rkp = prm[:, C_RK:C_RK + 4].unsqueeze(2).to_broadcast([128, 4, 128])
        G3 = G[:]

        def prep(c, d, P):
            cs = slice(128 * c, 128 * c + 128)
            Xc, AR = P["X"], P["AR"]
            At_, Rt = AR[:, :, 0, :], AR[:, :, 1, :]
            k.dma(out=Xc[:], in_=self.xmT[:, :, cs].rearrange("b p t -> p b t"))
            r_, kx, v_ = Xc[:, 0:4, :], Xc[:, 4:8, :], Xc[:, 8:12, :]
            rs = slice(64 * d, 64 * d + 64)
            pw = ps[4 + d]
            k.op("dve", "tensor_tensor", out=kkr[:], in0=kx, in1=kkp, op=ALU.mult)
            k.op("act", "activation", out=sqk[:], in_=kkr[:], func=AF.Square)
            k.op("act", "activation", out=thb[rs, :], in_=Xc[rs, 12, :], func=AF.Tanh)
            yield
            self.mm(ps[4][:, :], self.bones_bf[:], fl(sqk))
            k.op("dve", "tensor_scalar", out=fl(rn), in0=ps[4][:, :], scalar1=1e-24, scalar2=None, op0=ALU.max)
            k.op("act", "activation", out=rn[:], in_=rn[:], func=AF.Ln)
            k.op("act", "activation", out=rn[:], in_=rn[:], func=AF.Exp, scale=-0.5)
            k.op("dve", "tensor_tensor", out=kk[:], in0=kkr[:], in1=rn[:], op=ALU.mult)
            yield
            for b in range(4):
                self.mm(pw[:, 128 * b:128 * b + 128], w2b[rs, 128 * b:128 * b + 128], thb[rs, :])
            for b in range(4):
                k.op("act", "activation", out=sgw[:, b, :], in_=pw[:, 128 * b:128 * b + 128], func=AF.Sigmoid,
                     bias=prm[:, C_W0 + 4 * d + b:C_W0 + 4 * d + b + 1])
            yield
            for b in range(4):
                self.mm(pw[:, 128 * b:128 * b + 128], a2b[rs, 128 * b:128 * b + 128], Xc[rs, 13, :])
            for b in range(4):
                k.op("act", "activation", out=a_[:, b, :], in_=pw[:, 128 * b:128 * b + 128], func=AF.Sigmoid,
                     bias=prm[:, C_A0 + 4 * d + b:C_A0 + 4 * d + b + 1])
            yield
            k.op("dve", "tensor_tensor_scan", out=fl(G), data0=fl(rmask), data1=fl(sgw), initial=0.0, op0=ALU.mult, op1=ALU.add)
            if d == 1:
                k.op("dve", "tensor_tensor", out=Gex[:], in0=G3[:, :, 127:128].to_broadcast([128, 4, 128]), in1=G[:], op=ALU.subtract)
                k.op("dve", "tensor_tensor", out=G[:], in0=Gex[:], in1=sgw[:], op=ALU.add)
            tot = G3[:, :, 127] if d == 0 else G3[:, :, 0]
            totb = (G3[:, :, 127:128] if d == 0 else G3[:, :, 0:1]).to_broadcast([128, 4, 128])
            yield
            k.op("act", "activation", out=E1[:], in_=G[:], func=AF.Exp, scale=-KAPPA)
            k.op("dve", "tensor_tensor", out=Rt, in0=r_, in1=E1[:], op=ALU.mult)
            k.op("dve", "tensor_tensor", out=Gex[:], in0=G[:], in1=sgw[:], op=ALU.subtract)
            k.op("act", "activation", out=E2[:], in_=Gex[:], func=AF.Exp, scale=-KAPPA)
            yield
            k.op("dve", "scalar_tensor_tensor", out=At_, in0=kk[:], scalar=-1.0, in1=E2[:], op0=ALU.mult, op1=ALU.mult)
            k.op("act", "activation", out=E1[:], in_=G[:], func=AF.Exp, scale=KAPPA)
            k.op("dve", "tensor_tensor", out=tmp[:], in0=totb, in1=G[:], op=ALU.subtract)
            k.op("act", "activation", out=E2[:], in_=tmp[:], func=AF.Exp, scale=-KAPPA)
            k.op("act", "activation", out=P["gamC"][:], in_=tot, func=AF.Exp, scale=-KAPPA)
            yield
            k.op("dve", "scalar_tensor_tensor", out=tmp[:], in0=a_[:], scalar=-1.0, in1=kap, op0=ALU.add, op1=ALU.mult)
            k.op("dve", "scalar_tensor_tensor", out=kd[:], in0=tmp[:], scalar=1.0, in1=kx, op0=ALU.add, op1=ALU.mult)
            k.op("dve", "tensor_tensor", out=akk[:], in0=a_[:], in1=kk[:], op=ALU.mult)
            yield
            k.op("dve", "tensor_tensor", out=P["Bt"][:], in0=akk[:], in1=E1[:], op=ALU.mult)
            k.op("dve", "tensor_tensor", out=P["Ktl"][:], in0=kd[:], in1=E1[:], op=ALU.mult)
            yield
            k.op("dve", "tensor_tensor", out=Bg[:], in0=akk[:], in1=E2[:], op=ALU.mult)
            k.op("dve", "tensor_tensor", out=Kg[:], in0=kd[:], in1=E2[:], op=ALU.mult)
            yield
            for src, dst_, pbk in ((Bg, P["Bgtok"], ps[4]), (Kg, P["Kgtok"], ps[5]), (None, P["Vtok"], ps[4])):
                pb = pbk[:].bitcast(BF16)
                for b in range(4):
                    in_ = Xc[:, 8 + b, :] if src is None else src[:, b, :]
                    k.op("pe", "transpose", out=pb[:, 128 * b:128 * b + 128], in_=in_, identity=self.idb[:])
                k.op("act", "activation", out=dst_[:], in_=pb[:, 0:512], func=AF.Copy)
                yield
            k.op("dve", "tensor_tensor", out=tmp[:], in0=r_, in1=kd[:], op=ALU.mult)
            k.op("dve", "tensor_tensor", out=rkb[:], in0=tmp[:], in1=rkp, op=ALU.mult)
            self.mm(ps[5][:, :], self.bones_bf[:], fl(rkb))
            yield
            if d == 0:
                k.op("dve", "tensor_tensor", out=Bn[:, :, cs], in0=ps[5][:, :].rearrange("p (b t) -> p b t", t=128), in1=v_, op=ALU.mult)
            else:
                k.op("dve", "tensor_tensor", out=P["bont"][:], in0=ps[5][:, :].rearrange("p (b t) -> p b t", t=128), in1=v_, op=ALU.mult)
                k.op("dve", "tensor_tensor", out=P["bont"][:], in0=P["bont"][:], in1=Bn[:, :, cs], op=ALU.add)
                yield
                k.op("act", "activation", out=sgl[:], in_=Xc[:, 14, :], func=AF.Sigmoid)
                for b in range(4):
                    self.mm(ps[4][:, 128 * b:128 * b + 128], g2b[:, 128 * b:128 * b + 128], sgl[:])
                k.op("act", "activation", out=fl(P["Gt"]), in_=ps[4][:, :], func=AF.Copy)
            yield

        def stages(c, d, P, pump):
            cs = slice(128 * c, 128 * c + 128)
            AR, Bt, Ktl, Vtok, Bgtok, Kgtok, gamC = P["AR"], P["Bt"], P["Ktl"], P["Vtok"], P["Bgtok"], P["Kgtok"], P["gamC"]
            mST = self.ML if d == 0 else self.MU
            mo, mto = (0, 7) if d == 0 else (7, 0)
            hv = []
            for hd in range(8):
                blk, par = hd // 2, hd % 2
                hr = slice(64 * par, 64 * par + 64)
                hv.append(dict(blk=blk, par=par, hr=hr, H=hs[hd],
                               Ah=AR[hr, blk, 0, :], Bh=Bt[hr, blk, :], Kh=Ktl[hr, blk, :], Rh=AR[hr, blk, 1, :],
                               ARh=AR[hr, blk, :, :].rearrange("p a t -> p (a t)"),
                               Vh=Vtok[:, 128 * blk + 64 * par:128 * blk + 64 * par + 64], STh=STb[hr, blk, :]))
            nbk[0] = 3
            for x in hv:
                H, par = x["H"], x["par"]
                p_ = dslot(par)
                self.mm(p_, x["Bh"], x["ARh"])
                k.op("dve", "tensor_tensor", out=H["NN"][:], in0=v3(p_), in1=MSS[d][:], op=ALU.mult)
                p_ = dslot(par)
                self.mm(p_, x["Kh"], x["ARh"])
                k.op("dve", "tensor_tensor", out=H["MM"][:], in0=v3(p_), in1=MSS[d][:], op=ALU.mult)
                p_ = slot(par)
                self.mm(p_, x["Ah"], x["Bh"])
                k.op("dve", "tensor_tensor", out=H["NmT"][:], in0=p_, in1=mST[:], op=ALU.mult)
            pump()
            for x in hv:
                H = x["H"]
                k.op("pool", "tensor_tensor", out=H["NkA"][:], in0=H["NN"][:, 0:1, :].to_broadcast([128, 6, 128]),
                     in1=LM[:, mo:mo + 6, :], op=ALU.mult)
                k.op("pool", "tensor_tensor", out=H["NkTA"][:], in0=H["NmT"][:].unsqueeze(1).to_broadcast([128, 7, 128]),
                     in1=LM[:, mto:mto + 7, :], op=ALU.mult)
                k.op("pool", "tensor_tensor", out=H["TT2"][:, 0, :], in0=H["NkA"][:, 0, :], in1=self.idb[:], op=ALU.add)
                k.op("pool", "tensor_tensor", out=H["TT2"][:, 1, :], in0=H["NkTA"][:, 0, :], in1=self.idb[:], op=ALU.add)
            pump()
            for lv in range(1, 7):
                lastlv = lv == 6
                for x in hv:
                    H, par = x["H"], x["par"]
                    pz = dslot(par)
                    self.mm(pz[:, 0:128], H["NkTA"][:, lv, :], H["TT2"][:, 0, :])
                    if not lastlv:
                        self.mm(pz[:, 128:256], H["NkA"][:, lv, :], H["TT2"][:, 1, :])
                        k.op("act", "activation", out=H["ZZ"][:], in_=v3(pz), func=AF.Copy)
                    else:
                        k.op("act", "activation", out=H["ZZ"][:, 0, :], in_=pz[:, 0:128], func=AF.Copy)
                pump()
                for x in hv:
                    H, par = x["H"], x["par"]
                    pt = dslot(par)
                    self.mm(pt[:, 0:128], H["TT2"][:, 1, :], H["ZZ"][:, 0, :])
                    if not lastlv:
                        self.mm(pt[:, 128:256], H["TT2"][:, 0, :], H["ZZ"][:, 1, :])
                        k.op("dve", "tensor_tensor", out=H["TT2"][:], in0=v3(pt), in1=H["TT2"][:], op=ALU.add)
                    else:
                        k.op("dve", "tensor_tensor", out=H["TT2"][:, 0, :], in0=pt[:, 0:128], in1=H["TT2"][:, 0, :], op=ALU.add)
                pump()
            nbk[0] = 2
            for x in hv:
                H, par = x["H"], x["par"]
                p_ = slot(par)
                self.mm(p_[:, 0:64], x["Ah"], x["STh"], True, False)
                self.mm(p_[:, 0:64], H["MM"][:, 0, :], x["Vh"], False, True)
                k.op("act", "activation", out=H["WTb"][:], in_=p_[:, 0:64], func=AF.Copy)
            pump()
            for x in hv:
                H, par = x["H"], x["par"]
                p_ = slot(par)
                self.mm(p_[:, 0:64], H["TT2"][:, 0, :], H["WTb"][:])
                k.op("act", "activation", out=H["UTb"][:], in_=p_[:, 0:64], func=AF.Copy)
            pump()
            for x in hv:
                H, par, blk, hr = x["H"], x["par"], x["blk"], x["hr"]
                po = ps[6 + par][:, 64 * blk:64 * blk + 64]
                self.mm(po, x["Rh"], x["STh"], True, False)
                self.mm(po, H["NN"][:, 1, :], H["UTb"][:], False, False)
                self.mm(po, H["MM"][:, 1, :], x["Vh"], False, True)
                p_ = slot(par)
                self.mm(p_[:, 0:64], Bgtok[:, 128 * blk:128 * blk + 128], H["UTb"][:], True, False)
                self.mm(p_[:, 0:64], Kgtok[:, 128 * blk:128 * blk + 128], x["Vh"], False, True)
                k.op("dve", "scalar_tensor_tensor", out=ST32[hr, blk, :], in0=ST32[hr, blk, :], scalar=gamC[hr, blk:blk + 1],
                     in1=p_[hr, 0:64], op0=ALU.mult, op1=ALU.add)
                k.op("act", "activation", out=STb[hr, blk, :], in_=ST32[hr, blk, :], func=AF.Copy)
            pump()
            Of4 = Of[:, c, :].rearrange("p (b q v) -> p b q v", q=2, v=64)
            if d == 0:
                for par in range(2):
                    self.evac(Of4[:, :, par, :], ps[6 + par][:, 0:256].rearrange("p (b v) -> p b v", v=64))
            else:
                Ot4 = fl(Ot).rearrange("p (b q v) -> p b q v", q=2, v=64)
                for par in range(2):
                    k.op("dve", "tensor_tensor", out=Ot4[:, :, par, :], in0=ps[6 + par][:, 0:256].rearrange("p (b v) -> p b v", v=64),
                         in1=Of4[:, :, par, :], op=ALU.add)
                O8 = fl(Ot).rearrange("p (h v) -> p h v", v=64)
                S8 = fl(sq2).rearrange("p (h v) -> p h v", v=64)
                k.op("dve", "tensor_reduce", out=st8[:, 0, :], in_=O8, axis=AX.X, op=ALU.add)
                k.op("act", "activation", out=fl(sq2), in_=fl(Ot), func=AF.Square)
                k.op("dve", "tensor_reduce", out=st8[:, 1, :], in_=S8, axis=AX.X, op=ALU.add)
                k.op("dve", "tensor_scalar", out=st8[:, 2, :], in0=st8[:, 0, :], scalar1=1.0 / 64, scalar2=None, op0=ALU.mult)
                k.op("dve", "tensor_tensor", out=st8[:, 3, :], in0=st8[:, 2, :], in1=st8[:, 2, :], op=ALU.mult)
                k.op("dve", "scalar_tensor_tensor", out=st8[:, 3, :], in0=st8[:, 1, :], scalar=1.0 / 64, in1=st8[:, 3, :], op0=ALU.mult, op1=ALU.subtract)
                k.op("act", "activation", out=st8[:, 3, :], in_=st8[:, 3, :], func=AF.Sqrt, bias=RW_LNX_EPS)
                k.op("dve", "reciprocal", out=st8[:, 3, :], in_=st8[:, 3, :])
                pump()
                k.op("dve", "tensor_tensor", out=O8, in0=O8, in1=st8[:, 2, :].unsqueeze(2).to_broadcast([128, 8, 64]), op=ALU.subtract)
                k.op("dve", "tensor_tensor", out=O8, in0=O8, in1=st8[:, 3, :].unsqueeze(2).to_broadcast([128, 8, 64]), op=ALU.mult)
                k.op("dve", "tensor_tensor", out=fl(Ot), in0=fl(Ot), in1=lng[:], op=ALU.mult)
                k.op("dve", "tensor_tensor", out=fl(Xnb), in0=fl(Ot), in1=lnb[:], op=ALU.add)
                pb = ps[7][:].bitcast(BF16)
                for b in range(4):
                    k.op("pe", "transpose", out=pb[:, 128 * b:128 * b + 128], in_=Xnb[:, b, :], identity=self.idb[:])
                k.op("dve", "tensor_tensor", out=fl(otmp), in0=pb[:, 0:512], in1=fl(P["bont"]), op=ALU.add)
                k.op("dve", "tensor_tensor", out=yst[:], in0=otmp[:], in1=P["Gt"][:], op=ALU.mult)
                k.dma(out=self.yT[3][:, :, cs].rearrange("b p t -> p b t"), in_=yst[:], q="pool", wd=True)

        def drain(g):
            if g is not None:
                for _ in g:
                    pass

        for d in range(2):
            k.op("pool", "memset", ap=ST32[:], constant=0.0)
            k.op("pool", "memset", ap=STb[:], constant=0.0)
            order = list(range(NT)) if d == 0 else list(range(NT - 1, -1, -1))
            drain(prep(order[0], d, PB[0]))
            for ci, c in enumerate(order):
                nxt = prep(order[ci + 1], d, PB[(ci + 1) % 2]) if ci + 1 < len(order) else None

                def pump(nxt=nxt):
                    if nxt is not None:
                        next(nxt, None)
                stages(c, d, PB[ci % 2], pump)
                drain(nxt)
        k.barrier()
        k.pop()

    def p7(self, l):
        k, cfg = self.k, self.cfg
        ps = self.ps
        k.push()
        bp = k.sb("m_bp", [128, 4, 4, D], BF16)
        wo = k.sb("m_wo", [128, 8, D], BF16)
        for nb in range(4):
            k.dma(out=bp[:, nb], in_=self.wb_bp[l, nb].rearrange("(kc p) o -> p kc o", p=128))
        k.dma(out=wo[:], in_=self.wb_out[l].rearrange("(kc p) o -> p kc o", p=128))
        ysb = [k.sb(f"m_ysb{i}", [128, 16, 512], BF16) for i in range(2)]
        gsb = [k.sb(f"m_gsb{i}", [128, 32, 512], BF16) for i in range(2)]
        mT = k.sb("m_mT", [128, 8, 512], BF16)
        acc = k.sb("m_acc", [128, 512], F32)
        tmp = [k.sb(f"m_tmp{i}", [128, 512], F32) for i in range(2)]
        xt = [k.sb(f"m_xt{i}", [128, D], F32) for i in range(2)]
        xo = [k.sb(f"m_xo{i}", [128, D], F32) for i in range(2)]
        ti = 0
        for bi, (t0, n) in enumerate(cfg.TB):
            i = bi % 2
            k.dma(out=ysb[i][:, :, 0:n], in_=self.yT[:, :, :, t0:t0 + n].rearrange("a b p t -> p (a b) t"))
            k.dma(out=gsb[i][:, :, 0:n], in_=self.gT[:, :, t0:t0 + n].rearrange("j p t -> p j t"))
            for j in range(8):
                for nb in range(4):
                    p = ps[(j * 4 + nb) % 4]
                    for kc in range(4):
                        self.mm(p[:, 0:n], bp[:, nb, kc, 128 * j:128 * j + 128], ysb[i][:, nb * 4 + kc, 0:n], kc == 0, kc == 3)
                    g_ = gsb[i][:, nb * 8 + j, 0:n]
                    if nb == 0:
                        k.op("dve", "tensor_tensor", out=acc[:, 0:n], in0=p[:, 0:n], in1=g_, op=ALU.mult)
                    else:
                        t_ = tmp[nb % 2]
                        k.op("dve", "tensor_tensor", out=t_[:, 0:n], in0=p[:, 0:n], in1=g_, op=ALU.mult)
                        if nb < 3:
                            k.op("pool", "tensor_tensor", out=acc[:, 0:n], in0=acc[:, 0:n], in1=t_[:, 0:n], op=ALU.add)
                        else:
                            k.op("pool", "tensor_tensor", out=mT[:, j, 0:n], in0=acc[:, 0:n], in1=t_[:, 0:n], op=ALU.add)
            for tt in range(n // 128):
                row0 = t0 + 128 * tt
                x_, o_ = xt[ti % 2], xo[ti % 2]
                ti += 1
                k.dma(out=x_[:], in_=self.xs[row0:row0 + 128, :])
                for half in range(2):
                    p = ps[4 + (tt * 2 + half) % 4]
                    hs_ = slice(512 * half, 512 * half + 512)
                    for j in range(8):
                        self.mm(p[:, :], mT[:, j, 128 * tt:128 * tt + 128], wo[:, j, hs_], j == 0, j == 7)
                    k.op("dve", "tensor_tensor", out=o_[:, hs_], in0=p[:, :], in1=x_[:, hs_], op=ALU.add)
                k.dma(out=self.xs2[row0:row0 + 128, :], in_=o_[:], q="pool", wd=True)
        k.barrier()
        k.pop()

    def p8(self, l, s, last):
        k, cfg = self.k, self.cfg
        ps = self.ps
        LP = cfg.LP
        k.push()
        w1 = k.sb("f_w1", [128, 8, DFF], BF16)
        w2 = k.sb("f_w2", [128, 32, D], BF16)
        w1v = self.wb_w1[l].rearrange("(kc p) f -> p kc f", p=128)
        w2v = self.wb_w2[l].rearrange("(kc p) o -> p kc o", p=128)
        for q_ in range(4):
            k.dma(out=w1[:, :, 1024 * q_:1024 * (q_ + 1)], in_=w1v[:, :, 1024 * q_:1024 * (q_ + 1)], wd=True)
            k.dma(out=w2[:, 8 * q_:8 * (q_ + 1), :], in_=w2v[:, 8 * q_:8 * (q_ + 1), :], wd=True)
        gb = k.sb("f_gb", [128, D], F32)
        k.dma(out=gb[:], in_=self.W["norm_mlp_g"][l].partition_broadcast(128))
        xt = [k.sb(f"f_xt{i}", [128, D], F32) for i in range(2)]
        hb = k.sb("f_hb", [128, D], BF16)
        sq = k.sb("f_sq", [128, D], BF16)
        st = k.sb("f_st", [128, 2], F32)
        h2T = k.sb("f_h2T", [128, 8, 256], BF16)
        uT = k.sb("f_uT", [128, 32, 256], BF16)
        rl = [k.sb(f"f_rl{i}", [128, 256], F32) for i in range(2)]
        xo = [k.sb(f"f_xo{i}", [128, D], F32) for i in range(2)]
        oi = 0
        for t0 in range(0, LP, 256):
            n = min(256, LP - t0)
            nt = n // 128
            for tt in range(nt):
                k.dma(out=xt[tt][:], in_=self.xs2[t0 + 128 * tt:t0 + 128 * tt + 128, :])
                self.norm_T(xt[tt][:], gb, hb, st, sq, h2T[:, :, 128 * tt:128 * tt + 128], ps[tt % 2])
            for jj in range(32):
                p = ps[2 + jj % 4]
                for kc in range(8):
                    self.mm(p[:, 0:n], w1[:, kc, 128 * jj:128 * jj + 128], h2T[:, kc, 0:n], kc == 0, kc == 7)
                r_ = rl[jj % 2]
                k.op("act", "activation", out=r_[:, 0:n], in_=p[:, 0:n], func=AF.Relu)
                k.op("pool", "tensor_tensor", out=uT[:, jj, 0:n], in0=r_[:, 0:n], in1=r_[:, 0:n], op=ALU.mult)
            for tt in range(nt):
                row0 = t0 + 128 * tt
                o_ = xo[oi % 2]
                oi += 1
                for half in range(2):
                    p = ps[6 + half]
                    hs_ = slice(512 * half, 512 * half + 512)
                    for jj in range(32):
                        self.mm(p[:, :], uT[:, jj, 128 * tt:128 * tt + 128], w2[:, jj, hs_], jj == 0, jj == 31)
                    k.op("dve", "tensor_tensor", out=o_[:, hs_], in0=p[:, :], in1=xt[tt][:, hs_], op=ALU.add)
                if not last:
                    k.dma(out=self.xs[row0:row0 + 128, :], in_=o_[:], q="pool", wd=True)
                elif row0 >= 128:
                    k.dma(out=self.yout[s][row0 - 128:row0, :], in_=o_[:], q="pool", wd=True)
        k.barrier()
        k.pop()

    def build(self):
        k, cfg = self.k, self.cfg
        stop = self.stop
        self.setup()
        done = False
        for s in range(cfg.NSEQ):
            self.p0(s)
            for l in range(cfg.DEPTH):
                self.p1(l)
                if stop == "p1":
                    k.pop(); done = True; break
                self.pa(l)
                self.p2(l)
                k.pop()
                if stop == "p2":
                    done = True; break
                self.p3(l)
                if stop == "p3":
                    done = True; break
                self.p4(l)
                if stop == "p4":
                    done = True; break
                self.p5(l)
                if stop == "p5":
                    done = True; break
                self.p6a(l)
                self.p6b(l)
                if stop == "p6x":
                    self.p6b(l)
                    done = True; break
                if stop == "p6":
                    done = True; break
                self.p7(l)
                if stop == "p7":
                    done = True; break
                self.p8(l, s, l == cfg.DEPTH - 1)
                if stop == "p8":
                    done = True; break
            if done:
                break
        k.finish()
        print(f"[build] inst={k.n_inst} waits={k.n_wait} sems={k.n_sems} cnt={k.cnt}")
        return self.nc


_CACHE = {}


def kernel(**inputs):
    x_prompt = np.ascontiguousarray(inputs["x_prompt"], dtype=np.float32)
    x_sample = np.ascontiguousarray(inputs["x_sample"], dtype=np.float32)
    B, S, _ = x_prompt.shape
    DB = x_sample.shape[0]
    DEPTH = inputs["w_in"].shape[0]
    n_cores = 8
    assert B == n_cores and DB <= n_cores and x_sample.shape[1] == S
    cfg = Cfg(S, 2, DEPTH)
    key = (S, DEPTH)
    m = M(cfg)
    nc = m.build()
    consts = host_consts(cfg)
    wts = {n: np.ascontiguousarray(inputs[n], dtype=np.float32) for n in WSHAPES(DEPTH)}
    zeros = np.zeros((S, D), np.float32)
    in_maps = []
    for c in range(n_cores):
        im = dict(wts)
        im.update(consts)
        im["xin0"] = x_prompt[c]
        im["xin1"] = x_sample[c] if c < DB else zeros
        in_maps.append(im)
    res = run_bass_kernel_spmd(nc, in_maps, core_ids=list(range(n_cores)))
    y_prompt = np.stack([np.asarray(res.results[c]["yout0"], dtype=np.float32) for c in range(n_cores)])
    y_sample = np.stack([np.asarray(res.results[c]["yout1"], dtype=np.float32) for c in range(DB)])
    return (y_prompt, y_sample)
```

```python
import numpy as np
import ml_dtypes
import concourse.bass as bass
import concourse.mybir as mybir
from concourse.bass_utils import run_bass_kernel_spmd

F32 = mybir.dt.float32
BF16 = mybir.dt.bfloat16
AF = mybir.ActivationFunctionType
ALU = mybir.AluOpType
AX = mybir.AxisListType

EPOCH = 24000
DMA_SEM_MAX = 24000


class Obj:
    def __init__(self, k, t, name, is_dram=False):
        self.k = k
        self.t = t
        self.name = name
        self.is_dram = is_dram
        self.w = []
        self.r = []
        self.sem = None

    def ap(self):
        return self.t.ap() if self.is_dram else self.t[:]

    def __getitem__(self, key):
        base = self.t.ap() if self.is_dram else self.t
        return V(self, base[key])


class SubObj(Obj):
    def __init__(self, view_ap, name):
        self.t = None
        self.view = view_ap
        self.name = name
        self.is_dram = False
        self.w = []
        self.r = []
        self.sem = None

    def __getitem__(self, key):
        return V(self, self.view[key])


class V:
    def __init__(self, obj, ap):
        self.obj = obj
        self.ap = ap

    def __getitem__(self, key):
        return V(self.obj, self.ap[key])

    def __getattr__(self, name):
        attr = getattr(self.ap, name)
        if callable(attr):
            def f(*a, **kw):
                r = attr(*a, **kw)
                if isinstance(r, bass.AP):
                    return V(self.obj, r)
                return r
            return f
        return attr


ENGS = ("pe", "act", "dve", "pool", "sp")


class K:
    def __init__(self, nc):
        self.nc = nc
        self.eng = {"pe": nc.tensor, "act": nc.scalar, "dve": nc.vector,
                    "pool": nc.gpsimd, "sp": nc.sync}
        self.cnt = {e: 0 for e in ENGS}
        self.esems = {e: [] for e in ENGS}
        self.known = {e: {} for e in ENGS}
        self.free_dma_sems = []
        self.all_dma_sems = []
        self.n_sems = 0
        self.n_inst = 0
        self.n_wait = 0
        self.sb_off = 0
        self.sb_base = 16640
        self.sb_limit = 229376
        self.sb_stack = []
        self.objs_live = []
        self.uid = 0

    def _new_sem(self, name):
        self.n_sems += 1
        return self.nc.alloc_semaphore(name)

    def _esem(self, e, idx):
        lst = self.esems[e]
        while len(lst) <= idx:
            lst.append(self._new_sem(f"e_{e}_{len(lst)}"))
        return lst[idx]

    def _dma_sem(self, obj):
        if obj.sem is None or obj.sem[1] > DMA_SEM_MAX:
            if self.free_dma_sems:
                obj.sem = self.free_dma_sems.pop()
            else:
                ent = [self._new_sem(f"d_{len(self.all_dma_sems)}"), 0]
                self.all_dma_sems.append(ent)
                obj.sem = ent
            if obj.sem[1] > DMA_SEM_MAX:
                ent = [self._new_sem(f"d_{len(self.all_dma_sems)}"), 0]
                self.all_dma_sems.append(ent)
                obj.sem = ent
        return obj.sem

    def _resolve(self, ev):
        if ev[0] == "e":
            _, e, n = ev
            return self._esem(e, (n - 1) // EPOCH), (n - 1) % EPOCH + 1
        else:
            ent = ev[1]
            return ent[0], ent[1]

    def _wait(self, e, ev):
        if ev[0] == "e" and ev[1] == "pe" and e == "pe":
            return
        sem, val = self._resolve(ev)
        key = sem.name if hasattr(sem, "name") else id(sem)
        if self.known[e].get(key, -1) >= val:
            return
        self.known[e][key] = val
        self.eng[e].wait_ge(sem, val)
        self.n_wait += 1

    def _deps(self, e, reads, writes, wd=False):
        for o in reads:
            for ev in o.w:
                self._wait(e, ev)
            if getattr(o, "is_psum", False):
                for ev in o.r:
                    if not (ev[0] == "e" and ev[1] == e):
                        self._wait(e, ev)
        for o in writes:
            if not wd:
                for ev in o.w:
                    self._wait(e, ev)
            for ev in o.r:
                self._wait(e, ev)

    def _record(self, ev, reads, writes, wd=False):
        for o in reads:
            if o in writes:
                continue
            o.r.append(ev)
            if len(o.r) > 12:
                o.r = self._compact(o.r)
        for o in writes:
            if wd:
                o.w.append(ev)
                if len(o.w) > 12:
                    o.w = self._compact(o.w)
            else:
                o.w = [ev]
            o.r = []

    @staticmethod
    def _compact(evs):
        best = {}
        out = []
        for ev in evs:
            if ev[0] == "e":
                if ev[1] not in best or best[ev[1]][2] < ev[2]:
                    best[ev[1]] = ev
            else:
                if not any(x[0] == "d" and x[1] is ev[1] for x in out):
                    out.append(ev)
        return out + list(best.values())

    def op(self, e, method, **kw):
        reads, writes, args = [], [], {}
        wd = kw.pop("wd", False)
        for name, v in kw.items():
            if isinstance(v, V):
                (writes if name in ("out", "accum_out", "ap") else reads).append(v.obj)
                args[name] = v.ap
            else:
                args[name] = v
        self._deps(e, reads, writes, wd)
        ins = getattr(self.eng[e], method)(**args)
        self.cnt[e] += 1
        n = self.cnt[e]
        ins.then_inc(self._esem(e, (n - 1) // EPOCH), 1)
        self._record(("e", e, n), reads, writes, wd)
        self.n_inst += 1
        return ins

    def dma(self, out, in_, q="sp", wd=False, **kw):
        oo, io = out.obj, in_.obj
        owner = io if (oo.is_dram and not io.is_dram) else oo
        self._deps(q, [io], [oo], wd)
        ent = self._dma_sem(owner)
        ins = self.eng[q].dma_start(out=out.ap, in_=in_.ap, **kw)
        ent[1] += 16
        ins.then_inc(ent[0], 16)
        self._record(("d", ent), [io], [oo], wd)
        self.n_inst += 1
        return ins

    def barrier(self):
        for e in ENGS:
            for f in ENGS:
                if f == "sp" or self.cnt[f] == 0:
                    continue
                if f == e:
                    continue
                self._wait(e, ("e", f, self.cnt[f]))
            for ent in self.all_dma_sems:
                if ent[1] > 0:
                    self._wait(e, ("d", ent))

    def finish(self):
        self.barrier()

    def dram(self, name, shape, dtype, kind="Internal"):
        t = self.nc.dram_tensor(name, list(shape), dtype, kind=kind)
        return Obj(self, t, name, is_dram=True)

    def push(self):
        self.sb_stack.append((self.sb_off, len(self.objs_live)))

    def pop(self):
        off, n = self.sb_stack.pop()
        for o in self.objs_live[n:]:
            if o.sem is not None:
                self.free_dma_sems.append(o.sem)
                o.sem = None
        del self.objs_live[n:]
        self.sb_off = off

    def sb(self, name, shape, dtype):
        nbytes = int(np.prod(shape[1:])) * mybir.dt.size(dtype)
        nbytes = (nbytes + 63) // 64 * 64
        self.uid += 1
        t = self.nc.alloc_sbuf_tensor_at(f"{name}_{self.uid}", list(shape), dtype,
                                         offset=self.sb_base + self.sb_off)
        self.sb_off += nbytes
        assert self.sb_base + self.sb_off <= self.sb_limit, \
            f"SBUF overflow: {self.sb_off} at {name}"
        o = Obj(self, t, name)
        self.objs_live.append(o)
        return o

    def ps(self, name, shape, dtype=F32):
        t = self.nc.alloc_psum_tensor(name, list(shape), dtype)
        o = Obj(self, t, name)
        o.is_psum = True
        return o


D = 1024
NIN = 7552
DFF = 4096
PAD = 112
NORM_EPS = 1e-6
SUBLN_EPS = 1e-5
RW_LNX_EPS = 64e-5
KAPPA = float(np.exp(-0.5))
B_HQ, B_HFF, B_HFB, B_HI, B_HG = 0, 4, 8, 12, 16
B_SB, B_SC, B_SH = 20, 24, 28
B_DQ, B_DK, B_DV = 32, 36, 40
B_RW = 44
NZB = 59
C_ONORM, C_CONV, C_QN, C_KN, C_SUBLN, C_MU, C_W0, C_A0, C_KK, C_KA, C_RK = 0, 4, 16, 17, 18, 19, 34, 42, 50, 54, 58
RPL = 62

WSHAPES = lambda DEPTH: {
    "meta_tokens": (16, D), "norm_mix_g": (DEPTH, D), "w_in": (DEPTH, D, NIN),
    "hgrn_lb_logits": (2, DEPTH, 512), "hgrn_onorm_g": (DEPTH, 512), "conv_w": (DEPTH, 3, 512),
    "diff_qnorm_g": (DEPTH, 64), "diff_knorm_g": (DEPTH, 64), "diff_lambda": (DEPTH, 4, 64),
    "diff_subln_g": (DEPTH, 128), "rwkv_mu": (DEPTH, 1920), "rwkv_w0": (DEPTH, 2, 512),
    "rwkv_w2": (DEPTH, 2, 64, 512), "rwkv_a0": (DEPTH, 2, 512), "rwkv_a2": (DEPTH, 2, 64, 512),
    "rwkv_g2": (DEPTH, 128, 512), "rwkv_k_k": (DEPTH, 512), "rwkv_k_a": (DEPTH, 512),
    "rwkv_r_k": (DEPTH, 8, 64), "rwkv_lnx_g": (DEPTH, 512), "rwkv_lnx_b": (DEPTH, 512),
    "w_gate": (DEPTH, D, 4 * D), "branch_proj": (DEPTH, 4, 512, D), "w_out": (DEPTH, D, D),
    "norm_mlp_g": (DEPTH, D), "mlp_w1": (DEPTH, D, DFF), "mlp_w2": (DEPTH, DFF, D),
}


class Cfg:
    def __init__(self, S, NSEQ, DEPTH):
        assert S % 128 == 0
        self.S, self.NSEQ, self.DEPTH = S, NSEQ, DEPTH
        self.L = S + 16
        self.LP = S + 128
        self.NT = self.LP // 128
        self.TB = [(c, min(512, self.LP - c)) for c in range(0, self.LP, 512)]
        self.NKT = (self.L + 127) // 128
        self.NC64 = self.LP // 64


def host_consts(cfg):
    LP = cfg.LP
    pos = (np.arange(LP, dtype=np.float32) - PAD).astype(np.float32)
    inv = (500000.0 ** (-np.arange(8, dtype=np.float32) / 8)).astype(np.float32)
    C = np.ones((128, LP), np.float32)
    Sn = np.zeros((128, LP), np.float32)
    rotT = np.zeros((128, 128), np.float32)
    for p in range(128):
        d = p % 64
        if d < 16:
            ang = (pos * inv[d % 8]).astype(np.float32)
            C[p] = np.cos(ang)
            Sn[p] = np.sin(ang)
            if d < 8:
                rotT[p + 8, p] = -1.0
            else:
                rotT[p - 8, p] = 1.0
    idx = np.arange(128)
    lv = np.zeros((14, 128, 128), np.float32)
    for kk_ in range(7):
        b = 1 << kk_
        same = (idx[:, None] // (2 * b)) == (idx[None, :] // (2 * b))
        mk = same & ((idx[:, None] % (2 * b)) < b) & ((idx[None, :] % (2 * b)) >= b)
        lv[kk_] = mk
        lv[7 + kk_] = mk.T
    return {"c_rope": np.stack([C, Sn]).astype(np.float32), "c_rotT": rotT, "c_lvl": lv}


class M:
    def __init__(self, cfg, debug=False, stop=None):
        self.cfg = cfg
        self.debug = debug
        self.stop = stop
        nc = bass.Bass("TRN2", target_bir_lowering=False)
        self.nc = nc
        k = K(nc)
        self.k = k
        S, LP, NSEQ, DEPTH = cfg.S, cfg.LP, cfg.NSEQ, cfg.DEPTH
        dk = "ExternalOutput" if debug else "Internal"
        self.xin = [k.dram(f"xin{s}", [S, D], F32, kind="ExternalInput") for s in range(NSEQ)]
        self.yout = [k.dram(f"yout{s}", [S, D], F32, kind="ExternalOutput") for s in range(NSEQ)]
        self.W = {n: k.dram(n, list(sh), F32, kind="ExternalInput") for n, sh in WSHAPES(DEPTH).items()}
        self.c_rope = k.dram("c_rope", [2, 128, LP], F32, kind="ExternalInput")
        self.c_rotT = k.dram("c_rotT", [128, 128], F32, kind="ExternalInput")
        self.c_lvl = k.dram("c_lvl", [14, 128, 128], F32, kind="ExternalInput")
        self.wb_in = k.dram("wb_in", [DEPTH, D, NIN], BF16)
        self.wb_gate = k.dram("wb_gate", [DEPTH, D, 4 * D], BF16)
        self.wb_bp = k.dram("wb_bp", [DEPTH, 4, 512, D], BF16)
        self.wb_out = k.dram("wb_out", [DEPTH, D, D], BF16)
        self.wb_w1 = k.dram("wb_w1", [DEPTH, D, DFF], BF16)
        self.wb_w2 = k.dram("wb_w2", [DEPTH, DFF, D], BF16)
        self.wb_rw2 = k.dram("wb_rw2", [DEPTH, 128, 512], BF16, kind=dk)
        self.wb_ra2 = k.dram("wb_ra2", [DEPTH, 128, 512], BF16, kind=dk)
        self.wb_rg2 = k.dram("wb_rg2", [DEPTH, 128, 512], BF16, kind=dk)
        self.xs = k.dram("xs", [LP, D], F32, kind=dk)
        self.xs2 = k.dram("xs2", [LP, D], F32, kind=dk)
        self.zT = k.dram("zT", [NZB, 128, LP], F32, kind=dk)
        self.gT = k.dram("gT", [32, 128, LP], BF16, kind=dk)
        self.vtok = k.dram("vtok", [cfg.NKT * 128, 512], BF16, kind=dk)
        self.yT = k.dram("yT", [4, 4, 128, LP], BF16, kind=dk)
        self.xmT = k.dram("xmT", [15, 128, LP], BF16, kind=dk)
        self.ps = [k.ps(f"ps{i}", [128, 512], F32) for i in range(8)]
        self.ei = 0

    def evac(self, out, in_, eng=None):
        k = self.k
        if eng is None:
            eng = ("act", "dve")[self.ei % 2]
            self.ei += 1
        if eng == "act":
            k.op("act", "activation", out=out, in_=in_, func=AF.Copy)
        else:
            k.op(eng, "tensor_copy", out=out, in_=in_)

    def mm(self, out, lhsT, rhs, start=True, stop=True):
        self.k.op("pe", "matmul", out=out, lhsT=lhsT, rhs=rhs, start=start, stop=stop)

    def setup(self):
        k, cfg, W = self.k, self.cfg, self.W
        DEPTH = cfg.DEPTH
        self.idb = k.sb("idb", [128, 128], BF16)
        self.idf = k.sb("idf", [128, 128], F32)
        self.ones_bf = k.sb("ones_bf", [128, 128], BF16)
        self.bones_bf = k.sb("bones_bf", [128, 128], BF16)
        self.MU = k.sb("MU", [128, 128], F32)
        self.MUi = k.sb("MUi", [128, 128], F32)
        self.ML = k.sb("ML", [128, 128], F32)
        self.MLi = k.sb("MLi", [128, 128], F32)
        self.rotT_bf = k.sb("rotT_bf", [128, 128], BF16)
        self.prm = [k.sb(f"prm{l}", [128, RPL], F32) for l in range(DEPTH)]
        self.lbc = k.sb("lbc", [128, 2, DEPTH, 4, 2], F32)
        self.nlam = k.sb("nlam", [128, DEPTH], F32)
        self.epsc = k.sb("epsc", [128, 2], F32)
        for t, c in ((self.idb, 0.0), (self.idf, 0.0), (self.ones_bf, 1.0), (self.bones_bf, 0.0),
                     (self.MU, 1.0), (self.MUi, 1.0), (self.ML, 1.0), (self.MLi, 1.0)):
            k.op("pool", "memset", ap=t[:], constant=c)
        for t in (self.idb, self.idf):
            k.op("pool", "affine_select", out=t[:], in_=t[:], pattern=[[-1, 128]],
                 compare_op=ALU.not_equal, fill=1.0, base=0, channel_multiplier=1)
        for t, cmp, sg in ((self.MU, ALU.is_gt, -1), (self.MUi, ALU.is_ge, -1), (self.ML, ALU.is_gt, 1), (self.MLi, ALU.is_ge, 1)):
            k.op("pool", "affine_select", out=t[:], in_=t[:], pattern=[[-sg, 128]],
                 compare_op=cmp, fill=0.0, base=0, channel_multiplier=sg)
        k.op("pool", "memset", ap=self.epsc[:, 0:1], constant=NORM_EPS)
        k.op("pool", "memset", ap=self.epsc[:, 1:2], constant=SUBLN_EPS)
        k.op("pool", "memset", ap=self.bones_bf[0:64, 0:64], constant=1.0)
        k.op("pool", "memset", ap=self.bones_bf[64:128, 64:128], constant=1.0)
        k.dma(out=self.rotT_bf[:], in_=self.c_rotT[:], q="pool")
        for l in range(DEPTH):
            for r in range(8):
                rs = slice(128 * r, 128 * (r + 1))
                k.dma(out=self.wb_in[l, rs, :], in_=W["w_in"][l, rs, :], q="pool", wd=True)
                k.dma(out=self.wb_gate[l, rs, :], in_=W["w_gate"][l, rs, :], q="pool", wd=True)
                k.dma(out=self.wb_w1[l, rs, :], in_=W["mlp_w1"][l, rs, :], q="pool", wd=True)
                k.dma(out=self.wb_out[l, rs, :], in_=W["w_out"][l, rs, :], q="pool", wd=True)
            for r in range(4):
                k.dma(out=self.wb_w2[l, 1024 * r:1024 * (r + 1), :], in_=W["mlp_w2"][l, 1024 * r:1024 * (r + 1), :], q="pool", wd=True)
                k.dma(out=self.wb_bp[l, r], in_=W["branch_proj"][l, r], q="pool", wd=True)
            k.dma(out=self.wb_rw2[l], in_=W["rwkv_w2"][l].rearrange("d r c -> (d r) c"), q="pool", wd=True)
            k.dma(out=self.wb_ra2[l], in_=W["rwkv_a2"][l].rearrange("d r c -> (d r) c"), q="pool", wd=True)
            k.dma(out=self.wb_rg2[l], in_=W["rwkv_g2"][l], q="pool", wd=True)
        k.push()
        for l in range(DEPTH):
            pr = k.sb(f"pr{l}", [RPL, 128], F32)
            def rows(r0, ap, n):
                k.dma(out=pr[r0:r0 + n, :], in_=ap, wd=True)
            rows(C_ONORM, W["hgrn_onorm_g"][l].rearrange("(b p) -> b p", p=128), 4)
            rows(C_CONV, W["conv_w"][l].rearrange("j (b p) -> (j b) p", p=128), 12)
            for h in range(2):
                k.dma(out=pr[C_QN:C_QN + 1, 64 * h:64 * h + 64], in_=W["diff_qnorm_g"][l:l + 1, :], wd=True)
                k.dma(out=pr[C_KN:C_KN + 1, 64 * h:64 * h + 64], in_=W["diff_knorm_g"][l:l + 1, :], wd=True)
            rows(C_SUBLN, W["diff_subln_g"][l:l + 1, :], 1)
            rows(C_MU, W["rwkv_mu"][l].rearrange("(b p) -> b p", p=128), 15)
            rows(C_W0, W["rwkv_w0"][l].rearrange("d (b p) -> (d b) p", p=128), 8)
            rows(C_A0, W["rwkv_a0"][l].rearrange("d (b p) -> (d b) p", p=128), 8)
            rows(C_KK, W["rwkv_k_k"][l].rearrange("(b p) -> b p", p=128), 4)
            rows(C_KA, W["rwkv_k_a"][l].rearrange("(b p) -> b p", p=128), 4)
            rows(C_RK, W["rwkv_r_k"][l].rearrange("(b q) n -> b (q n)", q=2), 4)
            k.op("pe", "transpose", out=self.ps[0][:, 0:RPL], in_=pr[:, :], identity=self.idf[0:RPL, 0:RPL])
            k.op("dve", "tensor_copy", out=self.prm[l][:], in_=self.ps[0][:, 0:RPL])
        nl = 2 * DEPTH * 4
        pl = k.sb("pl", [nl, 128], F32)
        k.dma(out=pl[:, :], in_=W["hgrn_lb_logits"][:].rearrange("d l (b p) -> (d l b) p", p=128))
        k.op("pe", "transpose", out=self.ps[1][:, 0:nl], in_=pl[:, :], identity=self.idf[0:nl, 0:nl])
        lg = k.sb("lg", [128, 2, DEPTH, 4], F32)
        k.op("dve", "tensor_copy", out=lg[:].rearrange("p d l b -> p (d l b)"), in_=self.ps[1][:, 0:nl])
        mx = k.sb("mx", [128, 2, 4], F32)
        sm = k.sb("sm", [128, 2, 4], F32)
        k.op("dve", "tensor_copy", out=mx[:], in_=lg[:, :, 0, :])
        for l in range(1, DEPTH):
            k.op("dve", "tensor_tensor", out=mx[:], in0=mx[:], in1=lg[:, :, l, :], op=ALU.max)
        for l in range(DEPTH):
            k.op("dve", "tensor_tensor", out=lg[:, :, l, :], in0=lg[:, :, l, :], in1=mx[:], op=ALU.subtract)
        k.op("act", "activation", out=lg[:].rearrange("p d l b -> p (d l b)"), in_=lg[:].rearrange("p d l b -> p (d l b)"), func=AF.Exp)
        k.op("dve", "tensor_copy", out=sm[:], in_=lg[:, :, 0, :])
        for l in range(1, DEPTH):
            k.op("dve", "tensor_tensor", out=sm[:], in0=sm[:], in1=lg[:, :, l, :], op=ALU.add)
        k.op("dve", "reciprocal", out=sm[:], in_=sm[:])
        for l in range(DEPTH):
            k.op("dve", "tensor_tensor", out=lg[:, :, l, :], in0=lg[:, :, l, :], in1=sm[:], op=ALU.mult)
        acc = k.sb("acc", [128, 2, 4], F32)
        k.op("dve", "memset", ap=acc[:], constant=0.0)
        for l in range(DEPTH):
            if l > 0:
                k.op("dve", "tensor_tensor", out=acc[:], in0=acc[:], in1=lg[:, :, l, :], op=ALU.add)
            k.op("dve", "tensor_scalar", out=self.lbc[:, :, l, :, 0], in0=acc[:], scalar1=-1.0, scalar2=1.0, op0=ALU.mult, op1=ALU.add)
            k.op("dve", "tensor_scalar", out=self.lbc[:, :, l, :, 1], in0=acc[:], scalar1=1e-20, scalar2=None, op0=ALU.max)
        lt = k.sb("lt", [128, DEPTH, 256], F32)
        pr2 = k.sb("pr2", [128, DEPTH, 2, 64], F32)
        ss = k.sb("ss2", [128, DEPTH, 2], F32)
        for l in range(DEPTH):
            k.dma(out=lt[:, l, :], in_=W["diff_lambda"][l].rearrange("a n -> (a n)").partition_broadcast(128), wd=True)
        for l in range(DEPTH):
            for j in range(2):
                k.op("dve", "tensor_tensor", out=pr2[:, l, j, :], in0=lt[:, l, 128 * j:128 * j + 64], in1=lt[:, l, 128 * j + 64:128 * j + 128], op=ALU.mult)
                k.op("dve", "tensor_reduce", out=ss[:, l, j:j + 1], in_=pr2[:, l, j, :], axis=AX.X, op=ALU.add)
        k.op("act", "activation", out=ss[:].rearrange("p l j -> p (l j)"), in_=ss[:].rearrange("p l j -> p (l j)"), func=AF.Exp)
        for l in range(DEPTH):
            lam_init = 0.8 - 0.6 * float(np.exp(-0.3 * l))
            k.op("dve", "tensor_tensor", out=self.nlam[:, l:l + 1], in0=ss[:, l, 1:2], in1=ss[:, l, 0:1], op=ALU.subtract)
            k.op("dve", "tensor_scalar", out=self.nlam[:, l:l + 1], in0=self.nlam[:, l:l + 1], scalar1=-lam_init, scalar2=None, op0=ALU.add)
        if self.debug:
            self.dbg_prm = k.dram("dbg_prm", [cfg.DEPTH, 128, RPL], F32, kind="ExternalOutput")
            for l in range(DEPTH):
                k.dma(out=self.dbg_prm[l], in_=self.prm[l][:], q="pool", wd=True)
        k.barrier()
        k.pop()

    def p0(self, s):
        k, cfg = self.k, self.cfg
        k.push()
        zt = k.sb("zt", [128, D], F32)
        k.op("pool", "memset", ap=zt[:], constant=0.0)
        k.dma(out=self.xs[0:PAD, :], in_=zt[0:PAD, :], q="pool", wd=True)
        k.dma(out=self.xs[PAD:128, :], in_=self.W["meta_tokens"][:, :], wd=True)
        for r in range(0, cfg.S, 512):
            rr = min(512, cfg.S - r)
            k.dma(out=self.xs[128 + r:128 + r + rr, :], in_=self.xin[s][r:r + rr, :], wd=True)
        k.barrier()
        k.pop()

    def norm_T(self, x_, gb, hb, st, sq, dst, pst):
        k = self.k
        k.op("pool", "memset", ap=st[:], constant=0.0)
        k.op("act", "activation", out=sq[:], in_=x_, func=AF.Square, accum_out=st[:, 0:1])
        k.op("act", "activation", out=st[:, 1:2], in_=st[:, 0:1], func=AF.Sqrt, scale=1.0 / D, bias=NORM_EPS)
        k.op("dve", "reciprocal", out=st[:, 1:2], in_=st[:, 1:2])
        k.op("dve", "scalar_tensor_tensor", out=hb[:], in0=x_, scalar=st[:, 1:2], in1=gb[:], op0=ALU.mult, op1=ALU.mult)
        pb = pst[:].bitcast(BF16)
        for kc in range(8):
            k.op("pe", "transpose", out=pb[:, kc * 128:(kc + 1) * 128], in_=hb[:, kc * 128:(kc + 1) * 128], identity=self.idb[:])
        self.evac(dst, pb[:, :].rearrange("p (kc t) -> p kc t", kc=8))

    def p1(self, l):
        k, cfg = self.k, self.cfg
        LP, NT = cfg.LP, cfg.NT
        k.push()
        self.hT = k.sb("hT", [128, 8, LP], BF16)
        k.push()
        gb = k.sb("gb", [128, D], F32)
        k.dma(out=gb[:], in_=self.W["norm_mix_g"][l].partition_broadcast(128))
        xt = [k.sb(f"xt{i}", [128, D], F32) for i in range(2)]
        hb = [k.sb(f"hb{i}", [128, D], BF16) for i in range(2)]
        sq = k.sb("sq", [128, D], F32)
        st = [k.sb(f"st{i}", [128, 2], F32) for i in range(2)]
        for i in range(NT):
            x_ = xt[i % 2]
            k.dma(out=x_[:], in_=self.xs[128 * i:128 * (i + 1), :])
            self.norm_T(x_[:], gb, hb[i % 2], st[i % 2], sq, self.hT[:, :, 128 * i:128 * (i + 1)], self.ps[i % 2])
        k.op("pool", "memset", ap=self.hT[:, :, 0:PAD], constant=0.0)
        k.barrier()
        k.pop()

    def proj_fm(self, wsrc, ncols, dst, skip=(), sigmoid=False, odt=F32, tag="pa"):
        k, cfg = self.k, self.cfg
        LP = cfg.LP
        k.push()
        wt = [k.sb(f"{tag}_wt{i}", [128, 8, 512], BF16) for i in range(2)]
        stg = [k.sb(f"{tag}_stg{i}", [128, LP], odt) for i in range(3)]
        wv = wsrc.rearrange("(kc p) n -> p kc n", p=128)
        cnt = 0
        pi = 0
        for g in range((ncols + 511) // 512):
            c0 = 512 * g
            cw = min(512, ncols - c0)
            w_ = wt[g % 2]
            k.dma(out=w_[:, :, 0:cw], in_=wv[:, :, c0:c0 + cw])
            for m in range(cw // 128):
                j = c0 // 128 + m
                if j in skip:
                    continue
                s_ = stg[cnt % 3]
                cnt += 1
                for (t0, n) in cfg.TB:
                    p = self.ps[pi % 4]
                    pi += 1
                    for kc in range(8):
                        self.mm(p[:, 0:n], w_[:, kc, 128 * m:128 * m + 128], self.hT[:, kc, t0:t0 + n], kc == 0, kc == 7)
                    if sigmoid:
                        k.op("act", "activation", out=s_[:, t0:t0 + n], in_=p[:, 0:n], func=AF.Sigmoid)
                    else:
                        self.evac(s_[:, t0:t0 + n], p[:, 0:n])
                k.dma(out=dst[j], in_=s_[:], q="pool", wd=True)
        k.barrier()
        k.pop()

    def pa(self, l):
        k, cfg = self.k, self.cfg
        self.proj_fm(self.wb_in[l], NIN, self.zT, skip=(40, 41, 42, 43), tag="pa")
        k.push()
        wvv = k.sb("wvv", [128, 8, 512], BF16)
        k.dma(out=wvv[:], in_=self.wb_in[l].rearrange("(kc p) n -> p kc n", p=128)[:, :, 5120:5632])
        vst = [k.sb(f"vst{i}", [128, 512], BF16) for i in range(2)]
        for kt in range(cfg.NKT):
            c = PAD + 128 * kt
            kn = min(128, cfg.LP - c)
            p = self.ps[kt % 4]
            for kc in range(8):
                self.mm(p[0:kn, :], self.hT[:, kc, c:c + kn], wvv[:, kc, :], kc == 0, kc == 7)
            self.evac(vst[kt % 2][0:kn, :], p[0:kn, :])
            k.dma(out=self.vtok[128 * kt:128 * kt + kn, :], in_=vst[kt % 2][0:kn, :], q="pool", wd=True)
        k.barrier()
        k.pop()

    def p2(self, l):
        self.proj_fm(self.wb_gate[l], 4 * D, self.gT, sigmoid=True, odt=BF16, tag="pg")

    def p3(self, l):
        k, cfg = self.k, self.cfg
        LP = cfg.LP
        prm = self.prm[l]
        k.push()
        zb = [k.sb(f"c_zb{i}", [128, LP], F32) for i in range(2)]
        zc = [k.sb(f"c_zc{i}", [128, LP], F32) for i in range(2)]
        zh = [k.sb(f"c_zh{i}", [128, LP], F32) for i in range(2)]
        u = k.sb("c_u", [128, LP + 2], F32)
        t1 = k.sb("c_t1", [128, LP], F32)
        yb = [k.sb(f"c_y{i}", [128, LP], BF16) for i in range(2)]
        k.op("pool", "memset", ap=u[:], constant=0.0)
        for b in range(4):
            i = b % 2
            k.dma(out=zb[i][:], in_=self.zT[B_SB + b])
            k.dma(out=zc[i][:], in_=self.zT[B_SC + b])
            k.dma(out=zh[i][:], in_=self.zT[B_SH + b])
            k.op("dve", "tensor_tensor", out=u[:, 1:LP + 1], in0=zc[i][:], in1=zh[i][:], op=ALU.mult)
            k.op("dve", "tensor_scalar", out=t1[:], in0=u[:, 0:LP], scalar1=prm[:, C_CONV + b:C_CONV + b + 1], scalar2=None, op0=ALU.mult)
            k.op("dve", "scalar_tensor_tensor", out=t1[:], in0=u[:, 1:LP + 1], scalar=prm[:, C_CONV + 4 + b:C_CONV + 5 + b], in1=t1[:], op0=ALU.mult, op1=ALU.add)
            k.op("dve", "scalar_tensor_tensor", out=t1[:], in0=u[:, 2:LP + 2], scalar=prm[:, C_CONV + 8 + b:C_CONV + 9 + b], in1=t1[:], op0=ALU.mult, op1=ALU.add)
            k.op("dve", "tensor_tensor", out=yb[i][:], in0=t1[:], in1=zb[i][:], op=ALU.mult)
            k.dma(out=self.yT[1, b], in_=yb[i][:], q="pool", wd=True)
        k.barrier()
        k.pop()

    def tr_chunks(self, src, dst, nchunks, width, psbase=6):
        k = self.k
        per = 1024 // 128
        for gi, c0 in enumerate(range(0, nchunks, per)):
            nb = min(per, nchunks - c0)
            pb = self.ps[psbase + gi % 2][:].bitcast(BF16)
            for j in range(nb):
                k.op("pe", "transpose", out=pb[0:width, 128 * j:128 * j + 128],
                     in_=src[:, width * (c0 + j):width * (c0 + j + 1)], identity=self.idb[:])
            self.evac(dst[:, c0:c0 + nb, :], pb[0:width, 0:128 * nb].rearrange("p (c t) -> p c t", t=128))

    def p4(self, l):
        k, cfg = self.k, self.cfg
        LP = cfg.LP
        NC = LP // 64
        prm, lbc = self.prm[l], self.lbc
        ps = self.ps
        k.push()
        one64 = k.sb("h_one", [128, 64], F32)
        k.op("pool", "memset", ap=one64[:], constant=1.0)
        zq = k.sb("h_zq", [128, LP], F32)
        zf = k.sb("h_zf", [128, LP], F32)
        tmp1 = k.sb("h_tmp1", [128, LP], F32)
        cum = k.sb("h_cum", [128, LP], F32)
        tmpA = k.sb("h_tmpA", [128, LP], F32)
        split = min(LP, 512 * ((LP // 2) // 512))
        oL = k.sb("h_oL", [128, max(split, 64)], F32)
        oH = k.sb("h_oH", [128, LP - split], F32)

        def osl(c0, n):
            return oL[:, c0:c0 + n] if c0 < split else oH[:, c0 - split:c0 - split + n]
        Qt = [k.sb(f"h_Qt{i}", [128, LP], BF16) for i in range(2)]
        Kt = [k.sb(f"h_Kt{i}", [128, LP], BF16) for i in range(2)]
        vtok = k.sb("h_vtok", [64, NC, 128], BF16)
        Kttok = [k.sb(f"h_Kttok{i}", [64, NC, 128], BF16) for i in range(2)]
        eref = [k.sb(f"h_eref{i}", [128, NC], F32) for i in range(2)]
        dec = [k.sb(f"h_dec{i}", [128, NC], F32) for i in range(2)]
        e5 = [k.sb(f"h_e5{i}", [128, NC], F32) for i in range(2)]
        S32s = [[k.sb(f"h_S32{d}{i}", [128, 128], F32) for i in range(2)] for d in range(2)]
        tmpSs = [[k.sb(f"h_tmpS{d}{i}", [128, 128], F32) for i in range(2)] for d in range(2)]
        Sbs = [[k.sb(f"h_Sb{d}{i}", [128, 128], BF16) for i in range(2)] for d in range(2)]
        Ats = [[k.sb(f"h_At{d}{i}", [64, 64], BF16) for i in range(2)] for d in range(2)]
        cum3 = cum[:].rearrange("p (c t) -> p c t", t=64)
        tmpA3 = tmpA[:].rearrange("p (c t) -> p c t", t=64)
        orders = [list(range(1, NC)), list(range(NC - 1, 0, -1))]
        masks = [self.MUi, self.MLi]
        for hd in range(4):
            k.dma(out=zq[:], in_=self.zT[B_HQ + hd])
            k.dma(out=tmpA[:], in_=self.zT[B_HI + hd])
            vb = Qt[1]
            k.op("act", "activation", out=vb[:], in_=tmpA[:], func=AF.Copy)
            self.tr_chunks(vb, vtok, NC, 64)
            for d in range(2):
                k.dma(out=zf[:], in_=self.zT[B_HFF + 4 * d + hd])
                k.op("act", "activation", out=zf[:], in_=zf[:], func=AF.Sigmoid)
                k.op("dve", "tensor_scalar", out=zf[:], in0=zf[:], scalar1=lbc[:, d, l, hd, 0:1], scalar2=lbc[:, d, l, hd, 1:2], op0=ALU.mult, op1=ALU.add)
                k.op("act", "activation", out=tmp1[:], in_=zf[:], func=AF.Ln)
                k.op("dve", "tensor_scalar", out=zf[:], in0=zf[:], scalar1=-1.0, scalar2=1.0, op0=ALU.mult, op1=ALU.add)
                for c in range(NC):
                    cs = slice(64 * c, 64 * c + 64)
                    k.op("dve", "tensor_tensor_scan", out=cum[:, cs], data0=one64[:], data1=tmp1[:, cs], initial=0.0, op0=ALU.mult, op1=ALU.add)
                if d == 1:
                    k.op("dve", "tensor_tensor", out=tmpA3, in0=cum3[:, :, 63:64].to_broadcast([128, NC, 64]), in1=cum3, op=ALU.subtract)
                    k.op("dve", "tensor_tensor", out=cum[:], in0=tmpA[:], in1=tmp1[:], op=ALU.add)
                tot = cum3[:, :, 63] if d == 0 else cum3[:, :, 0]
                refi = 31 if d == 0 else 32
                k.op("act", "activation", out=eref[d][:], in_=cum3[:, :, refi], func=AF.Exp)
                k.op("act", "activation", out=dec[d][:], in_=tot, func=AF.Exp)
                k.op("dve", "tensor_tensor", out=e5[d][:], in0=tot, in1=cum3[:, :, refi], op=ALU.subtract)
                k.op("act", "activation", out=e5[d][:], in_=e5[d][:], func=AF.Exp)
                k.op("dve", "tensor_tensor", out=tmpA3, in0=cum3, in1=cum3[:, :, refi:refi + 1].to_broadcast([128, NC, 64]), op=ALU.subtract)
                k.op("act", "activation", out=tmp1[:], in_=tmpA[:], func=AF.Exp)
                k.op("dve", "tensor_tensor", out=Qt[d][:], in0=zq[:], in1=tmp1[:], op=ALU.mult)
                k.op("act", "activation", out=tmpA[:], in_=tmpA[:], func=AF.Exp, scale=-1.0)
                k.op("dve", "tensor_tensor", out=Kt[d][:], in0=zf[:], in1=tmpA[:], op=ALU.mult)
                self.tr_chunks(Kt[d], Kttok[d], NC, 64)
                k.op("pool", "memset", ap=S32s[d][0][:], constant=0.0)
                k.op("pool", "memset", ap=Sbs[d][0][:], constant=0.0)
            touched = set()
            for ci in range(NC - 1):
                for d in range(2):
                    order = orders[d]
                    c = order[ci]
                    cs = slice(64 * c, 64 * c + 64)
                    pA, pO, pS = ps[d], ps[2 + d], ps[4 + d]
                    at = Ats[d][ci % 2]
                    Sb, tmpS = Sbs[d][ci % 2], tmpSs[d][ci % 2]
                    S32, S32n = S32s[d][ci % 2], S32s[d][(ci + 1) % 2]
                    self.mm(pA[0:64, 0:64], Kt[d][:, cs], Qt[d][:, cs])
                    if ci + 1 < len(order):
                        cn = order[ci + 1]
                        self.mm(pS[:, 0:128], Kttok[d][:, c, :], vtok[:, c, :])
                        k.op("dve", "tensor_scalar", out=tmpS[:], in0=S32[:], scalar1=dec[d][:, c:c + 1], scalar2=None, op0=ALU.mult)
                        k.op("dve", "scalar_tensor_tensor", out=S32n[:], in0=pS[:, 0:128], scalar=e5[d][:, c:c + 1], in1=tmpS[:], op0=ALU.mult, op1=ALU.add)
                        k.op("act", "activation", out=Sbs[d][(ci + 1) % 2][:], in_=S32n[:], func=AF.Copy, scale=eref[d][:, cn:cn + 1])
                    k.op("dve", "tensor_tensor", out=at[:], in0=pA[0:64, 0:64], in1=masks[d][0:64, 0:64], op=ALU.mult)
                    self.mm(pO[:, 0:64], vtok[:, c, :], at[:], True, False)
                    self.mm(pO[:, 0:64], Sb[:], Qt[d][:, cs], False, True)
                    o_ = osl(64 * c, 64)
                    if c not in touched:
                        touched.add(c)
                        k.op("act", "activation", out=o_, in_=pO[:, 0:64], func=AF.Copy)
                    else:
                        k.op("dve", "tensor_tensor", out=o_, in0=pO[:, 0:64], in1=o_, op=ALU.add)
            k.op("pool", "memset", ap=osl(0, 64), constant=0.0)
            sqb, yb = Kt[0], Qt[0]
            parts = [(0, split, oL), (split, LP - split, oH)] if split > 0 else [(0, LP, oH)]
            for (c0, n_, t_) in parts:
                k.op("act", "activation", out=sqb[:, c0:c0 + n_], in_=t_[:, 0:n_], func=AF.Square)
            for bi, (t0, n) in enumerate(cfg.TB):
                p = ps[bi % 2]
                self.mm(p[:, 0:n], self.ones_bf[:], sqb[:, t0:t0 + n])
                k.op("act", "activation", out=tmpA[:, t0:t0 + n], in_=p[:, 0:n], func=AF.Ln, scale=1.0 / 128, bias=self.epsc[:, 0:1])
            k.op("act", "activation", out=tmpA[:], in_=tmpA[:], func=AF.Exp, scale=-0.5)
            k.dma(out=zf[:], in_=self.zT[B_HG + hd])
            k.op("act", "activation", out=zf[:], in_=zf[:], func=AF.Silu)
            for (c0, n_, t_) in parts:
                k.op("dve", "scalar_tensor_tensor", out=cum[:, c0:c0 + n_], in0=t_[:, 0:n_], scalar=prm[:, C_ONORM + hd:C_ONORM + hd + 1],
                     in1=tmpA[:, c0:c0 + n_], op0=ALU.mult, op1=ALU.mult)
            k.op("dve", "tensor_tensor", out=yb[:], in0=cum[:], in1=zf[:], op=ALU.mult)
            k.dma(out=self.yT[0, hd], in_=yb[:], q="pool", wd=True)
        k.barrier()
        k.pop()

    def p5(self, l):
        k, cfg = self.k, self.cfg
        LP, NKT = cfg.LP, cfg.NKT
        prm = self.prm[l]
        lam_init = 0.8 - 0.6 * float(np.exp(-0.3 * l))
        k.push()
        ropeC = k.sb("a_rc", [128, LP], F32)
        ropeS = k.sb("a_rs", [128, LP], F32)
        k.dma(out=ropeC[:], in_=self.c_rope[0])
        k.dma(out=ropeS[:], in_=self.c_rope[1])
        gq = k.sb("a_gq", [128, 3], F32)
        k.op("dve", "tensor_scalar", out=gq[:, 0:1], in0=prm[:, C_QN:C_QN + 1], scalar1=0.125, scalar2=None, op0=ALU.mult)
        k.op("dve", "tensor_copy", out=gq[:, 1:2], in_=prm[:, C_KN:C_KN + 1])
        k.op("dve", "tensor_scalar", out=gq[:, 2:3], in0=prm[:, C_SUBLN:C_SUBLN + 1], scalar1=1.0 - lam_init, scalar2=None, op0=ALU.mult)
        z = k.sb("a_z", [128, LP], F32)
        tmpA = k.sb("a_tmpA", [128, LP], F32)
        sqb = k.sb("a_sqb", [128, LP], BF16)
        znb = k.sb("a_znb", [128, LP], BF16)
        qh = k.sb("a_qh", [128, LP], BF16)
        kh = k.sb("a_kh", [128, LP], BF16)
        vt = k.sb("a_vt", [128, NKT, 128], BF16)
        oT = k.sb("a_oT", [128, LP], F32)
        E = [k.sb(f"a_E{i}", [128, 512], BF16) for i in range(4)]
        accE = [k.sb(f"a_acc{m}", [128, 512], F32) for m in range(2)]
        accb = [k.sb(f"a_accb{m}", [128, 512], BF16) for m in range(2)]
        r0 = k.sb("a_r0", [128, 512], F32)
        r1 = k.sb("a_r1", [128, 512], F32)
        t1 = k.sb("a_t1", [128, 512], F32)
        for hd in range(4):
            for blk, gcol, dst in ((B_DQ + hd, 0, qh), (B_DK + hd, 1, kh)):
                k.dma(out=z[:], in_=self.zT[blk])
                k.op("act", "activation", out=sqb[:], in_=z[:], func=AF.Square)
                for bi, (t0, n) in enumerate(cfg.TB):
                    p = self.ps[bi % 2]
                    self.mm(p[:, 0:n], self.bones_bf[:], sqb[:, t0:t0 + n])
                    k.op("act", "activation", out=tmpA[:, t0:t0 + n], in_=p[:, 0:n], func=AF.Ln, scale=1.0 / 64, bias=self.epsc[:, 0:1])
                k.op("act", "activation", out=tmpA[:], in_=tmpA[:], func=AF.Exp, scale=-0.5)
                k.op("dve", "scalar_tensor_tensor", out=z[:], in0=z[:], scalar=gq[:, gcol:gcol + 1], in1=tmpA[:], op0=ALU.mult, op1=ALU.mult)
                k.op("act", "activation", out=znb[:], in_=z[:], func=AF.Copy)
                for bi, (t0, n) in enumerate(cfg.TB):
                    p = self.ps[2 + bi % 2]
                    self.mm(p[:, 0:n], self.rotT_bf[:], znb[:, t0:t0 + n])
                    k.op("dve", "tensor_tensor", out=tmpA[:, t0:t0 + n], in0=p[:, 0:n], in1=ropeS[:, t0:t0 + n], op=ALU.mult)
                k.op("dve", "tensor_tensor", out=z[:], in0=z[:], in1=ropeC[:], op=ALU.mult)
                k.op("dve", "tensor_tensor", out=dst[:], in0=z[:], in1=tmpA[:], op=ALU.add)
            k.dma(out=vt[:], in_=self.vtok[:, hd * 128:(hd + 1) * 128].rearrange("(kt p) v -> p kt v", p=128))
            for qi, (q0, n) in enumerate(cfg.TB):
                def scores(kt):
                    c = PAD + 128 * kt
                    kn = min(128, LP - c)
                    es = []
                    for m in range(2):
                        pS = self.ps[2 * (kt % 2) + m]
                        self.mm(pS[0:kn, 0:n], kh[64 * m:64 * m + 64, c:c + kn], qh[64 * m:64 * m + 64, q0:q0 + n])
                        e_ = E[2 * (kt % 2) + m]
                        k.op("act", "activation", out=e_[0:kn, 0:n], in_=pS[0:kn, 0:n], func=AF.Exp)
                        es.append(e_)
                    return kn, es
                nxt = scores(0)
                for kt in range(NKT):
                    kn, es = nxt
                    if kt + 1 < NKT:
                        nxt = scores(kt + 1)
                    for m in range(2):
                        self.mm(self.ps[4 + m][:, 0:n], vt[0:kn, kt, :], es[m][0:kn, 0:n], kt == 0, kt == NKT - 1)
                        if m == 0:
                            if kt == 0:
                                k.op("dve", "tensor_copy", out=accE[m][0:kn, 0:n], in_=es[m][0:kn, 0:n])
                            else:
                                k.op("dve", "tensor_tensor", out=accE[m][0:kn, 0:n], in0=accE[m][0:kn, 0:n], in1=es[m][0:kn, 0:n], op=ALU.add)
                        else:
                            self.mm(self.ps[7][:, 0:n], self.ones_bf[0:kn, :], es[m][0:kn, 0:n], kt == 0, kt == NKT - 1)
                k.op("act", "activation", out=accb[0][:, 0:n], in_=accE[0][:, 0:n], func=AF.Copy)
                self.mm(self.ps[6][:, 0:n], self.ones_bf[:], accb[0][:, 0:n])
                k.op("dve", "reciprocal", out=r0[:, 0:n], in_=self.ps[6][:, 0:n])
                k.op("dve", "reciprocal", out=r1[:, 0:n], in_=self.ps[7][:, 0:n])
                k.op("dve", "tensor_tensor", out=oT[:, q0:q0 + n], in0=self.ps[4][:, 0:n], in1=r0[:, 0:n], op=ALU.mult)
                k.op("dve", "tensor_tensor", out=t1[:, 0:n], in0=self.ps[5][:, 0:n], in1=r1[:, 0:n], op=ALU.mult)
                k.op("dve", "scalar_tensor_tensor", out=oT[:, q0:q0 + n], in0=t1[:, 0:n], scalar=self.nlam[:, l:l + 1], in1=oT[:, q0:q0 + n], op0=ALU.mult, op1=ALU.add)
            k.op("act", "activation", out=sqb[:], in_=oT[:], func=AF.Square)
            for bi, (t0, n) in enumerate(cfg.TB):
                p = self.ps[bi % 2]
                self.mm(p[:, 0:n], self.ones_bf[:], sqb[:, t0:t0 + n])
                k.op("act", "activation", out=tmpA[:, t0:t0 + n], in_=p[:, 0:n], func=AF.Ln, scale=1.0 / 128, bias=self.epsc[:, 1:2])
            k.op("act", "activation", out=tmpA[:], in_=tmpA[:], func=AF.Exp, scale=-0.5)
            k.op("dve", "scalar_tensor_tensor", out=znb[:], in0=oT[:], scalar=gq[:, 2:3], in1=tmpA[:], op0=ALU.mult, op1=ALU.mult)
            k.dma(out=self.yT[2, hd], in_=znb[:], q="pool", wd=True)
        k.barrier()
        k.pop()

    def p6a(self, l):
        k, cfg = self.k, self.cfg
        LP = cfg.LP
        prm = self.prm[l]
        k.push()
        zb = [k.sb(f"r_zb{i}", [128, LP + 2], F32) for i in range(2)]
        s_ = k.sb("r_s", [128, LP], F32)
        xm = [k.sb(f"r_xm{i}", [128, LP], BF16) for i in range(2)]
        hm = k.sb("r_hm", [128, 15, 2], F32)
        for t in zb:
            k.op("pool", "memset", ap=t[:], constant=0.0)
        k.op("dve", "tensor_scalar", out=hm[:, :, 0], in0=prm[:, C_MU:C_MU + 15], scalar1=-1.0, scalar2=1.0, op0=ALU.mult, op1=ALU.add)
        k.op("dve", "tensor_scalar", out=hm[:, :, 1], in0=prm[:, C_MU:C_MU + 15], scalar1=0.5, scalar2=None, op0=ALU.mult)
        for b in range(15):
            i = b % 2
            k.dma(out=zb[i][:, 1:LP + 1], in_=self.zT[B_RW + b])
            k.op("dve", "tensor_tensor", out=s_[:], in0=zb[i][:, 0:LP], in1=zb[i][:, 2:LP + 2], op=ALU.add)
            k.op("dve", "tensor_scalar", out=s_[:], in0=s_[:], scalar1=hm[:, b, 1:2], scalar2=None, op0=ALU.mult)
            k.op("dve", "scalar_tensor_tensor", out=xm[i][:], in0=zb[i][:, 1:LP + 1], scalar=hm[:, b, 0:1], in1=s_[:], op0=ALU.mult, op1=ALU.add)
            k.op("pool", "memset", ap=xm[i][:, 0:PAD], constant=0.0)
            k.dma(out=self.xmT[b], in_=xm[i][:], q="pool", wd=True)
        k.barrier()
        k.pop()

    def p6b(self, l):
        k, cfg = self.k, self.cfg
        LP, NT = cfg.LP, cfg.NT
        prm = self.prm[l]
        ps = self.ps
        k.push()
        Of = k.sb("w_Of", [128, NT, 512], BF16)
        Bn = k.sb("w_Bn", [128, 4, LP], BF16)
        w2b = k.sb("w_w2b", [128, 512], BF16)
        a2b = k.sb("w_a2b", [128, 512], BF16)
        g2b = k.sb("w_g2b", [128, 512], BF16)
        lng = k.sb("w_lng", [128, 512], F32)
        lnb = k.sb("w_lnb", [128, 512], F32)
        k.dma(out=w2b[:], in_=self.wb_rw2[l])
        k.dma(out=a2b[:], in_=self.wb_ra2[l])
        k.dma(out=g2b[:], in_=self.wb_rg2[l])
        k.dma(out=lng[:], in_=self.W["rwkv_lnx_g"][l].partition_broadcast(128))
        k.dma(out=lnb[:], in_=self.W["rwkv_lnx_b"][l].partition_broadcast(128))
        LM = k.sb("w_LM", [128, 14, 128], BF16)
        k.dma(out=LM[:], in_=self.c_lvl[:].rearrange("m p j -> p m j"), q="pool")
        rmask = k.sb("w_rmask", [128, 4, 128], F32)
        k.op("pool", "memset", ap=rmask[:], constant=1.0)
        k.op("pool", "memset", ap=rmask[:, :, 0:1], constant=0.0)
        ST32 = k.sb("w_ST32", [128, 4, 64], F32)
        STb = k.sb("w_STb", [128, 4, 64], BF16)
        f32t = lambda n: k.sb("w_" + n, [128, 4, 128], F32)
        bf16t = lambda n: k.sb("w_" + n, [128, 4, 128], BF16)
        kkr, rn, kk, sgw, a_, G, Gex, E1, E2, kd, akk = [
            f32t(n) for n in ("kkr", "rn", "kk", "sgw", "a", "G", "Gex", "E1", "E2", "kd", "akk")]
        tmp = Gex
        Bg, Kg, rkb = [bf16t(n) for n in ("Bg", "Kg", "rkb")]
        sqk = rkb
        thb = k.sb("w_thb", [128, 128], BF16)
        sgl = k.sb("w_sgl", [128, 128], BF16)
        PB = []
        for i_ in range(2):
            PB.append(dict(
                X=k.sb(f"w_X{i_}", [128, 15, 128], BF16),
                AR=k.sb(f"w_AR{i_}", [128, 4, 2, 128], BF16),
                Bt=bf16t(f"Bt{i_}"), Ktl=bf16t(f"Ktl{i_}"),
                Bgtok=k.sb(f"w_Bgtok{i_}", [128, 512], BF16), Kgtok=k.sb(f"w_Kgtok{i_}", [128, 512], BF16),
                Vtok=k.sb(f"w_Vtok{i_}", [128, 512], BF16), gamC=k.sb(f"w_gamC{i_}", [128, 4], F32),
                bont=f32t(f"bont{i_}"), Gt=f32t(f"Gt{i_}")))
        Ot, sq2, otmp = f32t("Ot"), f32t("sq2"), f32t("otmp")
        Xnb, yst = bf16t("Xnb"), bf16t("yst")
        st8 = k.sb("w_st8", [128, 4, 8], F32)
        MSS = [k.sb(f"w_MSS{i}", [128, 2, 128], BF16) for i in range(2)]
        for i_, (ma, mb) in enumerate(((self.MU, self.MUi), (self.ML, self.MLi))):
            k.op("dve", "tensor_copy", out=MSS[i_][:, 0, :], in_=ma[:])
            k.op("dve", "tensor_copy", out=MSS[i_][:, 1, :], in_=mb[:])
        hs = []
        for hd_ in range(8):
            d_ = {}
            d_["NmT"] = k.sb(f"w_NmT{hd_}", [128, 128], BF16)
            for n in ("NN", "MM", "TT2", "ZZ"):
                d_[n] = k.sb(f"w_{n}{hd_}", [128, 2, 128], BF16)
            d_["NkA"] = k.sb(f"w_NkA{hd_}", [128, 6, 128], BF16)
            d_["NkTA"] = k.sb(f"w_NkTA{hd_}", [128, 7, 128], BF16)
            d_["WTb"] = k.sb(f"w_WTb{hd_}", [128, 64], BF16)
            d_["UTb"] = k.sb(f"w_UTb{hd_}", [128, 64], BF16)
            hs.append(d_)
        SB_ = ((0, 2, 6), (1, 3, 7))
        sctr = [0, 0]
        nbk = [3]

        def slot(par_):
            i_ = sctr[par_]
            sctr[par_] += 1
            j_ = (i_ // nbk[0]) % 4
            return ps[SB_[par_][i_ % nbk[0]]][:, 128 * j_:128 * j_ + 128]

        def dslot(par_):
            i_ = sctr[par_]
            sctr[par_] += 1
            j_ = (i_ // nbk[0]) % 2
            return ps[SB_[par_][i_ % nbk[0]]][:, 256 * j_:256 * j_ + 256]

        fl = lambda t: t[:].rearrange("p b t -> p (b t)")
        v3 = lambda ap_: ap_.rearrange("p (a t) -> p a t", a=2)
        kkp = prm[:, C_KK:C_KK + 4].unsqueeze(2).to_broadcast([128, 4, 128])
        kap = prm[:, C_KA:C_KA + 4].unsqueeze(2).to_broadcast([128, 4, 128])
        rkp = prm[:, C_RK:C_RK + 4].unsqueeze(2).to_broadcast([128, 4, 128])
        G3 = G[:]

        def prep(c, d, P):
            cs = slice(128 * c, 128 * c + 128)
            Xc, AR = P["X"], P["AR"]
            At_, Rt = AR[:, :, 0, :], AR[:, :, 1, :]
            k.dma(out=Xc[:], in_=self.xmT[:, :, cs].rearrange("b p t -> p b t"))
            r_, kx, v_ = Xc[:, 0:4, :], Xc[:, 4:8, :], Xc[:, 8:12, :]
            rs = slice(64 * d, 64 * d + 64)
            pw = ps[4 + d]
            k.op("dve", "tensor_tensor", out=kkr[:], in0=kx, in1=kkp, op=ALU.mult)
            k.op("act", "activation", out=sqk[:], in_=kkr[:], func=AF.Square)
            k.op("act", "activation", out=thb[rs, :], in_=Xc[rs, 12, :], func=AF.Tanh)
            yield
            self.mm(ps[4][:, :], self.bones_bf[:], fl(sqk))
            k.op("dve", "tensor_scalar", out=fl(rn), in0=ps[4][:, :], scalar1=1e-24, scalar2=None, op0=ALU.max)
            k.op("act", "activation", out=rn[:], in_=rn[:], func=AF.Ln)
            k.op("act", "activation", out=rn[:], in_=rn[:], func=AF.Exp, scale=-0.5)
            k.op("dve", "tensor_tensor", out=kk[:], in0=kkr[:], in1=rn[:], op=ALU.mult)
            yield
            for b in range(4):
                self.mm(pw[:, 128 * b:128 * b + 128], w2b[rs, 128 * b:128 * b + 128], thb[rs, :])
            for b in range(4):
                k.op("act", "activation", out=sgw[:, b, :], in_=pw[:, 128 * b:128 * b + 128], func=AF.Sigmoid,
                     bias=prm[:, C_W0 + 4 * d + b:C_W0 + 4 * d + b + 1])
            yield
            for b in range(4):
                self.mm(pw[:, 128 * b:128 * b + 128], a2b[rs, 128 * b:128 * b + 128], Xc[rs, 13, :])
            for b in range(4):
                k.op("act", "activation", out=a_[:, b, :], in_=pw[:, 128 * b:128 * b + 128], func=AF.Sigmoid,
                     bias=prm[:, C_A0 + 4 * d + b:C_A0 + 4 * d + b + 1])
            yield
            k.op("dve", "tensor_tensor_scan", out=fl(G), data0=fl(rmask), data1=fl(sgw), initial=0.0, op0=ALU.mult, op1=ALU.add)
            if d == 1:
                k.op("dve", "tensor_tensor", out=Gex[:], in0=G3[:, :, 127:128].to_broadcast([128, 4, 128]), in1=G[:], op=ALU.subtract)
                k.op("dve", "tensor_tensor", out=G[:], in0=Gex[:], in1=sgw[:], op=ALU.add)
            tot = G3[:, :, 127] if d == 0 else G3[:, :, 0]
            totb = (G3[:, :, 127:128] if d == 0 else G3[:, :, 0:1]).to_broadcast([128, 4, 128])
            yield
            k.op("act", "activation", out=E1[:], in_=G[:], func=AF.Exp, scale=-KAPPA)
            k.op("dve", "tensor_tensor", out=Rt, in0=r_, in1=E1[:], op=ALU.mult)
            k.op("dve", "tensor_tensor", out=Gex[:], in0=G[:], in1=sgw[:], op=ALU.subtract)
            k.op("act", "activation", out=E2[:], in_=Gex[:], func=AF.Exp, scale=-KAPPA)
            yield
            k.op("dve", "scalar_tensor_tensor", out=At_, in0=kk[:], scalar=-1.0, in1=E2[:], op0=ALU.mult, op1=ALU.mult)
            k.op("act", "activation", out=E1[:], in_=G[:], func=AF.Exp, scale=KAPPA)
            k.op("dve", "tensor_tensor", out=tmp[:], in0=totb, in1=G[:], op=ALU.subtract)
            k.op("act", "activation", out=E2[:], in_=tmp[:], func=AF.Exp, scale=-KAPPA)
            k.op("act", "activation", out=P["gamC"][:], in_=tot, func=AF.Exp, scale=-KAPPA)
            yield
            k.op("dve", "scalar_tensor_tensor", out=tmp[:], in0=a_[:], scalar=-1.0, in1=kap, op0=ALU.add, op1=ALU.mult)
            k.op("dve", "scalar_tensor_tensor", out=kd[:], in0=tmp[:], scalar=1.0, in1=kx, op0=ALU.add, op1=ALU.mult)
            k.op("dve", "tensor_tensor", out=akk[:], in0=a_[:], in1=kk[:], op=ALU.mult)
            yield
            k.op("dve", "tensor_tensor", out=P["Bt"][:], in0=akk[:], in1=E1[:], op=ALU.mult)
            k.op("dve", "tensor_tensor", out=P["Ktl"][:], in0=kd[:], in1=E1[:], op=ALU.mult)
            yield
            k.op("dve", "tensor_tensor", out=Bg[:], in0=akk[:], in1=E2[:], op=ALU.mult)
            k.op("dve", "tensor_tensor", out=Kg[:], in0=kd[:], in1=E2[:], op=ALU.mult)
            yield
            for src, dst_, pbk in ((Bg, P["Bgtok"], ps[4]), (Kg, P["Kgtok"], ps[5]), (None, P["Vtok"], ps[4])):
                pb = pbk[:].bitcast(BF16)
                for b in range(4):
                    in_ = Xc[:, 8 + b, :] if src is None else src[:, b, :]
                    k.op("pe", "transpose", out=pb[:, 128 * b:128 * b + 128], in_=in_, identity=self.idb[:])
                k.op("act", "activation", out=dst_[:], in_=pb[:, 0:512], func=AF.Copy)
                yield
            k.op("dve", "tensor_tensor", out=tmp[:], in0=r_, in1=kd[:], op=ALU.mult)
            k.op("dve", "tensor_tensor", out=rkb[:], in0=tmp[:], in1=rkp, op=ALU.mult)
            self.mm(ps[5][:, :], self.bones_bf[:], fl(rkb))
            yield
            if d == 0:
                k.op("dve", "tensor_tensor", out=Bn[:, :, cs], in0=ps[5][:, :].rearrange("p (b t) -> p b t", t=128), in1=v_, op=ALU.mult)
            else:
                k.op("dve", "tensor_tensor", out=P["bont"][:], in0=ps[5][:, :].rearrange("p (b t) -> p b t", t=128), in1=v_, op=ALU.mult)
                k.op("dve", "tensor_tensor", out=P["bont"][:], in0=P["bont"][:], in1=Bn[:, :, cs], op=ALU.add)
                yield
                k.op("act", "activation", out=sgl[:], in_=Xc[:, 14, :], func=AF.Sigmoid)
                for b in range(4):
                    self.mm(ps[4][:, 128 * b:128 * b + 128], g2b[:, 128 * b:128 * b + 128], sgl[:])
                k.op("act", "activation", out=fl(P["Gt"]), in_=ps[4][:, :], func=AF.Copy)
            yield

        def stages(c, d, P, pump):
            cs = slice(128 * c, 128 * c + 128)
            AR, Bt, Ktl, Vtok, Bgtok, Kgtok, gamC = P["AR"], P["Bt"], P["Ktl"], P["Vtok"], P["Bgtok"], P["Kgtok"], P["gamC"]
            mST = self.ML if d == 0 else self.MU
            mo, mto = (0, 7) if d == 0 else (7, 0)
            hv = []
            for hd in range(8):
                blk, par = hd // 2, hd % 2
                hr = slice(64 * par, 64 * par + 64)
                hv.append(dict(blk=blk, par=par, hr=hr, H=hs[hd],
                               Ah=AR[hr, blk, 0, :], Bh=Bt[hr, blk, :], Kh=Ktl[hr, blk, :], Rh=AR[hr, blk, 1, :],
                               ARh=AR[hr, blk, :, :].rearrange("p a t -> p (a t)"),
                               Vh=Vtok[:, 128 * blk + 64 * par:128 * blk + 64 * par + 64], STh=STb[hr, blk, :]))
            nbk[0] = 3
            for x in hv:
                H, par = x["H"], x["par"]
                p_ = dslot(par)
                self.mm(p_, x["Bh"], x["ARh"])
                k.op("dve", "tensor_tensor", out=H["NN"][:], in0=v3(p_), in1=MSS[d][:], op=ALU.mult)
                p_ = dslot(par)
                self.mm(p_, x["Kh"], x["ARh"])
                k.op("dve", "tensor_tensor", out=H["MM"][:], in0=v3(p_), in1=MSS[d][:], op=ALU.mult)
                p_ = slot(par)
                self.mm(p_, x["Ah"], x["Bh"])
                k.op("dve", "tensor_tensor", out=H["NmT"][:], in0=p_, in1=mST[:], op=ALU.mult)
            pump()
            for x in hv:
                H = x["H"]
                k.op("pool", "tensor_tensor", out=H["NkA"][:], in0=H["NN"][:, 0:1, :].to_broadcast([128, 6, 128]),
                     in1=LM[:, mo:mo + 6, :], op=ALU.mult)
                k.op("pool", "tensor_tensor", out=H["NkTA"][:], in0=H["NmT"][:].unsqueeze(1).to_broadcast([128, 7, 128]),
                     in1=LM[:, mto:mto + 7, :], op=ALU.mult)
                k.op("pool", "tensor_tensor", out=H["TT2"][:, 0, :], in0=H["NkA"][:, 0, :], in1=self.idb[:], op=ALU.add)
                k.op("pool", "tensor_tensor", out=H["TT2"][:, 1, :], in0=H["NkTA"][:, 0, :], in1=self.idb[:], op=ALU.add)
            pump()
            for lv in range(1, 7):
                lastlv = lv == 6
                for x in hv:
                    H, par = x["H"], x["par"]
                    pz = dslot(par)
                    self.mm(pz[:, 0:128], H["NkTA"][:, lv, :], H["TT2"][:, 0, :])
                    if not lastlv:
                        self.mm(pz[:, 128:256], H["NkA"][:, lv, :], H["TT2"][:, 1, :])
                        k.op("act", "activation", out=H["ZZ"][:], in_=v3(pz), func=AF.Copy)
                    else:
                        k.op("act", "activation", out=H["ZZ"][:, 0, :], in_=pz[:, 0:128], func=AF.Copy)
                pump()
                for x in hv:
                    H, par = x["H"], x["par"]
                    pt = dslot(par)
                    self.mm(pt[:, 0:128], H["TT2"][:, 1, :], H["ZZ"][:, 0, :])
                    if not lastlv:
                        self.mm(pt[:, 128:256], H["TT2"][:, 0, :], H["ZZ"][:, 1, :])
                        k.op("dve", "tensor_tensor", out=H["TT2"][:], in0=v3(pt), in1=H["TT2"][:], op=ALU.add)
                    else:
                        k.op("dve", "tensor_tensor", out=H["TT2"][:, 0, :], in0=pt[:, 0:128], in1=H["TT2"][:, 0, :], op=ALU.add)
                pump()
            nbk[0] = 2
            for x in hv:
                H, par = x["H"], x["par"]
                p_ = slot(par)
                self.mm(p_[:, 0:64], x["Ah"], x["STh"], True, False)
                self.mm(p_[:, 0:64], H["MM"][:, 0, :], x["Vh"], False, True)
                k.op("act", "activation", out=H["WTb"][:], in_=p_[:, 0:64], func=AF.Copy)
            pump()
            for x in hv:
                H, par = x["H"], x["par"]
                p_ = slot(par)
                self.mm(p_[:, 0:64], H["TT2"][:, 0, :], H["WTb"][:])
                k.op("act", "activation", out=H["UTb"][:], in_=p_[:, 0:64], func=AF.Copy)
            pump()
            for x in hv:
                H, par, blk, hr = x["H"], x["par"], x["blk"], x["hr"]
                po = ps[6 + par][:, 64 * blk:64 * blk + 64]
                self.mm(po, x["Rh"], x["STh"], True, False)
                self.mm(po, H["NN"][:, 1, :], H["UTb"][:], False, False)
                self.mm(po, H["MM"][:, 1, :], x["Vh"], False, True)
                p_ = slot(par)
                self.mm(p_[:, 0:64], Bgtok[:, 128 * blk:128 * blk + 128], H["UTb"][:], True, False)
                self.mm(p_[:, 0:64], Kgtok[:, 128 * blk:128 * blk + 128], x["Vh"], False, True)
                k.op("dve", "scalar_tensor_tensor", out=ST32[hr, blk, :], in0=ST32[hr, blk, :], scalar=gamC[hr, blk:blk + 1],
                     in1=p_[hr, 0:64], op0=ALU.mult, op1=ALU.add)
                k.op("act", "activation", out=STb[hr, blk, :], in_=ST32[hr, blk, :], func=AF.Copy)
            pump()
            Of4 = Of[:, c, :].rearrange("p (b q v) -> p b q v", q=2, v=64)
            if d == 0:
                for par in range(2):
                    self.evac(Of4[:, :, par, :], ps[6 + par][:, 0:256].rearrange("p (b v) -> p b v", v=64))
            else:
                Ot4 = fl(Ot).rearrange("p (b q v) -> p b q v", q=2, v=64)
                for par in range(2):
                    k.op("dve", "tensor_tensor", out=Ot4[:, :, par, :], in0=ps[6 + par][:, 0:256].rearrange("p (b v) -> p b v", v=64),
                         in1=Of4[:, :, par, :], op=ALU.add)
                O8 = fl(Ot).rearrange("p (h v) -> p h v", v=64)
                S8 = fl(sq2).rearrange("p (h v) -> p h v", v=64)
                k.op("dve", "tensor_reduce", out=st8[:, 0, :], in_=O8, axis=AX.X, op=ALU.add)
                k.op("act", "activation", out=fl(sq2), in_=fl(Ot), func=AF.Square)
                k.op("dve", "tensor_reduce", out=st8[:, 1, :], in_=S8, axis=AX.X, op=ALU.add)
                k.op("dve", "tensor_scalar", out=st8[:, 2, :], in0=st8[:, 0, :], scalar1=1.0 / 64, scalar2=None, op0=ALU.mult)
                k.op("dve", "tensor_tensor", out=st8[:, 3, :], in0=st8[:, 2, :], in1=st8[:, 2, :], op=ALU.mult)
                k.op("dve", "scalar_tensor_tensor", out=st8[:, 3, :], in0=st8[:, 1, :], scalar=1.0 / 64, in1=st8[:, 3, :], op0=ALU.mult, op1=ALU.subtract)
                k.op("act", "activation", out=st8[:, 3, :], in_=st8[:, 3, :], func=AF.Sqrt, bias=RW_LNX_EPS)
                k.op("dve", "reciprocal", out=st8[:, 3, :], in_=st8[:, 3, :])
                pump()
                k.op("dve", "tensor_tensor", out=O8, in0=O8, in1=st8[:, 2, :].unsqueeze(2).to_broadcast([128, 8, 64]), op=ALU.subtract)
                k.op("dve", "tensor_tensor", out=O8, in0=O8, in1=st8[:, 3, :].unsqueeze(2).to_broadcast([128, 8, 64]), op=ALU.mult)
                k.op("dve", "tensor_tensor", out=fl(Ot), in0=fl(Ot), in1=lng[:], op=ALU.mult)
                k.op("dve", "tensor_tensor", out=fl(Xnb), in0=fl(Ot), in1=lnb[:], op=ALU.add)
                pb = ps[7][:].bitcast(BF16)
                for b in range(4):
                    k.op("pe", "transpose", out=pb[:, 128 * b:128 * b + 128], in_=Xnb[:, b, :], identity=self.idb[:])
                k.op("dve", "tensor_tensor", out=fl(otmp), in0=pb[:, 0:512], in1=fl(P["bont"]), op=ALU.add)
                k.op("dve", "tensor_tensor", out=yst[:], in0=otmp[:], in1=P["Gt"][:], op=ALU.mult)
                k.dma(out=self.yT[3][:, :, cs].rearrange("b p t -> p b t"), in_=yst[:], q="pool", wd=True)

        def drain(g):
            if g is not None:
                for _ in g:
                    pass

        for d in range(2):
            k.op("pool", "memset", ap=ST32[:], constant=0.0)
            k.op("pool", "memset", ap=STb[:], constant=0.0)
            order = list(range(NT)) if d == 0 else list(range(NT - 1, -1, -1))
            drain(prep(order[0], d, PB[0]))
            for ci, c in enumerate(order):
                nxt = prep(order[ci + 1], d, PB[(ci + 1) % 2]) if ci + 1 < len(order) else None

                def pump(nxt=nxt):
                    if nxt is not None:
                        next(nxt, None)
                stages(c, d, PB[ci % 2], pump)
                drain(nxt)
        k.barrier()
        k.pop()

    def p7(self, l):
        k, cfg = self.k, self.cfg
        ps = self.ps
        k.push()
        bp = k.sb("m_bp", [128, 4, 4, D], BF16)
        wo = k.sb("m_wo", [128, 8, D], BF16)
        for nb in range(4):
            k.dma(out=bp[:, nb], in_=self.wb_bp[l, nb].rearrange("(kc p) o -> p kc o", p=128))
        k.dma(out=wo[:], in_=self.wb_out[l].rearrange("(kc p) o -> p kc o", p=128))
        ysb = [k.sb(f"m_ysb{i}", [128, 16, 512], BF16) for i in range(2)]
        gsb = [k.sb(f"m_gsb{i}", [128, 32, 512], BF16) for i in range(2)]
        mT = k.sb("m_mT", [128, 8, 512], BF16)
        acc = k.sb("m_acc", [128, 512], F32)
        tmp = [k.sb(f"m_tmp{i}", [128, 512], F32) for i in range(2)]
        xt = [k.sb(f"m_xt{i}", [128, D], F32) for i in range(2)]
        xo = [k.sb(f"m_xo{i}", [128, D], F32) for i in range(2)]
        ti = 0
        for bi, (t0, n) in enumerate(cfg.TB):
            i = bi % 2
            k.dma(out=ysb[i][:, :, 0:n], in_=self.yT[:, :, :, t0:t0 + n].rearrange("a b p t -> p (a b) t"))
            k.dma(out=gsb[i][:, :, 0:n], in_=self.gT[:, :, t0:t0 + n].rearrange("j p t -> p j t"))
            for j in range(8):
                for nb in range(4):
                    p = ps[(j * 4 + nb) % 4]
                    for kc in range(4):
                        self.mm(p[:, 0:n], bp[:, nb, kc, 128 * j:128 * j + 128], ysb[i][:, nb * 4 + kc, 0:n], kc == 0, kc == 3)
                    g_ = gsb[i][:, nb * 8 + j, 0:n]
                    if nb == 0:
                        k.op("dve", "tensor_tensor", out=acc[:, 0:n], in0=p[:, 0:n], in1=g_, op=ALU.mult)
                    else:
                        t_ = tmp[nb % 2]
                        k.op("dve", "tensor_tensor", out=t_[:, 0:n], in0=p[:, 0:n], in1=g_, op=ALU.mult)
                        if nb < 3:
                            k.op("pool", "tensor_tensor", out=acc[:, 0:n], in0=acc[:, 0:n], in1=t_[:, 0:n], op=ALU.add)
                        else:
                            k.op("pool", "tensor_tensor", out=mT[:, j, 0:n], in0=acc[:, 0:n], in1=t_[:, 0:n], op=ALU.add)
            for tt in range(n // 128):
                row0 = t0 + 128 * tt
                x_, o_ = xt[ti % 2], xo[ti % 2]
                ti += 1
                k.dma(out=x_[:], in_=self.xs[row0:row0 + 128, :])
                for half in range(2):
                    p = ps[4 + (tt * 2 + half) % 4]
                    hs_ = slice(512 * half, 512 * half + 512)
                    for j in range(8):
                        self.mm(p[:, :], mT[:, j, 128 * tt:128 * tt + 128], wo[:, j, hs_], j == 0, j == 7)
                    k.op("dve", "tensor_tensor", out=o_[:, hs_], in0=p[:, :], in1=x_[:, hs_], op=ALU.add)
                k.dma(out=self.xs2[row0:row0 + 128, :], in_=o_[:], q="pool", wd=True)
        k.barrier()
        k.pop()

    def p8(self, l, s, last):
        k, cfg = self.k, self.cfg
        ps = self.ps
        LP = cfg.LP
        k.push()
        w1 = k.sb("f_w1", [128, 8, DFF], BF16)
        w2 = k.sb("f_w2", [128, 32, D], BF16)
        w1v = self.wb_w1[l].rearrange("(kc p) f -> p kc f", p=128)
        w2v = self.wb_w2[l].rearrange("(kc p) o -> p kc o", p=128)
        for q_ in range(4):
            k.dma(out=w1[:, :, 1024 * q_:1024 * (q_ + 1)], in_=w1v[:, :, 1024 * q_:1024 * (q_ + 1)], wd=True)
            k.dma(out=w2[:, 8 * q_:8 * (q_ + 1), :], in_=w2v[:, 8 * q_:8 * (q_ + 1), :], wd=True)
        gb = k.sb("f_gb", [128, D], F32)
        k.dma(out=gb[:], in_=self.W["norm_mlp_g"][l].partition_broadcast(128))
        xt = [k.sb(f"f_xt{i}", [128, D], F32) for i in range(2)]
        hb = k.sb("f_hb", [128, D], BF16)
        sq = k.sb("f_sq", [128, D], BF16)
        st = k.sb("f_st", [128, 2], F32)
        h2T = k.sb("f_h2T", [128, 8, 256], BF16)
        uT = k.sb("f_uT", [128, 32, 256], BF16)
        rl = [k.sb(f"f_rl{i}", [128, 256], F32) for i in range(2)]
        xo = [k.sb(f"f_xo{i}", [128, D], F32) for i in range(2)]
        oi = 0
        for t0 in range(0, LP, 256):
            n = min(256, LP - t0)
            nt = n // 128
            for tt in range(nt):
                k.dma(out=xt[tt][:], in_=self.xs2[t0 + 128 * tt:t0 + 128 * tt + 128, :])
                self.norm_T(xt[tt][:], gb, hb, st, sq, h2T[:, :, 128 * tt:128 * tt + 128], ps[tt % 2])
            for jj in range(32):
                p = ps[2 + jj % 4]
                for kc in range(8):
                    self.mm(p[:, 0:n], w1[:, kc, 128 * jj:128 * jj + 128], h2T[:, kc, 0:n], kc == 0, kc == 7)
                r_ = rl[jj % 2]
                k.op("act", "activation", out=r_[:, 0:n], in_=p[:, 0:n], func=AF.Relu)
                k.op("pool", "tensor_tensor", out=uT[:, jj, 0:n], in0=r_[:, 0:n], in1=r_[:, 0:n], op=ALU.mult)
            for tt in range(nt):
                row0 = t0 + 128 * tt
                o_ = xo[oi % 2]
                oi += 1
                for half in range(2):
                    p = ps[6 + half]
                    hs_ = slice(512 * half, 512 * half + 512)
                    for jj in range(32):
                        self.mm(p[:, :], uT[:, jj, 128 * tt:128 * tt + 128], w2[:, jj, hs_], jj == 0, jj == 31)
                    k.op("dve", "tensor_tensor", out=o_[:, hs_], in0=p[:, :], in1=xt[tt][:, hs_], op=ALU.add)
                if not last:
                    k.dma(out=self.xs[row0:row0 + 128, :], in_=o_[:], q="pool", wd=True)
                elif row0 >= 128:
                    k.dma(out=self.yout[s][row0 - 128:row0, :], in_=o_[:], q="pool", wd=True)
        k.barrier()
        k.pop()

    def build(self):
        k, cfg = self.k, self.cfg
        stop = self.stop
        self.setup()
        done = False
        for s in range(cfg.NSEQ):
            self.p0(s)
            for l in range(cfg.DEPTH):
                self.p1(l)
                if stop == "p1":
                    k.pop(); done = True; break
                self.pa(l)
                self.p2(l)
                k.pop()
                if stop == "p2":
                    done = True; break
                self.p3(l)
                if stop == "p3":
                    done = True; break
                self.p4(l)
                if stop == "p4":
                    done = True; break
                self.p5(l)
                if stop == "p5":
                    done = True; break
                self.p6a(l)
                self.p6b(l)
                if stop == "p6x":
                    self.p6b(l)
                    done = True; break
                if stop == "p6":
                    done = True; break
                self.p7(l)
                if stop == "p7":
                    done = True; break
                self.p8(l, s, l == cfg.DEPTH - 1)
                if stop == "p8":
                    done = True; break
            if done:
                break
        k.finish()
        print(f"[build] inst={k.n_inst} waits={k.n_wait} sems={k.n_sems} cnt={k.cnt}")
        return self.nc


_CACHE = {}


def kernel(**inputs):
    x_prompt = np.ascontiguousarray(inputs["x_prompt"], dtype=np.float32)
    x_sample = np.ascontiguousarray(inputs["x_sample"], dtype=np.float32)
    B, S, _ = x_prompt.shape
    DB = x_sample.shape[0]
    DEPTH = inputs["w_in"].shape[0]
    n_cores = 8
    assert B == n_cores and DB <= n_cores and x_sample.shape[1] == S
    cfg = Cfg(S, 2, DEPTH)
    key = (S, DEPTH)
    m = M(cfg)
    nc = m.build()
    consts = host_consts(cfg)
    wts = {n: np.ascontiguousarray(inputs[n], dtype=np.float32) for n in WSHAPES(DEPTH)}
    zeros = np.zeros((S, D), np.float32)
    in_maps = []
    for c in range(n_cores):
        im = dict(wts)
        im.update(consts)
        im["xin0"] = x_prompt[c]
        im["xin1"] = x_sample[c] if c < DB else zeros
        in_maps.append(im)
    res = run_bass_kernel_spmd(nc, in_maps, core_ids=list(range(n_cores)))
    y_prompt = np.stack([np.asarray(res.results[c]["yout0"], dtype=np.float32) for c in range(n_cores)])
    y_sample = np.stack([np.asarray(res.results[c]["yout1"], dtype=np.float32) for c in range(DB)])
    return (y_prompt, y_sample)
```

```python
import numpy as np
import ml_dtypes
import concourse.bass as bass
import concourse.mybir as mybir
from concourse.bass_utils import run_bass_kernel_spmd

F32 = mybir.dt.float32
BF16 = mybir.dt.bfloat16
AF = mybir.ActivationFunctionType
ALU = mybir.AluOpType
AX = mybir.AxisListType

EPOCH = 24000
DMA_SEM_MAX = 24000


class Obj:
    def __init__(self, k, t, name, is_dram=False):
        self.k = k
        self.t = t
        self.name = name
        self.is_dram = is_dram
        self.w = []
        self.r = []
        self.sem = None

    def ap(self):
        return self.t.ap() if self.is_dram else self.t[:]

    def __getitem__(self, key):
        base = self.t.ap() if self.is_dram else self.t
        return V(self, base[key])


class SubObj(Obj):
    def __init__(self, view_ap, name):
        self.t = None
        self.view = view_ap
        self.name = name
        self.is_dram = False
        self.w = []
        self.r = []
        self.sem = None

    def __getitem__(self, key):
        return V(self, self.view[key])


class V:
    def __init__(self, obj, ap):
        self.obj = obj
        self.ap = ap

    def __getitem__(self, key):
        return V(self.obj, self.ap[key])

    def __getattr__(self, name):
        attr = getattr(self.ap, name)
        if callable(attr):
            def f(*a, **kw):
                r = attr(*a, **kw)
                if isinstance(r, bass.AP):
                    return V(self.obj, r)
                return r
            return f
        return attr


ENGS = ("pe", "act", "dve", "pool", "sp")


class K:
    def __init__(self, nc):
        self.nc = nc
        self.eng = {"pe": nc.tensor, "act": nc.scalar, "dve": nc.vector,
                    "pool": nc.gpsimd, "sp": nc.sync}
        self.cnt = {e: 0 for e in ENGS}
        self.esems = {e: [] for e in ENGS}
        self.known = {e: {} for e in ENGS}
        self.free_dma_sems = []
        self.all_dma_sems = []
        self.n_sems = 0
        self.n_inst = 0
        self.n_wait = 0
        self.sb_off = 0
        self.sb_base = 16640
        self.sb_limit = 229376
        self.sb_stack = []
        self.objs_live = []
        self.uid = 0

    def _new_sem(self, name):
        self.n_sems += 1
        return self.nc.alloc_semaphore(name)

    def _esem(self, e, idx):
        lst = self.esems[e]
        while len(lst) <= idx:
            lst.append(self._new_sem(f"e_{e}_{len(lst)}"))
        return lst[idx]

    def _dma_sem(self, obj):
        if obj.sem is None or obj.sem[1] > DMA_SEM_MAX:
            if self.free_dma_sems:
                obj.sem = self.free_dma_sems.pop()
            else:
                ent = [self._new_sem(f"d_{len(self.all_dma_sems)}"), 0]
                self.all_dma_sems.append(ent)
                obj.sem = ent
            if obj.sem[1] > DMA_SEM_MAX:
                ent = [self._new_sem(f"d_{len(self.all_dma_sems)}"), 0]
                self.all_dma_sems.append(ent)
                obj.sem = ent
        return obj.sem

    def _resolve(self, ev):
        if ev[0] == "e":
            _, e, n = ev
            return self._esem(e, (n - 1) // EPOCH), (n - 1) % EPOCH + 1
        else:
            ent = ev[1]
            return ent[0], ent[1]

    def _wait(self, e, ev):
        if ev[0] == "e" and ev[1] == "pe" and e == "pe":
            return
        sem, val = self._resolve(ev)
        key = sem.name if hasattr(sem, "name") else id(sem)
        if self.known[e].get(key, -1) >= val:
            return
        self.known[e][key] = val
        self.eng[e].wait_ge(sem, val)
        self.n_wait += 1

    def _deps(self, e, reads, writes, wd=False):
        for o in reads:
            for ev in o.w:
                self._wait(e, ev)
            if getattr(o, "is_psum", False):
                for ev in o.r:
                    if not (ev[0] == "e" and ev[1] == e):
                        self._wait(e, ev)
        for o in writes:
            if not wd:
                for ev in o.w:
                    self._wait(e, ev)
            for ev in o.r:
                self._wait(e, ev)

    def _record(self, ev, reads, writes, wd=False):
        for o in reads:
            if o in writes:
                continue
            o.r.append(ev)
            if len(o.r) > 12:
                o.r = self._compact(o.r)
        for o in writes:
            if wd:
                o.w.append(ev)
                if len(o.w) > 12:
                    o.w = self._compact(o.w)
            else:
                o.w = [ev]
            o.r = []

    @staticmethod
    def _compact(evs):
        best = {}
        out = []
        for ev in evs:
            if ev[0] == "e":
                if ev[1] not in best or best[ev[1]][2] < ev[2]:
                    best[ev[1]] = ev
            else:
                if not any(x[0] == "d" and x[1] is ev[1] for x in out):
                    out.append(ev)
        return out + list(best.values())

    def op(self, e, method, **kw):
        reads, writes, args = [], [], {}
        wd = kw.pop("wd", False)
        for name, v in kw.items():
            if isinstance(v, V):
                (writes if name in ("out", "accum_out", "ap") else reads).append(v.obj)
                args[name] = v.ap
            else:
                args[name] = v
        self._deps(e, reads, writes, wd)
        ins = getattr(self.eng[e], method)(**args)
        self.cnt[e] += 1
        n = self.cnt[e]
        ins.then_inc(self._esem(e, (n - 1) // EPOCH), 1)
        self._record(("e", e, n), reads, writes, wd)
        self.n_inst += 1
        return ins

    def dma(self, out, in_, q="sp", wd=False, **kw):
        oo, io = out.obj, in_.obj
        owner = io if (oo.is_dram and not io.is_dram) else oo
        self._deps(q, [io], [oo], wd)
        ent = self._dma_sem(owner)
        ins = self.eng[q].dma_start(out=out.ap, in_=in_.ap, **kw)
        ent[1] += 16
        ins.then_inc(ent[0], 16)
        self._record(("d", ent), [io], [oo], wd)
        self.n_inst += 1
        return ins

    def barrier(self):
        for e in ENGS:
            for f in ENGS:
                if f == "sp" or self.cnt[f] == 0:
                    continue
                if f == e:
                    continue
                self._wait(e, ("e", f, self.cnt[f]))
            for ent in self.all_dma_sems:
                if ent[1] > 0:
                    self._wait(e, ("d", ent))

    def finish(self):
        self.barrier()

    def dram(self, name, shape, dtype, kind="Internal"):
        t = self.nc.dram_tensor(name, list(shape), dtype, kind=kind)
        return Obj(self, t, name, is_dram=True)

    def push(self):
        self.sb_stack.append((self.sb_off, len(self.objs_live)))

    def pop(self):
        off, n = self.sb_stack.pop()
        for o in self.objs_live[n:]:
            if o.sem is not None:
                self.free_dma_sems.append(o.sem)
                o.sem = None
        del self.objs_live[n:]
        self.sb_off = off

    def sb(self, name, shape, dtype):
        nbytes = int(np.prod(shape[1:])) * mybir.dt.size(dtype)
        nbytes = (nbytes + 63) // 64 * 64
        self.uid += 1
        t = self.nc.alloc_sbuf_tensor_at(f"{name}_{self.uid}", list(shape), dtype,
                                         offset=self.sb_base + self.sb_off)
        self.sb_off += nbytes
        assert self.sb_base + self.sb_off <= self.sb_limit, \
            f"SBUF overflow: {self.sb_off} at {name}"
        o = Obj(self, t, name)
        self.objs_live.append(o)
        return o

    def ps(self, name, shape, dtype=F32):
        t = self.nc.alloc_psum_tensor(name, list(shape), dtype)
        o = Obj(self, t, name)
        o.is_psum = True
        return o


D = 1024
NIN = 7552
DFF = 4096
PAD = 112
NORM_EPS = 1e-6
SUBLN_EPS = 1e-5
RW_LNX_EPS = 64e-5
KAPPA = float(np.exp(-0.5))
B_HQ, B_HFF, B_HFB, B_HI, B_HG = 0, 4, 8, 12, 16
B_SB, B_SC, B_SH = 20, 24, 28
B_DQ, B_DK, B_DV = 32, 36, 40
B_RW = 44
NZB = 59
C_ONORM, C_CONV, C_QN, C_KN, C_SUBLN, C_MU, C_W0, C_A0, C_KK, C_KA, C_RK = 0, 4, 16, 17, 18, 19, 34, 42, 50, 54, 58
RPL = 62

WSHAPES = lambda DEPTH: {
    "meta_tokens": (16, D), "norm_mix_g": (DEPTH, D), "w_in": (DEPTH, D, NIN),
    "hgrn_lb_logits": (2, DEPTH, 512), "hgrn_onorm_g": (DEPTH, 512), "conv_w": (DEPTH, 3, 512),
    "diff_qnorm_g": (DEPTH, 64), "diff_knorm_g": (DEPTH, 64), "diff_lambda": (DEPTH, 4, 64),
    "diff_subln_g": (DEPTH, 128), "rwkv_mu": (DEPTH, 1920), "rwkv_w0": (DEPTH, 2, 512),
    "rwkv_w2": (DEPTH, 2, 64, 512), "rwkv_a0": (DEPTH, 2, 512), "rwkv_a2": (DEPTH, 2, 64, 512),
    "rwkv_g2": (DEPTH, 128, 512), "rwkv_k_k": (DEPTH, 512), "rwkv_k_a": (DEPTH, 512),
    "rwkv_r_k": (DEPTH, 8, 64), "rwkv_lnx_g": (DEPTH, 512), "rwkv_lnx_b": (DEPTH, 512),
    "w_gate": (DEPTH, D, 4 * D), "branch_proj": (DEPTH, 4, 512, D), "w_out": (DEPTH, D, D),
    "norm_mlp_g": (DEPTH, D), "mlp_w1": (DEPTH, D, DFF), "mlp_w2": (DEPTH, DFF, D),
}


class Cfg:
    def __init__(self, S, NSEQ, DEPTH):
        assert S % 128 == 0
        self.S, self.NSEQ, self.DEPTH = S, NSEQ, DEPTH
        self.L = S + 16
        self.LP = S + 128
        self.NT = self.LP // 128
        self.TB = [(c, min(512, self.LP - c)) for c in range(0, self.LP, 512)]
        self.NKT = (self.L + 127) // 128
        self.NC64 = self.LP // 64


def host_consts(cfg):
    LP = cfg.LP
    pos = (np.arange(LP, dtype=np.float32) - PAD).astype(np.float32)
    inv = (500000.0 ** (-np.arange(8, dtype=np.float32) / 8)).astype(np.float32)
    C = np.ones((128, LP), np.float32)
    Sn = np.zeros((128, LP), np.float32)
    rotT = np.zeros((128, 128), np.float32)
    for p in range(128):
        d = p % 64
        if d < 16:
            ang = (pos * inv[d % 8]).astype(np.float32)
            C[p] = np.cos(ang)
            Sn[p] = np.sin(ang)
            if d < 8:
                rotT[p + 8, p] = -1.0
            else:
                rotT[p - 8, p] = 1.0
    idx = np.arange(128)
    lv = np.zeros((14, 128, 128), np.float32)
    for kk_ in range(7):
        b = 1 << kk_
        same = (idx[:, None] // (2 * b)) == (idx[None, :] // (2 * b))
        mk = same & ((idx[:, None] % (2 * b)) < b) & ((idx[None, :] % (2 * b)) >= b)
        lv[kk_] = mk
        lv[7 + kk_] = mk.T
    return {"c_rope": np.stack([C, Sn]).astype(np.float32), "c_rotT": rotT, "c_lvl": lv}


class M:
    def __init__(self, cfg, debug=False, stop=None):
        self.cfg = cfg
        self.debug = debug
        self.stop = stop
        nc = bass.Bass("TRN2", target_bir_lowering=False)
        self.nc = nc
        k = K(nc)
        self.k = k
        S, LP, NSEQ, DEPTH = cfg.S, cfg.LP, cfg.NSEQ, cfg.DEPTH
        dk = "ExternalOutput" if debug else "Internal"
        self.xin = [k.dram(f"xin{s}", [S, D], F32, kind="ExternalInput") for s in range(NSEQ)]
        self.yout = [k.dram(f"yout{s}", [S, D], F32, kind="ExternalOutput") for s in range(NSEQ)]
        self.W = {n: k.dram(n, list(sh), F32, kind="ExternalInput") for n, sh in WSHAPES(DEPTH).items()}
        self.c_rope = k.dram("c_rope", [2, 128, LP], F32, kind="ExternalInput")
        self.c_rotT = k.dram("c_rotT", [128, 128], F32, kind="ExternalInput")
        self.c_lvl = k.dram("c_lvl", [14, 128, 128], F32, kind="ExternalInput")
        self.wb_in = k.dram("wb_in", [DEPTH, D, NIN], BF16)
        self.wb_gate = k.dram("wb_gate", [DEPTH, D, 4 * D], BF16)
        self.wb_bp = k.dram("wb_bp", [DEPTH, 4, 512, D], BF16)
        self.wb_out = k.dram("wb_out", [DEPTH, D, D], BF16)
        self.wb_w1 = k.dram("wb_w1", [DEPTH, D, DFF], BF16)
        self.wb_w2 = k.dram("wb_w2", [DEPTH, DFF, D], BF16)
        self.wb_rw2 = k.dram("wb_rw2", [DEPTH, 128, 512], BF16, kind=dk)
        self.wb_ra2 = k.dram("wb_ra2", [DEPTH, 128, 512], BF16, kind=dk)
        self.wb_rg2 = k.dram("wb_rg2", [DEPTH, 128, 512], BF16, kind=dk)
        self.xs = k.dram("xs", [LP, D], F32, kind=dk)
        self.xs2 = k.dram("xs2", [LP, D], F32, kind=dk)
        self.zT = k.dram("zT", [NZB, 128, LP], F32, kind=dk)
        self.gT = k.dram("gT", [32, 128, LP], BF16, kind=dk)
        self.vtok = k.dram("vtok", [cfg.NKT * 128, 512], BF16, kind=dk)
        self.yT = k.dram("yT", [4, 4, 128, LP], BF16, kind=dk)
        self.xmT = k.dram("xmT", [15, 128, LP], BF16, kind=dk)
        self.ps = [k.ps(f"ps{i}", [128, 512], F32) for i in range(8)]
        self.ei = 0

    def evac(self, out, in_, eng=None):
        k = self.k
        if eng is None:
            eng = ("act", "dve")[self.ei % 2]
            self.ei += 1
        if eng == "act":
            k.op("act", "activation", out=out, in_=in_, func=AF.Copy)
        else:
            k.op(eng, "tensor_copy", out=out, in_=in_)

    def mm(self, out, lhsT, rhs, start=True, stop=True):
        self.k.op("pe", "matmul", out=out, lhsT=lhsT, rhs=rhs, start=start, stop=stop)

    def setup(self):
        k, cfg, W = self.k, self.cfg, self.W
        DEPTH = cfg.DEPTH
        self.idb = k.sb("idb", [128, 128], BF16)
        self.idf = k.sb("idf", [128, 128], F32)
        self.ones_bf = k.sb("ones_bf", [128, 128], BF16)
        self.bones_bf = k.sb("bones_bf", [128, 128], BF16)
        self.MU = k.sb("MU", [128, 128], F32)
        self.MUi = k.sb("MUi", [128, 128], F32)
        self.ML = k.sb("ML", [128, 128], F32)
        self.MLi = k.sb("MLi", [128, 128], F32)
        self.rotT_bf = k.sb("rotT_bf", [128, 128], BF16)
        self.prm = [k.sb(f"prm{l}", [128, RPL], F32) for l in range(DEPTH)]
        self.lbc = k.sb("lbc", [128, 2, DEPTH, 4, 2], F32)
        self.nlam = k.sb("nlam", [128, DEPTH], F32)
        self.epsc = k.sb("epsc", [128, 2], F32)
        for t, c in ((self.idb, 0.0), (self.idf, 0.0), (self.ones_bf, 1.0), (self.bones_bf, 0.0),
                     (self.MU, 1.0), (self.MUi, 1.0), (self.ML, 1.0), (self.MLi, 1.0)):
            k.op("pool", "memset", ap=t[:], constant=c)
        for t in (self.idb, self.idf):
            k.op("pool", "affine_select", out=t[:], in_=t[:], pattern=[[-1, 128]],
                 compare_op=ALU.not_equal, fill=1.0, base=0, channel_multiplier=1)
        for t, cmp, sg in ((self.MU, ALU.is_gt, -1), (self.MUi, ALU.is_ge, -1), (self.ML, ALU.is_gt, 1), (self.MLi, ALU.is_ge, 1)):
            k.op("pool", "affine_select", out=t[:], in_=t[:], pattern=[[-sg, 128]],
                 compare_op=cmp, fill=0.0, base=0, channel_multiplier=sg)
        k.op("pool", "memset", ap=self.epsc[:, 0:1], constant=NORM_EPS)
        k.op("pool", "memset", ap=self.epsc[:, 1:2], constant=SUBLN_EPS)
        k.op("pool", "memset", ap=self.bones_bf[0:64, 0:64], constant=1.0)
        k.op("pool", "memset", ap=self.bones_bf[64:128, 64:128], constant=1.0)
        k.dma(out=self.rotT_bf[:], in_=self.c_rotT[:], q="pool")
        for l in range(DEPTH):
            for r in range(8):
                rs = slice(128 * r, 128 * (r + 1))
                k.dma(out=self.wb_in[l, rs, :], in_=W["w_in"][l, rs, :], q="pool", wd=True)
                k.dma(out=self.wb_gate[l, rs, :], in_=W["w_gate"][l, rs, :], q="pool", wd=True)
                k.dma(out=self.wb_w1[l, rs, :], in_=W["mlp_w1"][l, rs, :], q="pool", wd=True)
                k.dma(out=self.wb_out[l, rs, :], in_=W["w_out"][l, rs, :], q="pool", wd=True)
            for r in range(4):
                k.dma(out=self.wb_w2[l, 1024 * r:1024 * (r + 1), :], in_=W["mlp_w2"][l, 1024 * r:1024 * (r + 1), :], q="pool", wd=True)
                k.dma(out=self.wb_bp[l, r], in_=W["branch_proj"][l, r], q="pool", wd=True)
            k.dma(out=self.wb_rw2[l], in_=W["rwkv_w2"][l].rearrange("d r c -> (d r) c"), q="pool", wd=True)
            k.dma(out=self.wb_ra2[l], in_=W["rwkv_a2"][l].rearrange("d r c -> (d r) c"), q="pool", wd=True)
            k.dma(out=self.wb_rg2[l], in_=W["rwkv_g2"][l], q="pool", wd=True)
        k.push()
        for l in range(DEPTH):
            pr = k.sb(f"pr{l}", [RPL, 128], F32)
            def rows(r0, ap, n):
                k.dma(out=pr[r0:r0 + n, :], in_=ap, wd=True)
            rows(C_ONORM, W["hgrn_onorm_g"][l].rearrange("(b p) -> b p", p=128), 4)
            rows(C_CONV, W["conv_w"][l].rearrange("j (b p) -> (j b) p", p=128), 12)
            for h in range(2):
                k.dma(out=pr[C_QN:C_QN + 1, 64 * h:64 * h + 64], in_=W["diff_qnorm_g"][l:l + 1, :], wd=True)
                k.dma(out=pr[C_KN:C_KN + 1, 64 * h:64 * h + 64], in_=W["diff_knorm_g"][l:l + 1, :], wd=True)
            rows(C_SUBLN, W["diff_subln_g"][l:l + 1, :], 1)
            rows(C_MU, W["rwkv_mu"][l].rearrange("(b p) -> b p", p=128), 15)
            rows(C_W0, W["rwkv_w0"][l].rearrange("d (b p) -> (d b) p", p=128), 8)
            rows(C_A0, W["rwkv_a0"][l].rearrange("d (b p) -> (d b) p", p=128), 8)
            rows(C_KK, W["rwkv_k_k"][l].rearrange("(b p) -> b p", p=128), 4)
            rows(C_KA, W["rwkv_k_a"][l].rearrange("(b p) -> b p", p=128), 4)
            rows(C_RK, W["rwkv_r_k"][l].rearrange("(b q) n -> b (q n)", q=2), 4)
            k.op("pe", "transpose", out=self.ps[0][:, 0:RPL], in_=pr[:, :], identity=self.idf[0:RPL, 0:RPL])
            k.op("dve", "tensor_copy", out=self.prm[l][:], in_=self.ps[0][:, 0:RPL])
        nl = 2 * DEPTH * 4
        pl = k.sb("pl", [nl, 128], F32)
        k.dma(out=pl[:, :], in_=W["hgrn_lb_logits"][:].rearrange("d l (b p) -> (d l b) p", p=128))
        k.op("pe", "transpose", out=self.ps[1][:, 0:nl], in_=pl[:, :], identity=self.idf[0:nl, 0:nl])
        lg = k.sb("lg", [128, 2, DEPTH, 4], F32)
        k.op("dve", "tensor_copy", out=lg[:].rearrange("p d l b -> p (d l b)"), in_=self.ps[1][:, 0:nl])
        mx = k.sb("mx", [128, 2, 4], F32)
        sm = k.sb("sm", [128, 2, 4], F32)
        k.op("dve", "tensor_copy", out=mx[:], in_=lg[:, :, 0, :])
        for l in range(1, DEPTH):
            k.op("dve", "tensor_tensor", out=mx[:], in0=mx[:], in1=lg[:, :, l, :], op=ALU.max)
        for l in range(DEPTH):
            k.op("dve", "tensor_tensor", out=lg[:, :, l, :], in0=lg[:, :, l, :], in1=mx[:], op=ALU.subtract)
        k.op("act", "activation", out=lg[:].rearrange("p d l b -> p (d l b)"), in_=lg[:].rearrange("p d l b -> p (d l b)"), func=AF.Exp)
        k.op("dve", "tensor_copy", out=sm[:], in_=lg[:, :, 0, :])
        for l in range(1, DEPTH):
            k.op("dve", "tensor_tensor", out=sm[:], in0=sm[:], in1=lg[:, :, l, :], op=ALU.add)
        k.op("dve", "reciprocal", out=sm[:], in_=sm[:])
        for l in range(DEPTH):
            k.op("dve", "tensor_tensor", out=lg[:, :, l, :], in0=lg[:, :, l, :], in1=sm[:], op=ALU.mult)
        acc = k.sb("acc", [128, 2, 4], F32)
        k.op("dve", "memset", ap=acc[:], constant=0.0)
        for l in range(DEPTH):
            if l > 0:
                k.op("dve", "tensor_tensor", out=acc[:], in0=acc[:], in1=lg[:, :, l, :], op=ALU.add)
            k.op("dve", "tensor_scalar", out=self.lbc[:, :, l, :, 0], in0=acc[:], scalar1=-1.0, scalar2=1.0, op0=ALU.mult, op1=ALU.add)
            k.op("dve", "tensor_scalar", out=self.lbc[:, :, l, :, 1], in0=acc[:], scalar1=1e-20, scalar2=None, op0=ALU.max)
        lt = k.sb("lt", [128, DEPTH, 256], F32)
        pr2 = k.sb("pr2", [128, DEPTH, 2, 64], F32)
        ss = k.sb("ss2", [128, DEPTH, 2], F32)
        for l in range(DEPTH):
            k.dma(out=lt[:, l, :], in_=W["diff_lambda"][l].rearrange("a n -> (a n)").partition_broadcast(128), wd=True)
        for l in range(DEPTH):
            for j in range(2):
                k.op("dve", "tensor_tensor", out=pr2[:, l, j, :], in0=lt[:, l, 128 * j:128 * j + 64], in1=lt[:, l, 128 * j + 64:128 * j + 128], op=ALU.mult)
                k.op("dve", "tensor_reduce", out=ss[:, l, j:j + 1], in_=pr2[:, l, j, :], axis=AX.X, op=ALU.add)
        k.op("act", "activation", out=ss[:].rearrange("p l j -> p (l j)"), in_=ss[:].rearrange("p l j -> p (l j)"), func=AF.Exp)
        for l in range(DEPTH):
            lam_init = 0.8 - 0.6 * float(np.exp(-0.3 * l))
            k.op("dve", "tensor_tensor", out=self.nlam[:, l:l + 1], in0=ss[:, l, 1:2], in1=ss[:, l, 0:1], op=ALU.subtract)
            k.op("dve", "tensor_scalar", out=self.nlam[:, l:l + 1], in0=self.nlam[:, l:l + 1], scalar1=-lam_init, scalar2=None, op0=ALU.add)
        if self.debug:
            self.dbg_prm = k.dram("dbg_prm", [cfg.DEPTH, 128, RPL], F32, kind="ExternalOutput")
            for l in range(DEPTH):
                k.dma(out=self.dbg_prm[l], in_=self.prm[l][:], q="pool", wd=True)
        k.barrier()
        k.pop()

    def p0(self, s):
        k, cfg = self.k, self.cfg
        k.push()
        zt = k.sb("zt", [128, D], F32)
        k.op("pool", "memset", ap=zt[:], constant=0.0)
        k.dma(out=self.xs[0:PAD, :], in_=zt[0:PAD, :], q="pool", wd=True)
        k.dma(out=self.xs[PAD:128, :], in_=self.W["meta_tokens"][:, :], wd=True)
        for r in range(0, cfg.S, 512):
            rr = min(512, cfg.S - r)
            k.dma(out=self.xs[128 + r:128 + r + rr, :], in_=self.xin[s][r:r + rr, :], wd=True)
        k.barrier()
        k.pop()

    def norm_T(self, x_, gb, hb, st, sq, dst, pst):
        k = self.k
        k.op("pool", "memset", ap=st[:], constant=0.0)
        k.op("act", "activation", out=sq[:], in_=x_, func=AF.Square, accum_out=st[:, 0:1])
        k.op("act", "activation", out=st[:, 1:2], in_=st[:, 0:1], func=AF.Sqrt, scale=1.0 / D, bias=NORM_EPS)
        k.op("dve", "reciprocal", out=st[:, 1:2], in_=st[:, 1:2])
        k.op("dve", "scalar_tensor_tensor", out=hb[:], in0=x_, scalar=st[:, 1:2], in1=gb[:], op0=ALU.mult, op1=ALU.mult)
        pb = pst[:].bitcast(BF16)
        for kc in range(8):
            k.op("pe", "transpose", out=pb[:, kc * 128:(kc + 1) * 128], in_=hb[:, kc * 128:(kc + 1) * 128], identity=self.idb[:])
        self.evac(dst, pb[:, :].rearrange("p (kc t) -> p kc t", kc=8))

    def p1(self, l):
        k, cfg = self.k, self.cfg
        LP, NT = cfg.LP, cfg.NT
        k.push()
        self.hT = k.sb("hT", [128, 8, LP], BF16)
        k.push()
        gb = k.sb("gb", [128, D], F32)
        k.dma(out=gb[:], in_=self.W["norm_mix_g"][l].partition_broadcast(128))
        xt = [k.sb(f"xt{i}", [128, D], F32) for i in range(2)]
        hb = [k.sb(f"hb{i}", [128, D], BF16) for i in range(2)]
        sq = k.sb("sq", [128, D], F32)
        st = [k.sb(f"st{i}", [128, 2], F32) for i in range(2)]
        for i in range(NT):
            x_ = xt[i % 2]
            k.dma(out=x_[:], in_=self.xs[128 * i:128 * (i + 1), :])
            self.norm_T(x_[:], gb, hb[i % 2], st[i % 2], sq, self.hT[:, :, 128 * i:128 * (i + 1)], self.ps[i % 2])
        k.op("pool", "memset", ap=self.hT[:, :, 0:PAD], constant=0.0)
        k.barrier()
        k.pop()

    def proj_fm(self, wsrc, ncols, dst, skip=(), sigmoid=False, odt=F32, tag="pa"):
        k, cfg = self.k, self.cfg
        LP = cfg.LP
        k.push()
        wt = [k.sb(f"{tag}_wt{i}", [128, 8, 512], BF16) for i in range(2)]
        stg = [k.sb(f"{tag}_stg{i}", [128, LP], odt) for i in range(3)]
        wv = wsrc.rearrange("(kc p) n -> p kc n", p=128)
        cnt = 0
        pi = 0
        for g in range((ncols + 511) // 512):
            c0 = 512 * g
            cw = min(512, ncols - c0)
            w_ = wt[g % 2]
            k.dma(out=w_[:, :, 0:cw], in_=wv[:, :, c0:c0 + cw])
            for m in range(cw // 128):
                j = c0 // 128 + m
                if j in skip:
                    continue
                s_ = stg[cnt % 3]
                cnt += 1
                for (t0, n) in cfg.TB:
                    p = self.ps[pi % 4]
                    pi += 1
                    for kc in range(8):
                        self.mm(p[:, 0:n], w_[:, kc, 128 * m:128 * m + 128], self.hT[:, kc, t0:t0 + n], kc == 0, kc == 7)
                    if sigmoid:
                        k.op("act", "activation", out=s_[:, t0:t0 + n], in_=p[:, 0:n], func=AF.Sigmoid)
                    else:
                        self.evac(s_[:, t0:t0 + n], p[:, 0:n])
                k.dma(out=dst[j], in_=s_[:], q="pool", wd=True)
        k.barrier()
        k.pop()

    def pa(self, l):
        k, cfg = self.k, self.cfg
        self.proj_fm(self.wb_in[l], NIN, self.zT, skip=(40, 41, 42, 43), tag="pa")
        k.push()
        wvv = k.sb("wvv", [128, 8, 512], BF16)
        k.dma(out=wvv[:], in_=self.wb_in[l].rearrange("(kc p) n -> p kc n", p=128)[:, :, 5120:5632])
        vst = [k.sb(f"vst{i}", [128, 512], BF16) for i in range(2)]
        for kt in range(cfg.NKT):
            c = PAD + 128 * kt
            kn = min(128, cfg.LP - c)
            p = self.ps[kt % 4]
            for kc in range(8):
                self.mm(p[0:kn, :], self.hT[:, kc, c:c + kn], wvv[:, kc, :], kc == 0, kc == 7)
            self.evac(vst[kt % 2][0:kn, :], p[0:kn, :])
            k.dma(out=self.vtok[128 * kt:128 * kt + kn, :], in_=vst[kt % 2][0:kn, :], q="pool", wd=True)
        k.barrier()
        k.pop()

    def p2(self, l):
        self.proj_fm(self.wb_gate[l], 4 * D, self.gT, sigmoid=True, odt=BF16, tag="pg")

    def p3(self, l):
        k, cfg = self.k, self.cfg
        LP = cfg.LP
        prm = self.prm[l]
        k.push()
        zb = [k.sb(f"c_zb{i}", [128, LP], F32) for i in range(2)]
        zc = [k.sb(f"c_zc{i}", [128, LP], F32) for i in range(2)]
        zh = [k.sb(f"c_zh{i}", [128, LP], F32) for i in range(2)]
        u = k.sb("c_u", [128, LP + 2], F32)
        t1 = k.sb("c_t1", [128, LP], F32)
        yb = [k.sb(f"c_y{i}", [128, LP], BF16) for i in range(2)]
        k.op("pool", "memset", ap=u[:], constant=0.0)
        for b in range(4):
            i = b % 2
            k.dma(out=zb[i][:], in_=self.zT[B_SB + b])
            k.dma(out=zc[i][:], in_=self.zT[B_SC + b])
            k.dma(out=zh[i][:], in_=self.zT[B_SH + b])
            k.op("dve", "tensor_tensor", out=u[:, 1:LP + 1], in0=zc[i][:], in1=zh[i][:], op=ALU.mult)
            k.op("dve", "tensor_scalar", out=t1[:], in0=u[:, 0:LP], scalar1=prm[:, C_CONV + b:C_CONV + b + 1], scalar2=None, op0=ALU.mult)
            k.op("dve", "scalar_tensor_tensor", out=t1[:], in0=u[:, 1:LP + 1], scalar=prm[:, C_CONV + 4 + b:C_CONV + 5 + b], in1=t1[:], op0=ALU.mult, op1=ALU.add)
            k.op("dve", "scalar_tensor_tensor", out=t1[:], in0=u[:, 2:LP + 2], scalar=prm[:, C_CONV + 8 + b:C_CONV + 9 + b], in1=t1[:], op0=ALU.mult, op1=ALU.add)
            k.op("dve", "tensor_tensor", out=yb[i][:], in0=t1[:], in1=zb[i][:], op=ALU.mult)
            k.dma(out=self.yT[1, b], in_=yb[i][:], q="pool", wd=True)
        k.barrier()
        k.pop()

    def tr_chunks(self, src, dst, nchunks, width, psbase=6):
        k = self.k
        per = 1024 // 128
        for gi, c0 in enumerate(range(0, nchunks, per)):
            nb = min(per, nchunks - c0)
            pb = self.ps[psbase + gi % 2][:].bitcast(BF16)
            for j in range(nb):
                k.op("pe", "transpose", out=pb[0:width, 128 * j:128 * j + 128],
                     in_=src[:, width * (c0 + j):width * (c0 + j + 1)], identity=self.idb[:])
            self.evac(dst[:, c0:c0 + nb, :], pb[0:width, 0:128 * nb].rearrange("p (c t) -> p c t", t=128))

    def p4(self, l):
        k, cfg = self.k, self.cfg
        LP = cfg.LP
        NC = LP // 64
        prm, lbc = self.prm[l], self.lbc
        k.push()
        msk = k.sb("h_msk", [128, LP], F32)
        k.op("pool", "memset", ap=msk[:], constant=1.0)
        k.op("pool", "memset", ap=msk[:].rearrange("p (c t) -> p c t", t=64)[:, :, 0:1], constant=0.0)
        zq = k.sb("h_zq", [128, LP], F32)
        zf = k.sb("h_zf", [128, LP], F32)
        tmp1 = k.sb("h_tmp1", [128, LP], F32)
        cum = k.sb("h_cum", [128, LP], F32)
        tmpA = k.sb("h_tmpA", [128, LP], F32)
        oacc = k.sb("h_oacc", [128, LP], F32)
        Qt = k.sb("h_Qt", [128, LP], BF16)
        Kt = k.sb("h_Kt", [128, LP], BF16)
        vb = k.sb("h_vb", [128, LP], BF16)
        vtok = k.sb("h_vtok", [64, NC, 128], BF16)
        Kttok = k.sb("h_Kttok", [64, NC, 128], BF16)
        eref = k.sb("h_eref", [128, NC], F32)
        dec = k.sb("h_dec", [128, NC], F32)
        e5 = k.sb("h_e5", [128, NC], F32)
        S32s = [k.sb(f"h_S32{i}", [128, 128], F32) for i in range(2)]
        tmpSs = [k.sb(f"h_tmpS{i}", [128, 128], F32) for i in range(2)]
        Sbs = [k.sb(f"h_Sb{i}", [128, 128], BF16) for i in range(2)]
        At = [k.sb(f"h_At{i}", [64, 64], BF16) for i in range(2)]
        cum3 = cum[:].rearrange("p (c t) -> p c t", t=64)
        tmpA3 = tmpA[:].rearrange("p (c t) -> p c t", t=64)
        for hd in range(4):
            k.dma(out=zq[:], in_=self.zT[B_HQ + hd])
            k.dma(out=tmpA[:], in_=self.zT[B_HI + hd])
            k.op("act", "activation", out=vb[:], in_=tmpA[:], func=AF.Copy)
            self.tr_chunks(vb, vtok, NC, 64)
            for d in range(2):
                k.dma(out=zf[:], in_=self.zT[B_HFF + 4 * d + hd])
                k.op("act", "activation", out=zf[:], in_=zf[:], func=AF.Sigmoid)
                k.op("dve", "tensor_scalar", out=zf[:], in0=zf[:], scalar1=lbc[:, d, l, hd, 0:1], scalar2=lbc[:, d, l, hd, 1:2], op0=ALU.mult, op1=ALU.add)
                k.op("act", "activation", out=tmp1[:], in_=zf[:], func=AF.Ln)
                k.op("dve", "tensor_scalar", out=zf[:], in0=zf[:], scalar1=-1.0, scalar2=1.0, op0=ALU.mult, op1=ALU.add)
                k.op("dve", "tensor_tensor_scan", out=cum[:], data0=msk[:], data1=tmp1[:], initial=0.0, op0=ALU.mult, op1=ALU.add)
                if d == 1:
                    k.op("dve", "tensor_tensor", out=tmpA3, in0=cum3[:, :, 63:64].to_broadcast([128, NC, 64]), in1=cum3, op=ALU.subtract)
                    k.op("dve", "tensor_tensor", out=cum[:], in0=tmpA[:], in1=tmp1[:], op=ALU.add)
                tot = cum3[:, :, 63] if d == 0 else cum3[:, :, 0]
                refi = 31 if d == 0 else 32
                k.op("act", "activation", out=eref[:], in_=cum3[:, :, refi], func=AF.Exp)
                k.op("act", "activation", out=dec[:], in_=tot, func=AF.Exp)
                k.op("dve", "tensor_tensor", out=e5[:], in0=tot, in1=cum3[:, :, refi], op=ALU.subtract)
                k.op("act", "activation", out=e5[:], in_=e5[:], func=AF.Exp)
                k.op("dve", "tensor_tensor", out=tmpA3, in0=cum3, in1=cum3[:, :, refi:refi + 1].to_broadcast([128, NC, 64]), op=ALU.subtract)
                k.op("act", "activation", out=tmpA[:], in_=tmpA[:], func=AF.Exp)
                k.op("dve", "tensor_tensor", out=Qt[:], in0=zq[:], in1=tmpA[:], op=ALU.mult)
                k.op("dve", "tensor_tensor", out=tmpA3, in0=cum3, in1=cum3[:, :, refi:refi + 1].to_broadcast([128, NC, 64]), op=ALU.subtract)
                k.op("act", "activation", out=tmpA[:], in_=tmpA[:], func=AF.Exp, scale=-1.0)
                k.op("dve", "tensor_tensor", out=Kt[:], in0=zf[:], in1=tmpA[:], op=ALU.mult)
                self.tr_chunks(Kt, Kttok, NC, 64)
                k.op("pool", "memset", ap=S32s[0][:], constant=0.0)
                k.op("pool", "memset", ap=Sbs[0][:], constant=0.0)
                order = list(range(1, NC)) if d == 0 else list(range(NC - 1, 0, -1))
                mask = self.MUi if d == 0 else self.MLi
                for ci, c in enumerate(order):
                    cs = slice(64 * c, 64 * c + 64)
                    pA, pO, pS = self.ps[ci % 2], self.ps[2 + ci % 2], self.ps[4 + ci % 2]
                    at = At[ci % 2]
                    Sb, tmpS = Sbs[ci % 2], tmpSs[ci % 2]
                    S32, S32n = S32s[ci % 2], S32s[(ci + 1) % 2]
                    more = ci + 1 < len(order)
                    self.mm(pA[0:64, 0:64], Kt[:, cs], Qt[:, cs])
                    if more:
                        cn = order[ci + 1]
                        self.mm(pS[:, 0:128], Kttok[:, c, :], vtok[:, c, :])
                        k.op("dve", "tensor_scalar", out=tmpS[:], in0=S32[:], scalar1=dec[:, c:c + 1], scalar2=None, op0=ALU.mult)
                        k.op("dve", "scalar_tensor_tensor", out=S32n[:], in0=pS[:, 0:128], scalar=e5[:, c:c + 1], in1=tmpS[:], op0=ALU.mult, op1=ALU.add)
                        k.op("act", "activation", out=Sbs[(ci + 1) % 2][:], in_=S32n[:], func=AF.Copy, scale=eref[:, cn:cn + 1])
                    k.op("dve", "tensor_tensor", out=at[:], in0=pA[0:64, 0:64], in1=mask[0:64, 0:64], op=ALU.mult)
                    self.mm(pO[:, 0:64], vtok[:, c, :], at[:], True, False)
                    self.mm(pO[:, 0:64], Sb[:], Qt[:, cs], False, True)
                    if d == 0:
                        k.op("act", "activation", out=oacc[:, cs], in_=pO[:, 0:64], func=AF.Copy)
                    else:
                        k.op("dve", "tensor_tensor", out=oacc[:, cs], in0=pO[:, 0:64], in1=oacc[:, cs], op=ALU.add)
            k.op("act", "activation", out=vb[:], in_=oacc[:], func=AF.Square)
            for bi, (t0, n) in enumerate(cfg.TB):
                p = self.ps[bi % 2]
                self.mm(p[:, 0:n], self.ones_bf[:], vb[:, t0:t0 + n])
                k.op("act", "activation", out=tmpA[:, t0:t0 + n], in_=p[:, 0:n], func=AF.Ln, scale=1.0 / 128, bias=self.epsc[:, 0:1])
            k.op("act", "activation", out=tmpA[:], in_=tmpA[:], func=AF.Exp, scale=-0.5)
            k.dma(out=zf[:], in_=self.zT[B_HG + hd])
            k.op("act", "activation", out=zf[:], in_=zf[:], func=AF.Silu)
            k.op("dve", "scalar_tensor_tensor", out=cum[:], in0=oacc[:], scalar=prm[:, C_ONORM + hd:C_ONORM + hd + 1], in1=tmpA[:], op0=ALU.mult, op1=ALU.mult)
            k.op("dve", "tensor_tensor", out=Qt[:], in0=cum[:], in1=zf[:], op=ALU.mult)
            k.dma(out=self.yT[0, hd], in_=Qt[:], q="pool", wd=True)
        k.barrier()
        k.pop()

    def p5(self, l):
        k, cfg = self.k, self.cfg
        LP, NKT = cfg.LP, cfg.NKT
        prm = self.prm[l]
        lam_init = 0.8 - 0.6 * float(np.exp(-0.3 * l))
        k.push()
        ropeC = k.sb("a_rc", [128, LP], F32)
        ropeS = k.sb("a_rs", [128, LP], F32)
        k.dma(out=ropeC[:], in_=self.c_rope[0])
        k.dma(out=ropeS[:], in_=self.c_rope[1])
        gq = k.sb("a_gq", [128, 3], F32)
        k.op("dve", "tensor_scalar", out=gq[:, 0:1], in0=prm[:, C_QN:C_QN + 1], scalar1=0.125, scalar2=None, op0=ALU.mult)
        k.op("dve", "tensor_copy", out=gq[:, 1:2], in_=prm[:, C_KN:C_KN + 1])
        k.op("dve", "tensor_scalar", out=gq[:, 2:3], in0=prm[:, C_SUBLN:C_SUBLN + 1], scalar1=1.0 - lam_init, scalar2=None, op0=ALU.mult)
        z = k.sb("a_z", [128, LP], F32)
        tmpA = k.sb("a_tmpA", [128, LP], F32)
        sqb = k.sb("a_sqb", [128, LP], BF16)
        znb = k.sb("a_znb", [128, LP], BF16)
        qh = k.sb("a_qh", [128, LP], BF16)
        kh = k.sb("a_kh", [128, LP], BF16)
        vt = k.sb("a_vt", [128, NKT, 128], BF16)
        oT = k.sb("a_oT", [128, LP], F32)
        E = [k.sb(f"a_E{i}", [128, 512], BF16) for i in range(4)]
        r0 = k.sb("a_r0", [128, 512], F32)
        r1 = k.sb("a_r1", [128, 512], F32)
        t1 = k.sb("a_t1", [128, 512], F32)
        for hd in range(4):
            for blk, gcol, dst in ((B_DQ + hd, 0, qh), (B_DK + hd, 1, kh)):
                k.dma(out=z[:], in_=self.zT[blk])
                k.op("act", "activation", out=sqb[:], in_=z[:], func=AF.Square)
                for bi, (t0, n) in enumerate(cfg.TB):
                    p = self.ps[bi % 2]
                    self.mm(p[:, 0:n], self.bones_bf[:], sqb[:, t0:t0 + n])
                    k.op("act", "activation", out=tmpA[:, t0:t0 + n], in_=p[:, 0:n], func=AF.Ln, scale=1.0 / 64, bias=self.epsc[:, 0:1])
                k.op("act", "activation", out=tmpA[:], in_=tmpA[:], func=AF.Exp, scale=-0.5)
                k.op("dve", "scalar_tensor_tensor", out=z[:], in0=z[:], scalar=gq[:, gcol:gcol + 1], in1=tmpA[:], op0=ALU.mult, op1=ALU.mult)
                k.op("act", "activation", out=znb[:], in_=z[:], func=AF.Copy)
                for bi, (t0, n) in enumerate(cfg.TB):
                    p = self.ps[2 + bi % 2]
                    self.mm(p[:, 0:n], self.rotT_bf[:], znb[:, t0:t0 + n])
                    k.op("dve", "tensor_tensor", out=tmpA[:, t0:t0 + n], in0=p[:, 0:n], in1=ropeS[:, t0:t0 + n], op=ALU.mult)
                k.op("dve", "tensor_tensor", out=z[:], in0=z[:], in1=ropeC[:], op=ALU.mult)
                k.op("dve", "tensor_tensor", out=dst[:], in0=z[:], in1=tmpA[:], op=ALU.add)
            k.dma(out=vt[:], in_=self.vtok[:, hd * 128:(hd + 1) * 128].rearrange("(kt p) v -> p kt v", p=128))
            for qi, (q0, n) in enumerate(cfg.TB):
                def scores(kt):
                    c = PAD + 128 * kt
                    kn = min(128, LP - c)
                    es = []
                    for m in range(2):
                        pS = self.ps[2 * (kt % 2) + m]
                        self.mm(pS[0:kn, 0:n], kh[64 * m:64 * m + 64, c:c + kn], qh[64 * m:64 * m + 64, q0:q0 + n])
                        e_ = E[2 * (kt % 2) + m]
                        k.op("act", "activation", out=e_[0:kn, 0:n], in_=pS[0:kn, 0:n], func=AF.Exp)
                        es.append(e_)
                    return kn, es
                nxt = scores(0)
                for kt in range(NKT):
                    kn, es = nxt
                    if kt + 1 < NKT:
                        nxt = scores(kt + 1)
                    for m in range(2):
                        self.mm(self.ps[4 + m][:, 0:n], vt[0:kn, kt, :], es[m][0:kn, 0:n], kt == 0, kt == NKT - 1)
                        self.mm(self.ps[6 + m][:, 0:n], self.ones_bf[0:kn, :], es[m][0:kn, 0:n], kt == 0, kt == NKT - 1)
                k.op("act", "activation", out=r0[:, 0:n], in_=self.ps[6][:, 0:n], func=AF.Ln)
                k.op("act", "activation", out=r1[:, 0:n], in_=self.ps[7][:, 0:n], func=AF.Ln)
                k.op("act", "activation", out=r0[:, 0:n], in_=r0[:, 0:n], func=AF.Exp, scale=-1.0)
                k.op("act", "activation", out=r1[:, 0:n], in_=r1[:, 0:n], func=AF.Exp, scale=-1.0)
                k.op("dve", "tensor_tensor", out=oT[:, q0:q0 + n], in0=self.ps[4][:, 0:n], in1=r0[:, 0:n], op=ALU.mult)
                k.op("dve", "tensor_tensor", out=t1[:, 0:n], in0=self.ps[5][:, 0:n], in1=r1[:, 0:n], op=ALU.mult)
                k.op("dve", "scalar_tensor_tensor", out=oT[:, q0:q0 + n], in0=t1[:, 0:n], scalar=self.nlam[:, l:l + 1], in1=oT[:, q0:q0 + n], op0=ALU.mult, op1=ALU.add)
            k.op("act", "activation", out=sqb[:], in_=oT[:], func=AF.Square)
            for bi, (t0, n) in enumerate(cfg.TB):
                p = self.ps[bi % 2]
                self.mm(p[:, 0:n], self.ones_bf[:], sqb[:, t0:t0 + n])
                k.op("act", "activation", out=tmpA[:, t0:t0 + n], in_=p[:, 0:n], func=AF.Ln, scale=1.0 / 128, bias=self.epsc[:, 1:2])
            k.op("act", "activation", out=tmpA[:], in_=tmpA[:], func=AF.Exp, scale=-0.5)
            k.op("dve", "scalar_tensor_tensor", out=znb[:], in0=oT[:], scalar=gq[:, 2:3], in1=tmpA[:], op0=ALU.mult, op1=ALU.mult)
            k.dma(out=self.yT[2, hd], in_=znb[:], q="pool", wd=True)
        k.barrier()
        k.pop()

    def p6a(self, l):
        k, cfg = self.k, self.cfg
        LP = cfg.LP
        prm = self.prm[l]
        k.push()
        zb = [k.sb(f"r_zb{i}", [128, LP + 2], F32) for i in range(2)]
        s_ = k.sb("r_s", [128, LP], F32)
        xm = [k.sb(f"r_xm{i}", [128, LP], BF16) for i in range(2)]
        hm = k.sb("r_hm", [128, 15, 2], F32)
        for t in zb:
            k.op("pool", "memset", ap=t[:], constant=0.0)
        k.op("dve", "tensor_scalar", out=hm[:, :, 0], in0=prm[:, C_MU:C_MU + 15], scalar1=-1.0, scalar2=1.0, op0=ALU.mult, op1=ALU.add)
        k.op("dve", "tensor_scalar", out=hm[:, :, 1], in0=prm[:, C_MU:C_MU + 15], scalar1=0.5, scalar2=None, op0=ALU.mult)
        for b in range(15):
            i = b % 2
            k.dma(out=zb[i][:, 1:LP + 1], in_=self.zT[B_RW + b])
            k.op("dve", "tensor_tensor", out=s_[:], in0=zb[i][:, 0:LP], in1=zb[i][:, 2:LP + 2], op=ALU.add)
            k.op("dve", "tensor_scalar", out=s_[:], in0=s_[:], scalar1=hm[:, b, 1:2], scalar2=None, op0=ALU.mult)
            k.op("dve", "scalar_tensor_tensor", out=xm[i][:], in0=zb[i][:, 1:LP + 1], scalar=hm[:, b, 0:1], in1=s_[:], op0=ALU.mult, op1=ALU.add)
            k.op("pool", "memset", ap=xm[i][:, 0:PAD], constant=0.0)
            k.dma(out=self.xmT[b], in_=xm[i][:], q="pool", wd=True)
        k.barrier()
        k.pop()

    def p6b(self, l):
        k, cfg = self.k, self.cfg
        LP, NT = cfg.LP, cfg.NT
        prm = self.prm[l]
        ps = self.ps
        k.push()
        Of = k.sb("w_Of", [128, NT, 512], BF16)
        Bn = k.sb("w_Bn", [128, 4, LP], BF16)
        w2b = k.sb("w_w2b", [128, 512], BF16)
        a2b = k.sb("w_a2b", [128, 512], BF16)
        g2b = k.sb("w_g2b", [128, 512], BF16)
        lng = k.sb("w_lng", [128, 512], F32)
        lnb = k.sb("w_lnb", [128, 512], F32)
        k.dma(out=w2b[:], in_=self.wb_rw2[l])
        k.dma(out=a2b[:], in_=self.wb_ra2[l])
        k.dma(out=g2b[:], in_=self.wb_rg2[l])
        k.dma(out=lng[:], in_=self.W["rwkv_lnx_g"][l].partition_broadcast(128))
        k.dma(out=lnb[:], in_=self.W["rwkv_lnx_b"][l].partition_broadcast(128))
        LM = k.sb("w_LM", [128, 14, 128], BF16)
        k.dma(out=LM[:], in_=self.c_lvl[:].rearrange("m p j -> p m j"), q="pool")
        rmask = k.sb("w_rmask", [128, 4, 128], F32)
        k.op("pool", "memset", ap=rmask[:], constant=1.0)
        k.op("pool", "memset", ap=rmask[:, :, 0:1], constant=0.0)
        ST32 = k.sb("w_ST32", [128, 4, 64], F32)
        STb = k.sb("w_STb", [128, 4, 64], BF16)
        f32t = lambda n: k.sb("w_" + n, [128, 4, 128], F32)
        bf16t = lambda n: k.sb("w_" + n, [128, 4, 128], BF16)
        kkr, rn, kk, sgw, a_, G, Gex, E1, E2, kd, akk = [
            f32t(n) for n in ("kkr", "rn", "kk", "sgw", "a", "G", "Gex", "E1", "E2", "kd", "akk")]
        tmp = Gex
        Bg, Kg, rkb = [bf16t(n) for n in ("Bg", "Kg", "rkb")]
        sqk = rkb
        thb = k.sb("w_thb", [128, 128], BF16)
        sgl = k.sb("w_sgl", [128, 128], BF16)
        PB = []
        for i_ in range(2):
            PB.append(dict(
                X=k.sb(f"w_X{i_}", [128, 15, 128], BF16),
                AR=k.sb(f"w_AR{i_}", [128, 4, 2, 128], BF16),
                Bt=bf16t(f"Bt{i_}"), Ktl=bf16t(f"Ktl{i_}"),
                Bgtok=k.sb(f"w_Bgtok{i_}", [128, 512], BF16), Kgtok=k.sb(f"w_Kgtok{i_}", [128, 512], BF16),
                Vtok=k.sb(f"w_Vtok{i_}", [128, 512], BF16), gamC=k.sb(f"w_gamC{i_}", [128, 4], F32),
                bont=f32t(f"bont{i_}"), Gt=f32t(f"Gt{i_}")))
        Ot, sq2, otmp = f32t("Ot"), f32t("sq2"), f32t("otmp")
        Xnb, yst = bf16t("Xnb"), bf16t("yst")
        st8 = k.sb("w_st8", [128, 4, 8], F32)
        MSS = [k.sb(f"w_MSS{i}", [128, 2, 128], BF16) for i in range(2)]
        for i_, (ma, mb) in enumerate(((self.MU, self.MUi), (self.ML, self.MLi))):
            k.op("dve", "tensor_copy", out=MSS[i_][:, 0, :], in_=ma[:])
            k.op("dve", "tensor_copy", out=MSS[i_][:, 1, :], in_=mb[:])
        hs = []
        for hd_ in range(8):
            d_ = {}
            d_["NmT"] = k.sb(f"w_NmT{hd_}", [128, 128], BF16)
            for n in ("NN", "MM", "TT2", "ZZ"):
                d_[n] = k.sb(f"w_{n}{hd_}", [128, 2, 128], BF16)
            d_["NkA"] = k.sb(f"w_NkA{hd_}", [128, 6, 128], BF16)
            d_["NkTA"] = k.sb(f"w_NkTA{hd_}", [128, 7, 128], BF16)
            d_["WTb"] = k.sb(f"w_WTb{hd_}", [128, 64], BF16)
            d_["UTb"] = k.sb(f"w_UTb{hd_}", [128, 64], BF16)
            hs.append(d_)
        SB_ = ((0, 2, 6), (1, 3, 7))
        sctr = [0, 0]
        nbk = [3]

        def slot(par_):
            i_ = sctr[par_]
            sctr[par_] += 1
            j_ = (i_ // nbk[0]) % 4
            return ps[SB_[par_][i_ % nbk[0]]][:, 128 * j_:128 * j_ + 128]

        def dslot(par_):
            i_ = sctr[par_]
            sctr[par_] += 1
            j_ = (i_ // nbk[0]) % 2
            return ps[SB_[par_][i_ % nbk[0]]][:, 256 * j_:256 * j_ + 256]

        fl = lambda t: t[:].rearrange("p b t -> p (b t)")
        v3 = lambda ap_: ap_.rearrange("p (a t) -> p a t", a=2)
        kkp = prm[:, C_KK:C_KK + 4].unsqueeze(2).to_broadcast([128, 4, 128])
        kap = prm[:, C_KA:C_KA + 4].unsqueeze(2).to_broadcast([128, 4, 128])
        rkp = prm[:, C_RK:C_RK + 4].unsqueeze(2).to_broadcast([128, 4, 128])
        G3 = G[:]

        def prep(c, d, P):
            cs = slice(128 * c, 128 * c + 128)
            Xc, AR = P["X"], P["AR"]
            At_, Rt = AR[:, :, 0, :], AR[:, :, 1, :]
            k.dma(out=Xc[:], in_=self.xmT[:, :, cs].rearrange("b p t -> p b t"))
            r_, kx, v_ = Xc[:, 0:4, :], Xc[:, 4:8, :], Xc[:, 8:12, :]
            rs = slice(64 * d, 64 * d + 64)
            pw = ps[4 + d]
            k.op("dve", "tensor_tensor", out=kkr[:], in0=kx, in1=kkp, op=ALU.mult)
            k.op("act", "activation", out=sqk[:], in_=kkr[:], func=AF.Square)
            k.op("act", "activation", out=thb[rs, :], in_=Xc[rs, 12, :], func=AF.Tanh)
            yield
            self.mm(ps[4][:, :], self.bones_bf[:], fl(sqk))
            k.op("dve", "tensor_scalar", out=fl(rn), in0=ps[4][:, :], scalar1=1e-24, scalar2=None, op0=ALU.max)
            k.op("act", "activation", out=rn[:], in_=rn[:], func=AF.Ln)
            k.op("act", "activation", out=rn[:], in_=rn[:], func=AF.Exp, scale=-0.5)
            k.op("dve", "tensor_tensor", out=kk[:], in0=kkr[:], in1=rn[:], op=ALU.mult)
            yield
            for b in range(4):
                self.mm(pw[:, 128 * b:128 * b + 128], w2b[rs, 128 * b:128 * b + 128], thb[rs, :])
            for b in range(4):
                k.op("act", "activation", out=sgw[:, b, :], in_=pw[:, 128 * b:128 * b + 128], func=AF.Sigmoid,
                     bias=prm[:, C_W0 + 4 * d + b:C_W0 + 4 * d + b + 1])
            yield
            for b in range(4):
                self.mm(pw[:, 128 * b:128 * b + 128], a2b[rs, 128 * b:128 * b + 128], Xc[rs, 13, :])
            for b in range(4):
                k.op("act", "activation", out=a_[:, b, :], in_=pw[:, 128 * b:128 * b + 128], func=AF.Sigmoid,
                     bias=prm[:, C_A0 + 4 * d + b:C_A0 + 4 * d + b + 1])
            yield
            k.op("dve", "tensor_tensor_scan", out=fl(G), data0=fl(rmask), data1=fl(sgw), initial=0.0, op0=ALU.mult, op1=ALU.add)
            if d == 1:
                k.op("dve", "tensor_tensor", out=Gex[:], in0=G3[:, :, 127:128].to_broadcast([128, 4, 128]), in1=G[:], op=ALU.subtract)
                k.op("dve", "tensor_tensor", out=G[:], in0=Gex[:], in1=sgw[:], op=ALU.add)
            tot = G3[:, :, 127] if d == 0 else G3[:, :, 0]
            totb = (G3[:, :, 127:128] if d == 0 else G3[:, :, 0:1]).to_broadcast([128, 4, 128])
            yield
            k.op("act", "activation", out=E1[:], in_=G[:], func=AF.Exp, scale=-KAPPA)
            k.op("dve", "tensor_tensor", out=Rt, in0=r_, in1=E1[:], op=ALU.mult)
            k.op("dve", "tensor_tensor", out=Gex[:], in0=G[:], in1=sgw[:], op=ALU.subtract)
            k.op("act", "activation", out=E2[:], in_=Gex[:], func=AF.Exp, scale=-KAPPA)
            yield
            k.op("dve", "scalar_tensor_tensor", out=At_, in0=kk[:], scalar=-1.0, in1=E2[:], op0=ALU.mult, op1=ALU.mult)
            k.op("act", "activation", out=E1[:], in_=G[:], func=AF.Exp, scale=KAPPA)
            k.op("dve", "tensor_tensor", out=tmp[:], in0=totb, in1=G[:], op=ALU.subtract)
            k.op("act", "activation", out=E2[:], in_=tmp[:], func=AF.Exp, scale=-KAPPA)
            k.op("act", "activation", out=P["gamC"][:], in_=tot, func=AF.Exp, scale=-KAPPA)
            yield
            k.op("dve", "scalar_tensor_tensor", out=tmp[:], in0=a_[:], scalar=-1.0, in1=kap, op0=ALU.add, op1=ALU.mult)
            k.op("dve", "scalar_tensor_tensor", out=kd[:], in0=tmp[:], scalar=1.0, in1=kx, op0=ALU.add, op1=ALU.mult)
            k.op("dve", "tensor_tensor", out=akk[:], in0=a_[:], in1=kk[:], op=ALU.mult)
            yield
            k.op("dve", "tensor_tensor", out=P["Bt"][:], in0=akk[:], in1=E1[:], op=ALU.mult)
            k.op("dve", "tensor_tensor", out=P["Ktl"][:], in0=kd[:], in1=E1[:], op=ALU.mult)
            yield
            k.op("dve", "tensor_tensor", out=Bg[:], in0=akk[:], in1=E2[:], op=ALU.mult)
            k.op("dve", "tensor_tensor", out=Kg[:], in0=kd[:], in1=E2[:], op=ALU.mult)
            yield
            for src, dst_, pbk in ((Bg, P["Bgtok"], ps[4]), (Kg, P["Kgtok"], ps[5]), (None, P["Vtok"], ps[4])):
                pb = pbk[:].bitcast(BF16)
                for b in range(4):
                    in_ = Xc[:, 8 + b, :] if src is None else src[:, b, :]
                    k.op("pe", "transpose", out=pb[:, 128 * b:128 * b + 128], in_=in_, identity=self.idb[:])
                k.op("act", "activation", out=dst_[:], in_=pb[:, 0:512], func=AF.Copy)
                yield
            k.op("dve", "tensor_tensor", out=tmp[:], in0=r_, in1=kd[:], op=ALU.mult)
            k.op("dve", "tensor_tensor", out=rkb[:], in0=tmp[:], in1=rkp, op=ALU.mult)
            self.mm(ps[5][:, :], self.bones_bf[:], fl(rkb))
            yield
            if d == 0:
                k.op("dve", "tensor_tensor", out=Bn[:, :, cs], in0=ps[5][:, :].rearrange("p (b t) -> p b t", t=128), in1=v_, op=ALU.mult)
            else:
                k.op("dve", "tensor_tensor", out=P["bont"][:], in0=ps[5][:, :].rearrange("p (b t) -> p b t", t=128), in1=v_, op=ALU.mult)
                k.op("dve", "tensor_tensor", out=P["bont"][:], in0=P["bont"][:], in1=Bn[:, :, cs], op=ALU.add)
                yield
                k.op("act", "activation", out=sgl[:], in_=Xc[:, 14, :], func=AF.Sigmoid)
                for b in range(4):
                    self.mm(ps[4][:, 128 * b:128 * b + 128], g2b[:, 128 * b:128 * b + 128], sgl[:])
                k.op("act", "activation", out=fl(P["Gt"]), in_=ps[4][:, :], func=AF.Copy)
            yield

        def stages(c, d, P, pump):
            cs = slice(128 * c, 128 * c + 128)
            AR, Bt, Ktl, Vtok, Bgtok, Kgtok, gamC = P["AR"], P["Bt"], P["Ktl"], P["Vtok"], P["Bgtok"], P["Kgtok"], P["gamC"]
            mST = self.ML if d == 0 else self.MU
            mo, mto = (0, 7) if d == 0 else (7, 0)
            hv = []
            for hd in range(8):
                blk, par = hd // 2, hd % 2
                hr = slice(64 * par, 64 * par + 64)
                hv.append(dict(blk=blk, par=par, hr=hr, H=hs[hd],
                               Ah=AR[hr, blk, 0, :], Bh=Bt[hr, blk, :], Kh=Ktl[hr, blk, :], Rh=AR[hr, blk, 1, :],
                               ARh=AR[hr, blk, :, :].rearrange("p a t -> p (a t)"),
                               Vh=Vtok[:, 128 * blk + 64 * par:128 * blk + 64 * par + 64], STh=STb[hr, blk, :]))
            nbk[0] = 3
            for x in hv:
                H, par = x["H"], x["par"]
                p_ = dslot(par)
                self.mm(p_, x["Bh"], x["ARh"])
                k.op("dve", "tensor_tensor", out=H["NN"][:], in0=v3(p_), in1=MSS[d][:], op=ALU.mult)
                p_ = dslot(par)
                self.mm(p_, x["Kh"], x["ARh"])
                k.op("dve", "tensor_tensor", out=H["MM"][:], in0=v3(p_), in1=MSS[d][:], op=ALU.mult)
                p_ = slot(par)
                self.mm(p_, x["Ah"], x["Bh"])
                k.op("dve", "tensor_tensor", out=H["NmT"][:], in0=p_, in1=mST[:], op=ALU.mult)
            pump()
            for x in hv:
                H = x["H"]
                k.op("pool", "tensor_tensor", out=H["NkA"][:], in0=H["NN"][:, 0:1, :].to_broadcast([128, 6, 128]),
                     in1=LM[:, mo:mo + 6, :], op=ALU.mult)
                k.op("pool", "tensor_tensor", out=H["NkTA"][:], in0=H["NmT"][:].unsqueeze(1).to_broadcast([128, 7, 128]),
                     in1=LM[:, mto:mto + 7, :], op=ALU.mult)
                k.op("pool", "tensor_tensor", out=H["TT2"][:, 0, :], in0=H["NkA"][:, 0, :], in1=self.idb[:], op=ALU.add)
                k.op("pool", "tensor_tensor", out=H["TT2"][:, 1, :], in0=H["NkTA"][:, 0, :], in1=self.idb[:], op=ALU.add)
            pump()
            for lv in range(1, 7):
                lastlv = lv == 6
                for x in hv:
                    H, par = x["H"], x["par"]
                    pz = dslot(par)
                    self.mm(pz[:, 0:128], H["NkTA"][:, lv, :], H["TT2"][:, 0, :])
                    if not lastlv:
                        self.mm(pz[:, 128:256], H["NkA"][:, lv, :], H["TT2"][:, 1, :])
                        k.op("act", "activation", out=H["ZZ"][:], in_=v3(pz), func=AF.Copy)
                    else:
                        k.op("act", "activation", out=H["ZZ"][:, 0, :], in_=pz[:, 0:128], func=AF.Copy)
                pump()
                for x in hv:
                    H, par = x["H"], x["par"]
                    pt = dslot(par)
                    self.mm(pt[:, 0:128], H["TT2"][:, 1, :], H["ZZ"][:, 0, :])
                    if not lastlv:
                        self.mm(pt[:, 128:256], H["TT2"][:, 0, :], H["ZZ"][:, 1, :])
                        k.op("dve", "tensor_tensor", out=H["TT2"][:], in0=v3(pt), in1=H["TT2"][:], op=ALU.add)
                    else:
                        k.op("dve", "tensor_tensor", out=H["TT2"][:, 0, :], in0=pt[:, 0:128], in1=H["TT2"][:, 0, :], op=ALU.add)
                pump()
            nbk[0] = 2
            for x in hv:
                H, par = x["H"], x["par"]
                p_ = slot(par)
                self.mm(p_[:, 0:64], x["Ah"], x["STh"], True, False)
                self.mm(p_[:, 0:64], H["MM"][:, 0, :], x["Vh"], False, True)
                k.op("act", "activation", out=H["WTb"][:], in_=p_[:, 0:64], func=AF.Copy)
            pump()
            for x in hv:
                H, par = x["H"], x["par"]
                p_ = slot(par)
                self.mm(p_[:, 0:64], H["TT2"][:, 0, :], H["WTb"][:])
                k.op("act", "activation", out=H["UTb"][:], in_=p_[:, 0:64], func=AF.Copy)
            pump()
            for x in hv:
                H, par, blk, hr = x["H"], x["par"], x["blk"], x["hr"]
                po = ps[6 + par][:, 64 * blk:64 * blk + 64]
                self.mm(po, x["Rh"], x["STh"], True, False)
                self.mm(po, H["NN"][:, 1, :], H["UTb"][:], False, False)
                self.mm(po, H["MM"][:, 1, :], x["Vh"], False, True)
                p_ = slot(par)
                self.mm(p_[:, 0:64], Bgtok[:, 128 * blk:128 * blk + 128], H["UTb"][:], True, False)
                self.mm(p_[:, 0:64], Kgtok[:, 128 * blk:128 * blk + 128], x["Vh"], False, True)
                k.op("dve", "scalar_tensor_tensor", out=ST32[hr, blk, :], in0=ST32[hr, blk, :], scalar=gamC[hr, blk:blk + 1],
                     in1=p_[hr, 0:64], op0=ALU.mult, op1=ALU.add)
                k.op("act", "activation", out=STb[hr, blk, :], in_=ST32[hr, blk, :], func=AF.Copy)
            pump()
            Of4 = Of[:, c, :].rearrange("p (b q v) -> p b q v", q=2, v=64)
            if d == 0:
                for par in range(2):
                    self.evac(Of4[:, :, par, :], ps[6 + par][:, 0:256].rearrange("p (b v) -> p b v", v=64))
            else:
                Ot4 = fl(Ot).rearrange("p (b q v) -> p b q v", q=2, v=64)
                for par in range(2):
                    k.op("dve", "tensor_tensor", out=Ot4[:, :, par, :], in0=ps[6 + par][:, 0:256].rearrange("p (b v) -> p b v", v=64),
                         in1=Of4[:, :, par, :], op=ALU.add)
                O8 = fl(Ot).rearrange("p (h v) -> p h v", v=64)
                S8 = fl(sq2).rearrange("p (h v) -> p h v", v=64)
                k.op("dve", "tensor_reduce", out=st8[:, 0, :], in_=O8, axis=AX.X, op=ALU.add)
                k.op("act", "activation", out=fl(sq2), in_=fl(Ot), func=AF.Square)
                k.op("dve", "tensor_reduce", out=st8[:, 1, :], in_=S8, axis=AX.X, op=ALU.add)
                k.op("dve", "tensor_scalar", out=st8[:, 2, :], in0=st8[:, 0, :], scalar1=1.0 / 64, scalar2=None, op0=ALU.mult)
                k.op("dve", "tensor_tensor", out=st8[:, 3, :], in0=st8[:, 2, :], in1=st8[:, 2, :], op=ALU.mult)
                k.op("dve", "scalar_tensor_tensor", out=st8[:, 3, :], in0=st8[:, 1, :], scalar=1.0 / 64, in1=st8[:, 3, :], op0=ALU.mult, op1=ALU.subtract)
                k.op("act", "activation", out=st8[:, 3, :], in_=st8[:, 3, :], func=AF.Sqrt, bias=RW_LNX_EPS)
                k.op("dve", "reciprocal", out=st8[:, 3, :], in_=st8[:, 3, :])
                pump()
                k.op("dve", "tensor_tensor", out=O8, in0=O8, in1=st8[:, 2, :].unsqueeze(2).to_broadcast([128, 8, 64]), op=ALU.subtract)
                k.op("dve", "tensor_tensor", out=O8, in0=O8, in1=st8[:, 3, :].unsqueeze(2).to_broadcast([128, 8, 64]), op=ALU.mult)
                k.op("dve", "tensor_tensor", out=fl(Ot), in0=fl(Ot), in1=lng[:], op=ALU.mult)
                k.op("dve", "tensor_tensor", out=fl(Xnb), in0=fl(Ot), in1=lnb[:], op=ALU.add)
                pb = ps[7][:].bitcast(BF16)
                for b in range(4):
                    k.op("pe", "transpose", out=pb[:, 128 * b:128 * b + 128], in_=Xnb[:, b, :], identity=self.idb[:])
                k.op("dve", "tensor_tensor", out=fl(otmp), in0=pb[:, 0:512], in1=fl(P["bont"]), op=ALU.add)
                k.op("dve", "tensor_tensor", out=yst[:], in0=otmp[:], in1=P["Gt"][:], op=ALU.mult)
                k.dma(out=self.yT[3][:, :, cs].rearrange("b p t -> p b t"), in_=yst[:], q="pool", wd=True)

        def drain(g):
            if g is not None:
                for _ in g:
                    pass

        for d in range(2):
            k.op("pool", "memset", ap=ST32[:], constant=0.0)
            k.op("pool", "memset", ap=STb[:], constant=0.0)
            order = list(range(NT)) if d == 0 else list(range(NT - 1, -1, -1))
            drain(prep(order[0], d, PB[0]))
            for ci, c in enumerate(order):
                nxt = prep(order[ci + 1], d, PB[(ci + 1) % 2]) if ci + 1 < len(order) else None

                def pump(nxt=nxt):
                    if nxt is not None:
                        next(nxt, None)
                stages(c, d, PB[ci % 2], pump)
                drain(nxt)
        k.barrier()
        k.pop()

    def p7(self, l):
        k, cfg = self.k, self.cfg
        ps = self.ps
        k.push()
        bp = k.sb("m_bp", [128, 4, 4, D], BF16)
        wo = k.sb("m_wo", [128, 8, D], BF16)
        for nb in range(4):
            k.dma(out=bp[:, nb], in_=self.wb_bp[l, nb].rearrange("(kc p) o -> p kc o", p=128))
        k.dma(out=wo[:], in_=self.wb_out[l].rearrange("(kc p) o -> p kc o", p=128))
        ysb = [k.sb(f"m_ysb{i}", [128, 16, 512], BF16) for i in range(2)]
        gsb = [k.sb(f"m_gsb{i}", [128, 32, 512], BF16) for i in range(2)]
        mT = k.sb("m_mT", [128, 8, 512], BF16)
        acc = k.sb("m_acc", [128, 512], F32)
        tmp = [k.sb(f"m_tmp{i}", [128, 512], F32) for i in range(2)]
        xt = [k.sb(f"m_xt{i}", [128, D], F32) for i in range(2)]
        xo = [k.sb(f"m_xo{i}", [128, D], F32) for i in range(2)]
        ti = 0
        for bi, (t0, n) in enumerate(cfg.TB):
            i = bi % 2
            k.dma(out=ysb[i][:, :, 0:n], in_=self.yT[:, :, :, t0:t0 + n].rearrange("a b p t -> p (a b) t"))
            k.dma(out=gsb[i][:, :, 0:n], in_=self.gT[:, :, t0:t0 + n].rearrange("j p t -> p j t"))
            for j in range(8):
                for nb in range(4):
                    p = ps[(j * 4 + nb) % 4]
                    for kc in range(4):
                        self.mm(p[:, 0:n], bp[:, nb, kc, 128 * j:128 * j + 128], ysb[i][:, nb * 4 + kc, 0:n], kc == 0, kc == 3)
                    g_ = gsb[i][:, nb * 8 + j, 0:n]
                    if nb == 0:
                        k.op("dve", "tensor_tensor", out=acc[:, 0:n], in0=p[:, 0:n], in1=g_, op=ALU.mult)
                    else:
                        t_ = tmp[nb % 2]
                        k.op("dve", "tensor_tensor", out=t_[:, 0:n], in0=p[:, 0:n], in1=g_, op=ALU.mult)
                        if nb < 3:
                            k.op("pool", "tensor_tensor", out=acc[:, 0:n], in0=acc[:, 0:n], in1=t_[:, 0:n], op=ALU.add)
                        else:
                            k.op("pool", "tensor_tensor", out=mT[:, j, 0:n], in0=acc[:, 0:n], in1=t_[:, 0:n], op=ALU.add)
            for tt in range(n // 128):
                row0 = t0 + 128 * tt
                x_, o_ = xt[ti % 2], xo[ti % 2]
                ti += 1
                k.dma(out=x_[:], in_=self.xs[row0:row0 + 128, :])
                for half in range(2):
                    p = ps[4 + (tt * 2 + half) % 4]
                    hs_ = slice(512 * half, 512 * half + 512)
                    for j in range(8):
                        self.mm(p[:, :], mT[:, j, 128 * tt:128 * tt + 128], wo[:, j, hs_], j == 0, j == 7)
                    k.op("dve", "tensor_tensor", out=o_[:, hs_], in0=p[:, :], in1=x_[:, hs_], op=ALU.add)
                k.dma(out=self.xs2[row0:row0 + 128, :], in_=o_[:], q="pool", wd=True)
        k.barrier()
        k.pop()

    def p8(self, l, s, last):
        k, cfg = self.k, self.cfg
        ps = self.ps
        LP = cfg.LP
        k.push()
        w1 = k.sb("f_w1", [128, 8, DFF], BF16)
        w2 = k.sb("f_w2", [128, 32, D], BF16)
        w1v = self.wb_w1[l].rearrange("(kc p) f -> p kc f", p=128)
        w2v = self.wb_w2[l].rearrange("(kc p) o -> p kc o", p=128)
        for q_ in range(4):
            k.dma(out=w1[:, :, 1024 * q_:1024 * (q_ + 1)], in_=w1v[:, :, 1024 * q_:1024 * (q_ + 1)], wd=True)
            k.dma(out=w2[:, 8 * q_:8 * (q_ + 1), :], in_=w2v[:, 8 * q_:8 * (q_ + 1), :], wd=True)
        gb = k.sb("f_gb", [128, D], F32)
        k.dma(out=gb[:], in_=self.W["norm_mlp_g"][l].partition_broadcast(128))
        xt = [k.sb(f"f_xt{i}", [128, D], F32) for i in range(2)]
        hb = k.sb("f_hb", [128, D], BF16)
        sq = k.sb("f_sq", [128, D], BF16)
        st = k.sb("f_st", [128, 2], F32)
        h2T = k.sb("f_h2T", [128, 8, 256], BF16)
        uT = k.sb("f_uT", [128, 32, 256], BF16)
        rl = [k.sb(f"f_rl{i}", [128, 256], F32) for i in range(2)]
        xo = [k.sb(f"f_xo{i}", [128, D], F32) for i in range(2)]
        oi = 0
        for t0 in range(0, LP, 256):
            n = min(256, LP - t0)
            nt = n // 128
            for tt in range(nt):
                k.dma(out=xt[tt][:], in_=self.xs2[t0 + 128 * tt:t0 + 128 * tt + 128, :])
                self.norm_T(xt[tt][:], gb, hb, st, sq, h2T[:, :, 128 * tt:128 * tt + 128], ps[tt % 2])
            for jj in range(32):
                p = ps[2 + jj % 4]
                for kc in range(8):
                    self.mm(p[:, 0:n], w1[:, kc, 128 * jj:128 * jj + 128], h2T[:, kc, 0:n], kc == 0, kc == 7)
                r_ = rl[jj % 2]
                k.op("act", "activation", out=r_[:, 0:n], in_=p[:, 0:n], func=AF.Relu)
                k.op("pool", "tensor_tensor", out=uT[:, jj, 0:n], in0=r_[:, 0:n], in1=r_[:, 0:n], op=ALU.mult)
            for tt in range(nt):
                row0 = t0 + 128 * tt
                o_ = xo[oi % 2]
                oi += 1
                for half in range(2):
                    p = ps[6 + half]
                    hs_ = slice(512 * half, 512 * half + 512)
                    for jj in range(32):
                        self.mm(p[:, :], uT[:, jj, 128 * tt:128 * tt + 128], w2[:, jj, hs_], jj == 0, jj == 31)
                    k.op("dve", "tensor_tensor", out=o_[:, hs_], in0=p[:, :], in1=xt[tt][:, hs_], op=ALU.add)
                if not last:
                    k.dma(out=self.xs[row0:row0 + 128, :], in_=o_[:], q="pool", wd=True)
                elif row0 >= 128:
                    k.dma(out=self.yout[s][row0 - 128:row0, :], in_=o_[:], q="pool", wd=True)
        k.barrier()
        k.pop()

    def build(self):
        k, cfg = self.k, self.cfg
        stop = self.stop
        self.setup()
        done = False
        for s in range(cfg.NSEQ):
            self.p0(s)
            for l in range(cfg.DEPTH):
                self.p1(l)
                if stop == "p1":
                    k.pop(); done = True; break
                self.pa(l)
                self.p2(l)
                k.pop()
                if stop == "p2":
                    done = True; break
                self.p3(l)
                if stop == "p3":
                    done = True; break
                self.p4(l)
                if stop == "p4":
                    done = True; break
                self.p5(l)
                if stop == "p5":
                    done = True; break
                self.p6a(l)
                self.p6b(l)
                if stop == "p6x":
                    self.p6b(l)
                    done = True; break
                if stop == "p6":
                    done = True; break
                self.p7(l)
                if stop == "p7":
                    done = True; break
                self.p8(l, s, l == cfg.DEPTH - 1)
                if stop == "p8":
                    done = True; break
            if done:
                break
        k.finish()
        print(f"[build] inst={k.n_inst} waits={k.n_wait} sems={k.n_sems} cnt={k.cnt}")
        return self.nc


_CACHE = {}


def kernel(**inputs):
    x_prompt = np.ascontiguousarray(inputs["x_prompt"], dtype=np.float32)
    x_sample = np.ascontiguousarray(inputs["x_sample"], dtype=np.float32)
    B, S, _ = x_prompt.shape
    DB = x_sample.shape[0]
    DEPTH = inputs["w_in"].shape[0]
    n_cores = 8
    assert B == n_cores and DB <= n_cores and x_sample.shape[1] == S
    cfg = Cfg(S, 2, DEPTH)
    key = (S, DEPTH)
    m = M(cfg)
    nc = m.build()
    consts = host_consts(cfg)
    wts = {n: np.ascontiguousarray(inputs[n], dtype=np.float32) for n in WSHAPES(DEPTH)}
    zeros = np.zeros((S, D), np.float32)
    in_maps = []
    for c in range(n_cores):
        im = dict(wts)
        im.update(consts)
        im["xin0"] = x_prompt[c]
        im["xin1"] = x_sample[c] if c < DB else zeros
        in_maps.append(im)
    res = run_bass_kernel_spmd(nc, in_maps, core_ids=list(range(n_cores)))
    y_prompt = np.stack([np.asarray(res.results[c]["yout0"], dtype=np.float32) for c in range(n_cores)])
    y_sample = np.stack([np.asarray(res.results[c]["yout1"], dtype=np.float32) for c in range(DB)])
    return (y_prompt, y_sample)
```
